# Optimizing a Trainium2 kernel written in Bass

```python
import math
import jax, jax.numpy as jnp
from jax import lax
import numpy as np

D_MODEL = 1024
BATCH = 8
SEQ = 2048
DEPTH = 4
DEC_BATCH = 128
DEC_SEQ = 4
PAST_LEN = 16384
PAGE_SIZE = 128

N_META = 16
MIX_WIDTH = D_MODEL
D_A = MIX_WIDTH // 2
D_B = MIX_WIDTH - D_A
HG_HEADS = 4
HG_DK = 128
HG_DV = D_A // HG_HEADS
S5_GROUP = 16
S5_GROUPS = D_B // S5_GROUP
S5_STATE = 64
FFN_DIM = 2816
CHUNK = 64
EPS = 1e-6
N_QK = HG_HEADS * HG_DK
IN_COLS = 2 * N_QK + 2 * D_A + D_B
F32 = jnp.float32

kernel_name = 'hymba_hgrn2_s5_macaron_step'


def rmsnorm(x, g):
    xf = x.astype(F32)
    y = xf * lax.rsqrt(jnp.mean(xf * xf, axis=-1, keepdims=True) + EPS)
    return (y * g.astype(F32)).astype(x.dtype)


def swiglu(x, w_gate, w_up, w_down):
    return (jax.nn.silu(x @ w_gate) * (x @ w_up)) @ w_down


def hgrn2_recurrence(q, log_f, k, v, s0, chunk, lead_pad):
    if lead_pad > 0:
        pad = ((0, 0), (lead_pad, 0), (0, 0), (0, 0))
        q, log_f, k, v = (jnp.pad(t, pad) for t in (q, log_f, k, v))
    bsz, length, heads, _ = q.shape
    dv = v.shape[-1]
    n_blocks = length // chunk

    def to_blocks(t):
        return t.reshape(bsz, n_blocks, chunk, heads, t.shape[-1]).transpose(1, 0, 3, 2, 4)

    causal = jnp.tril(jnp.ones((chunk, chunk), dtype=bool))[:, :, None]

    def step(state, blk):
        qc, gc, kc, vc = blk
        b = jnp.cumsum(gc, axis=2)
        o_inter = jnp.einsum('bhtk,bhkv->bhtv', qc * jnp.exp(b), state)
        diff = b[:, :, :, None, :] - b[:, :, None, :, :]
        decay = jnp.exp(jnp.where(causal, diff, -jnp.inf))
        scores = jnp.einsum('bhtk,bhsk,bhtsk->bhts', qc, kc, decay)
        o_intra = jnp.einsum('bhts,bhsv->bhtv', scores, vc)
        b_last = b[:, :, -1:, :]
        state = (jnp.exp(b_last[:, :, 0, :, None]) * state
                 + jnp.einsum('bhsk,bhsv->bhkv', kc * jnp.exp(b_last - b), vc))
        return state, o_inter + o_intra

    s_fin, o = lax.scan(step, s0, (to_blocks(q), to_blocks(log_f), to_blocks(k), to_blocks(v)))
    o = o.transpose(1, 0, 3, 2, 4).reshape(bsz, length, heads, dv)
    return o[:, lead_pad:], s_fin


def _complex_affine_combine(e1, e2):
    a1r, a1i, b1r, b1i = e1
    a2r, a2i, b2r, b2i = e2
    return (a2r * a1r - a2i * a1i, a2r * a1i + a2i * a1r,
            a2r * b1r - a2i * b1i + b2r, a2r * b1i + a2i * b1r + b2i)


def s5_ssm(u, lam_re, lam_im, log_dt, b_re, b_im, c_re, c_im, d_skip, x0_re, x0_im):
    dt = jnp.exp(log_dt.astype(F32))[:, None]
    lr = jnp.minimum(lam_re.astype(F32), -1e-4)
    li = lam_im.astype(F32)
    mag = jnp.exp(lr * dt)
    ar, ai = mag * jnp.cos(li * dt), mag * jnp.sin(li * dt)
    den = lr * lr + li * li
    cr = ((ar - 1.0) * lr + ai * li) / den
    ci = (ai * lr - (ar - 1.0) * li) / den
    br, bi = b_re.astype(F32), b_im.astype(F32)
    bbr = cr[..., None] * br - ci[..., None] * bi
    bbi = cr[..., None] * bi + ci[..., None] * br
    bu_r = jnp.einsum('blgh,gph->blgp', u, bbr)
    bu_i = jnp.einsum('blgh,gph->blgp', u, bbi)
    bu_r = jnp.concatenate([x0_r[:, None] if False else x0_re.astype(F32)[:, None], bu_r], axis=1)
    bu_i = jnp.concatenate([x0_im.astype(F32)[:, None], bu_i], axis=1)
    a_r = jnp.broadcast_to(ar, bu_r.shape)
    a_i = jnp.broadcast_to(ai, bu_i.shape)
    _, _, xr, xi = lax.associative_scan(_complex_affine_combine, (a_r, a_i, bu_r, bu_i), axis=1)
    xr, xi = xr[:, 1:], xi[:, 1:]
    y = (jnp.einsum('blgp,ghp->blgh', xr, c_re.astype(F32))
         - jnp.einsum('blgp,ghp->blgh', xi, c_im.astype(F32))
         + d_skip.astype(F32) * u)
    return y, xr[:, -1], xi[:, -1]


def mixer(hn, l, params, lb, s_hg, s_re, s_im, chunk, lead_pad):
    (_, _, _, _, _, _, w_in, hgrn_norm, s5_lambda_re, s5_lambda_im, s5_log_dt, s5_b_re, s5_b_im,
     s5_c_re, s5_c_im, s5_d, s5_w_glu, s5_b_glu, s5_norm, w_out, _, _, _, _, _) = params
    bsz, length, _ = hn.shape
    z = hn @ w_in[l]
    zq, zf, zi, zg, zu = jnp.split(z.astype(F32), [N_QK, 2 * N_QK, 2 * N_QK + D_A, 2 * N_QK + 2 * D_A], axis=-1)
    q = jax.nn.silu(zq).reshape(bsz, length, HG_HEADS, HG_DK)
    log_f = jnp.logaddexp(jnp.log(lb), jnp.log1p(-lb) + jax.nn.log_sigmoid(zf))
    k = (1.0 - lb) * jax.nn.sigmoid(-zf)
    log_f = log_f.reshape(bsz, length, HG_HEADS, HG_DK)
    k = k.reshape(bsz, length, HG_HEADS, HG_DK)
    v = zi.reshape(bsz, length, HG_HEADS, HG_DV)
    o_hg, s_hg_new = hgrn2_recurrence(q, log_f, k, v, s_hg.astype(F32), chunk, lead_pad)
    o_hg = rmsnorm(o_hg, hgrn_norm[l].reshape(HG_HEADS, HG_DV)).reshape(bsz, length, D_A) * jax.nn.silu(zg)
    u = zu.reshape(bsz, length, S5_GROUPS, S5_GROUP)
    y, s_re_new, s_im_new = s5_ssm(u, s5_lambda_re[l], s5_lambda_im[l], s5_log_dt[l], s5_b_re[l], s5_b_im[l],
                                   s5_c_re[l], s5_c_im[l], s5_d[l], s_re, s_im)
    y = jax.nn.gelu(y.reshape(bsz, length, D_B))
    y = y * jax.nn.sigmoid(y @ s5_w_glu[l].astype(F32) + s5_b_glu[l].astype(F32))
    y = rmsnorm(y, s5_norm[l])
    out = jnp.concatenate([o_hg, y], axis=-1).astype(hn.dtype) @ w_out[l]
    return out.astype(hn.dtype), s_hg_new, s_re_new, s_im_new


def layer_stack(h, s_hg, s_re, s_im, params, chunk, lead_pad):
    (lb_param, norm_ffn1, ffn1_w_gate, ffn1_w_up, ffn1_w_down, norm_mix, _, _, _, _, _, _, _, _, _, _, _, _, _, _,
     norm_ffn2, ffn2_w_gate, ffn2_w_up, ffn2_w_down, norm_final) = params
    lb_all = jnp.cumsum(jax.nn.softmax(lb_param.astype(F32), axis=0), axis=0)
    lb_all = lb_all - lb_all[0:1]
    new_hg, new_re, new_im = [], [], []
    for l in range(DEPTH):
        h = h + 0.5 * swiglu(rmsnorm(h, norm_ffn1[l]), ffn1_w_gate[l], ffn1_w_up[l], ffn1_w_down[l])
        m, a, b, c = mixer(rmsnorm(h, norm_mix[l]), l, params, lb_all[l], s_hg[l], s_re[l], s_im[l], chunk, lead_pad)
        h = h + m
        h = h + 0.5 * swiglu(rmsnorm(h, norm_ffn2[l]), ffn2_w_gate[l], ffn2_w_up[l], ffn2_w_down[l])
        new_hg.append(a)
        new_re.append(b)
        new_im.append(c)
    return rmsnorm(h, norm_final), jnp.stack(new_hg), jnp.stack(new_re), jnp.stack(new_im)


def setup_inputs(seed: int = 0) -> dict:
    key = jax.random.key(seed)
    ks = iter(jax.random.split(key, 40))
    nrm = lambda shape, scale: scale * jax.random.normal(next(ks), shape, F32)
    gain = lambda shape: 1.0 + nrm(shape, 0.02)
    lam_im_base = jnp.broadcast_to(math.pi * jnp.arange(S5_STATE, dtype=F32), (DEPTH, S5_GROUPS, S5_STATE))
    return {
        'x_prompt': nrm((BATCH, SEQ, D_MODEL), 1.0),
        'x_sample': nrm((DEC_BATCH, DEC_SEQ, D_MODEL), 1.0),
        'state_hgrn': nrm((DEPTH, DEC_BATCH, HG_HEADS, HG_DK, HG_DV), 0.5),
        'state_s5_re': nrm((DEPTH, DEC_BATCH, S5_GROUPS, S5_STATE), 0.1),
        'state_s5_im': nrm((DEPTH, DEC_BATCH, S5_GROUPS, S5_STATE), 0.1),
        'meta_tokens': nrm((N_META, D_MODEL), 1.0),
        'lb_param': nrm((DEPTH, N_QK), 0.1),
        'norm_ffn1': gain((DEPTH, D_MODEL)),
        'ffn1_w_gate': nrm((DEPTH, D_MODEL, FFN_DIM), D_MODEL ** -0.5),
        'ffn1_w_up': nrm((DEPTH, D_MODEL, FFN_DIM), D_MODEL ** -0.5),
        'ffn1_w_down': nrm((DEPTH, FFN_DIM, D_MODEL), FFN_DIM ** -0.5),
        'norm_mix': gain((DEPTH, D_MODEL)),
        'w_in': nrm((DEPTH, D_MODEL, IN_COLS), D_MODEL ** -0.5),
        'hgrn_norm': gain((DEPTH, D_A)),
        's5_lambda_re': -0.5 + nrm((DEPTH, S5_GROUPS, S5_STATE), 0.01),
        's5_lambda_im': lam_im_base + nrm((DEPTH, S5_GROUPS, S5_STATE), 0.01),
        's5_log_dt': jax.random.uniform(next(ks), (DEPTH, S5_GROUPS), F32, math.log(1e-3), math.log(1e-1)),
        's5_b_re': nrm((DEPTH, S5_GROUPS, S5_STATE, S5_GROUP), S5_GROUP ** -0.5),
        's5_b_im': nrm((DEPTH, S5_GROUPS, S5_STATE, S5_GROUP), S5_GROUP ** -0.5),
        's5_c_re': nrm((DEPTH, S5_GROUPS, S5_GROUP, S5_STATE), S5_STATE ** -0.5),
        's5_c_im': nrm((DEPTH, S5_GROUPS, S5_GROUP, S5_STATE), S5_STATE ** -0.5),
        's5_d': nrm((DEPTH, S5_GROUPS, S5_GROUP), 1.0),
        's5_w_glu': nrm((DEPTH, D_B, D_B), D_B ** -0.5),
        's5_b_glu': nrm((DEPTH, D_B), 0.01),
        's5_norm': gain((DEPTH, D_B)),
        'w_out': nrm((DEPTH, MIX_WIDTH, D_MODEL), MIX_WIDTH ** -0.5),
        'norm_ffn2': gain((DEPTH, D_MODEL)),
        'ffn2_w_gate': nrm((DEPTH, D_MODEL, FFN_DIM), D_MODEL ** -0.5),
        'ffn2_w_up': nrm((DEPTH, D_MODEL, FFN_DIM), D_MODEL ** -0.5),
        'ffn2_w_down': nrm((DEPTH, FFN_DIM, D_MODEL), FFN_DIM ** -0.5),
        'norm_final': gain((D_MODEL,)),
    }


def reference(x_prompt, x_sample, state_hgrn, state_s5_re, state_s5_im, meta_tokens, lb_param,
              norm_ffn1, ffn1_w_gate, ffn1_w_up, ffn1_w_down, norm_mix, w_in, hgrn_norm,
              s5_lambda_re, s5_lambda_im, s5_log_dt, s5_b_re, s5_b_im, s5_c_re, s5_c_im, s5_d,
              s5_w_glu, s5_b_glu, s5_norm, w_out, norm_ffn2, ffn2_w_gate, ffn2_w_up, ffn2_w_down,
              norm_final):
    params = (lb_param, norm_ffn1, ffn1_w_gate, ffn1_w_up, ffn1_w_down, norm_mix, w_in, hgrn_norm,
              s5_lambda_re, s5_lambda_im, s5_log_dt, s5_b_re, s5_b_im, s5_c_re, s5_c_im, s5_d,
              s5_w_glu, s5_b_glu, s5_norm, w_out, norm_ffn2, ffn2_w_gate, ffn2_w_up, ffn2_w_down,
              norm_final)
    bsz = x_prompt.shape[0]
    meta = jnp.broadcast_to(meta_tokens.astype(x_prompt.dtype)[None], (bsz, N_META, D_MODEL))
    h_prompt = jnp.concatenate([meta, x_prompt], axis=1)
    zero_hg = jnp.zeros((DEPTH, bsz, HG_HEADS, HG_DK, HG_DV), F32)
    zero_s5 = jnp.zeros((DEPTH, bsz, S5_GROUPS, S5_STATE), F32)
    y_p, hgrn_prompt, s5_re_prompt, s5_im_prompt = layer_stack(
        h_prompt, zero_hg, zero_s5, zero_s5, params, CHUNK, CHUNK - N_META)
    y_prompt = y_p[:, N_META:]
    y_sample, hgrn_sample, s5_re_sample, s5_im_sample = layer_stack(
        x_sample, state_hgrn, state_s5_re, state_s5_im, params, x_sample.shape[1], 0)
    return (y_prompt, y_sample, hgrn_prompt, s5_re_prompt, s5_im_prompt, hgrn_sample, s5_re_sample, s5_im_sample)
```

```python
import math
import numpy as np
import concourse.bass as bass
import concourse.mybir as mybir
from concourse.bass_utils import run_bass_kernel_spmd

F32 = mybir.dt.float32
BF16 = mybir.dt.bfloat16
I32 = mybir.dt.int32
AF = mybir.ActivationFunctionType
ALU = mybir.AluOpType

D = 1024
FF = 2816
NHT = 22
DEPTH = 4
NCORE = 8
TG1 = 1104
TG2 = 1024
TALL = TG1 + TG2
EPS = 1e-6
NSEQ = 16
POST_DRAIN = 5
TWO_PI = 2.0 * math.pi

R_NF1, R_NMX, R_NF2, R_NFIN, R_HGN, R_S5N, R_BGLU, R_LBP, R_S5D = 0, 32, 64, 96, 104, 120, 136, 152, 168
NPROW = 184


class Res:
    __slots__ = ("name", "w", "r", "a", "dsem")

    def __init__(self, name, dsem=None):
        self.name = name
        self.w = None
        self.r = {}
        self.a = {}
        self.dsem = dsem


class KB:
    def __init__(self, nc):
        self.nc = nc
        self.eng = {}
        self.nosame = set()
        self.defer = None

    def add_eng(self, name, obj, sem):
        self.eng[name] = dict(obj=obj, sem=sem, count=0, seen={})

    def _deps(self, reads, writes):
        need = {}
        for r in reads:
            if r.w is not None:
                n, c = r.w
                if need.get(n, 0) < c:
                    need[n] = c
            for n, c in r.a.items():
                if need.get(n, 0) < c:
                    need[n] = c
        for w in writes:
            if w.w is not None:
                n, c = w.w
                if need.get(n, 0) < c:
                    need[n] = c
            for n, c in w.r.items():
                if need.get(n, 0) < c:
                    need[n] = c
            for n, c in w.a.items():
                if need.get(n, 0) < c:
                    need[n] = c
        return need

    def _wait(self, ename, need):
        E = self.eng[ename]
        for n, c in need.items():
            if n == ename and ename in self.nosame:
                continue
            if E["seen"].get(n, 0) >= c:
                continue
            E["obj"].wait_ge(self.eng[n]["sem"], c)
            E["seen"][n] = c

    def op(self, ename, fn, reads=(), writes=()):
        if self.defer is not None:
            self.defer.append(lambda: self.op(ename, fn, reads, writes))
            return
        self._wait(ename, self._deps(reads, writes))
        E = self.eng[ename]
        ins = fn(E["obj"])
        E["count"] += 1
        ins.then_inc(E["sem"], 1)
        c = E["count"]
        for r in reads:
            r.r[ename] = c
        for w in writes:
            w.w = (ename, c)
            w.r = {}
            w.a = {}

    def dma(self, qname, out, in_, sres, reads=(), writes=(), **kw):
        if self.defer is not None:
            self.defer.append(lambda: self.dma(qname, out, in_, sres, reads, writes, **kw))
            return
        self._wait(qname, self._deps(reads, writes))
        d = self.eng[sres.dsem]
        ins = self.eng[qname]["obj"].dma_start(out=out, in_=in_, **kw)
        d["count"] += 16
        ins.then_inc(d["sem"], 16)
        for r in reads:
            r.r[sres.dsem] = d["count"]
        for w in writes:
            w.w = (sres.dsem, d["count"])
            w.r = {}
            w.a = {}

    def alias(self, new, old):
        m = {}
        for o in old:
            if o.w is not None:
                n, c = o.w
                m[n] = max(m.get(n, 0), c)
            for n, c in o.r.items():
                m[n] = max(m.get(n, 0), c)
            for n, c in o.a.items():
                m[n] = max(m.get(n, 0), c)
        for r in new:
            r.w = None
            r.r = {}
            r.a = dict(m)

    def wait_all(self, ename, ress):
        need = {}
        for r in ress:
            if r.w is not None:
                n, c = r.w
                need[n] = max(need.get(n, 0), c)
            for n, c in r.r.items():
                need[n] = max(need.get(n, 0), c)
        self._wait(ename, need)


def build_nc(depth=DEPTH, mixer=True, hgrn=True, s5=True):
    nc = bass.Bass("TRN2", target_bir_lowering=False)
    import contextlib
    es = contextlib.ExitStack()

    def dram(name, shape, kind, dt=F32):
        return nc.dram_tensor(name, list(shape), dt, kind=kind).ap()

    IN, OUT = "ExternalInput", "ExternalOutput"
    xin = dram("xin", [TALL, D], IN)
    st_hg = dram("st_hg", [DEPTH, NSEQ, 4, 128, 128], IN)
    st_re = dram("st_re", [DEPTH, NSEQ, 2048], IN)
    st_im = dram("st_im", [DEPTH, NSEQ, 2048], IN)
    pvec = dram("pvec", [NPROW, 128], IN)
    s5pv = dram("s5pv", [DEPTH * 48, 128], IN)
    w_f1g = dram("ffn1_w_gate", [DEPTH, D, FF], IN)
    w_f1u = dram("ffn1_w_up", [DEPTH, D, FF], IN)
    w_f1d = dram("ffn1_w_down", [DEPTH, FF, D], IN)
    w_f2g = dram("ffn2_w_gate", [DEPTH, D, FF], IN)
    w_f2u = dram("ffn2_w_up", [DEPTH, D, FF], IN)
    w_f2d = dram("ffn2_w_down", [DEPTH, FF, D], IN)
    w_in = dram("w_in", [DEPTH, D, 2560], IN)
    w_out = dram("w_out", [DEPTH, D, D], IN)
    w_glu = dram("s5_w_glu", [DEPTH, 512, 512], IN)
    b_re = dram("s5_b_re", [DEPTH, 2048, 16], IN)
    b_im = dram("s5_b_im", [DEPTH, 2048, 16], IN)
    c_re = dram("s5_c_re", [DEPTH, 512, 64], IN)
    c_im = dram("s5_c_im", [DEPTH, 512, 64], IN)
    INT = "Internal"
    scr_cs = dram("scr_cs", [DEPTH, 128, 4096], INT)
    scr_bt = dram("scr_bt", [DEPTH, 2, 128, 1024], INT, BF16)
    scr_ct = dram("scr_ct", [DEPTH, 128, 4096], INT, BF16)
    scr_sm = dram("scr_sm", [DEPTH, 128, 272], INT)
    yout = dram("yout", [TALL, D], OUT)
    o_hgp = dram("o_hgp", [DEPTH, 4, 128, 128], OUT)
    o_s5p_re = dram("o_s5p_re", [DEPTH, 16, 128], OUT)
    o_s5p_im = dram("o_s5p_im", [DEPTH, 16, 128], OUT)
    o_hgs = dram("o_hgs", [DEPTH, NSEQ, 4, 128, 128], OUT)
    o_s5s_re = dram("o_s5s_re", [DEPTH, NSEQ, 2048], OUT)
    o_s5s_im = dram("o_s5s_im", [DEPTH, NSEQ, 2048], OUT)

    def sb(name, shape, dt=F32):
        return es.enter_context(nc.sbuf_tensor(name, list(shape), dt))

    def ps(name, shape, dt=F32):
        return es.enter_context(nc.psum_tensor(name, list(shape), dt))

    kb = KB(nc)
    nsem = [0]

    def newsem(name):
        nsem[0] += 1
        return es.enter_context(nc.semaphore(name))

    kb.add_eng("pe", nc.tensor, newsem("s_pe"))
    kb.add_eng("act", nc.scalar, newsem("s_act"))
    kb.add_eng("dve", nc.vector, newsem("s_dve"))
    kb.add_eng("pool", nc.gpsimd, newsem("s_pool"))
    kb.add_eng("sp", nc.sync, newsem("s_sp"))
    kb.nosame.add("pe")
    kb.nosame.add("sp")

    def dres(name):
        sname = "d_" + name
        kb.add_eng(sname, None, newsem(sname))
        return Res(name, dsem=sname)

    hT = sb("hT", [128, 8, TG1])
    xn = sb("xn", [128, 8, TG1], BF16)
    Sst = sb("Sst", [128, DEPTH, 4, 128])
    car = sb("car", [128, DEPTH, 2, 16])
    NA1 = 11 * TG1 + 11 * 1024
    arena1 = sb("arena1", [128, NA1], BF16)
    NA2 = 10496 + 1024
    arena2 = sb("arena2", [128, NA2])
    NRA = 4
    ringA = sb("ringA", [128, NRA, 2048], BF16)
    stg = sb("stg", [128, 2, 1024])
    PV = sb("PV", [128, NPROW])
    S5P = sb("S5P", [128, DEPTH * 48])
    identf = sb("identf", [128, 128])
    identb = sb("identb", [128, 128], BF16)
    onesb = sb("onesb", [128, 128], BF16)
    maskP = sb("maskP", [128, 128])
    mask0 = sb("mask0", [128, 128])
    Eseq = sb("Eseq", [16, 64], BF16)
    rowm = sb("rowm", [64, 16])
    epsT = sb("epsT", [128, 1])
    lbT = sb("lbT", [128, DEPTH, 4, 2])
    sq = sb("sq", [128, 3, 512], BF16)
    rt = sb("rt", [128, 2, 512])
    sgt = sb("sgt", [128, 2, 512])
    BtE = sb("BtE", [128, 4, 2, 128], BF16)
    BtO = sb("BtO", [128, 4, 2, 128], BF16)
    s5sm = sb("s5sm", [128, 27, 16])
    s5i = sb("s5i", [128, 2, 16], I32)
    evod = sb("evod", [128, 4])
    X0 = sb("X0", [128, 2, 16, NSEQ])
    inj = X0
    X1 = X0
    hgst = sb("hgst", [128, 4, 128])
    hgout = sb("hgout", [128, 3, 128])
    Sd = sb("Sd", [128, 4, 128])
    Sdb = sb("Sdb", [128, 4, 128], BF16)
    ktok = sb("ktok", [128, 4, 128], BF16)
    kms = sb("kms", [64, 4, 128], BF16)
    Am = sb("Am", [128, 4, 128], BF16)
    eb = sb("eb", [128, 4, 34])

    PBall = ps("pball", [128, 7 * 512])
    PB = [PBall[:, 512 * i:512 * (i + 1)] for i in range(7)]
    psB4 = PBall[:, 0:2048].rearrange("p (s r c) -> p s r c", s=4, r=2)
    PBb = ps("pbb", [128, 1024], BF16)
    PR = [Res("pb%d" % i) for i in range(7)]
    PRb = Res("pbb")

    hid = arena1[:, 0:11 * TG1].rearrange("p (a t) -> p a t", t=TG1)
    wdh = arena1[:, 11 * TG1:NA1].rearrange("p (a t) -> p a t", t=1024)
    qT = arena1[:, 0:4 * TG1].rearrange("p (a t) -> p a t", t=TG1)
    kT = arena1[:, 4 * TG1:8 * TG1].rearrange("p (a t) -> p a t", t=TG1)
    gate = arena1[:, 8 * TG1:12 * TG1].rearrange("p (a t) -> p a t", t=TG1)
    uT = arena1[:, 12 * TG1:16 * TG1].rearrange("p (a t) -> p a t", t=TG1)
    Vtok = arena1[:, 16 * TG1:16 * TG1 + 9 * 512].rearrange("p (a t) -> p a t", t=512)
    assert 16 * TG1 + 9 * 512 <= NA1
    xTb = arena1[:, 0:2048].rearrange("p (a r t) -> p a r t", r=2, t=256)
    ygb = arena1[:, 2048:3072].rearrange("p (a t) -> p a t", t=256)
    Ct = arena1[:, 4 * TG1:4 * TG1 + 4096].rearrange("p (a r t) -> p a r t", r=2, t=128)
    w1f = arena1[:, 8 * TG1:8 * TG1 + 2048].bitcast(F32)
    rrB = [arena1[:, 8 * TG1 + 2048 + 1024 * i:8 * TG1 + 2048 + 1024 * (i + 1)].bitcast(F32).rearrange(
        "p (a t) -> p a t", t=128) for i in range(2)]
    w2f = arena1[:, 16 * TG1:16 * TG1 + 2048].bitcast(F32)
    xTbB = arena1[:, 16 * TG1 + 2048:16 * TG1 + 4096].rearrange("p (a r t) -> p a r t", r=2, t=256)
    tmpA = arena2[:, 0:TG1]
    tmpB = arena2[:, TG1:2 * TG1]
    tmpC = arena2[:, 2 * TG1:3 * TG1]
    tmpD = arena2[:, 3 * TG1:4 * TG1]
    rmask = arena2[:, 8 * TG1:9 * TG1]
    o = 0
    cosT = arena2[:, o:o + 2048].rearrange("p (a t) -> p a t", t=128); o += 2048
    sinT = arena2[:, o:o + 2048].rearrange("p (a t) -> p a t", t=128); o += 2048
    tq = [arena2[:, o + i * 512:o + (i + 1) * 512].rearrange("p (a t) -> p a t", t=128) for i in range(4)]; o += 2048
    srcB = arena2[:, o:o + 2048].rearrange("p (r e a c) -> p r e a c", r=2, e=2, c=32)
    rin = [arena2[:, o + i * 512:o + (i + 1) * 512].rearrange("p (a t) -> p a t", t=128) for i in range(2)]; o += 1024
    rr = [arena2[:, o + i * 512:o + (i + 1) * 512].rearrange("p (a t) -> p a t", t=128) for i in range(2)]; o += 1024
    Braw = arena2[:, o:o + 512].rearrange("p (r a c) -> p r a c", r=2, c=16)
    Craw = arena2[:, o + 512:o + 1024].rearrange("p (r a c) -> p r a c", r=2, c=64)
    BB = arena2[:, o + 1024:o + 1536].rearrange("p (r a c) -> p r a c", r=2, c=16)
    srcC = arena2[:, o + 1536:o + 1792].rearrange("p (r c) -> p r c", r=2)
    coef0 = arena2[:, o:o + 16 * 80].rearrange("p (a t) -> p a t", t=80); o += 1280
    yfp = arena2[:, o:o + 1024].rearrange("p (a t) -> p a t", t=256); o += 1024
    pq = [arena2[:, o + i * 512:o + (i + 1) * 512].rearrange("p (a t) -> p a t", t=128) for i in range(2)]; o += 1024
    assert o <= NA2
    rr2 = [rr, rrB]
    yfp2 = [yfp, sgt[:, :, :].rearrange("p a t -> p (a t)").rearrange("p (a t) -> p a t", t=256)]
    xTb2 = [xTb, xTbB]
    s5stg = stg[0:16, :, :].rearrange("p a t -> p (a t)")

    R = {}

    def res(name):
        if name not in R:
            R[name] = Res(name)
        return R[name]

    r_h = [[res("h%d_%d" % (k, t)) for t in range(3)] for k in range(8)]
    r_xn = [[res("xn%d_%d" % (k, t)) for t in range(3)] for k in range(8)]
    r_a1 = res("arena1_all")
    r_stg = [dres("stg0"), dres("stg1")]
    r_ringA = [dres("ringA%d" % i) for i in range(NRA)]
    r_wdh = [dres("wdh%d" % i) for i in range(11)]
    r_hid = [[res("hid%d_%d" % (a, t)) for t in range(3)] for a in range(11)]
    r_const = dres("const")
    r_misc = res("misc")

    def pe(fn, reads, writes):
        kb.op("pe", fn, reads, writes)

    def act(fn, reads, writes):
        kb.op("act", fn, reads, writes)

    def dve(fn, reads, writes):
        kb.op("dve", fn, reads, writes)

    def pool(fn, reads, writes):
        kb.op("pool", fn, reads, writes)

    def mm_group(out_ap, pairs, reads, writes):
        n = len(pairs)

        def fn(e):
            ins = None
            for i, (l, r) in enumerate(pairs):
                ins = e.matmul(out_ap, lhsT=l, rhs=r, start=(i == 0), stop=(i == n - 1))
            return ins
        pe(fn, reads, writes)

    r_identf, r_identb, r_ones, r_masks = res("identf"), res("identb"), res("onesb"), res("masks")
    pool(lambda e: e.memset(identf[:], 0.0), [], [r_identf])
    pool(lambda e: e.affine_select(out=identf[:], in_=identf[:], pattern=[[-1, 128]], compare_op=ALU.not_equal,
                                   fill=1.0, base=0, channel_multiplier=1), [r_identf], [r_identf])
    pool(lambda e: e.tensor_copy(out=identb[:], in_=identf[:]), [r_identf], [r_identb])
    pool(lambda e: e.memset(onesb[:], 1.0), [], [r_ones])
    pool(lambda e: e.memset(epsT[:], EPS), [], [r_misc])
    pool(lambda e: e.memset(maskP[:], 1.0), [], [r_masks])
    pool(lambda e: e.affine_select(out=maskP[:], in_=maskP[:], pattern=[[1, 128]], compare_op=ALU.is_ge,
                                   fill=0.0, base=0, channel_multiplier=-1), [r_masks], [r_masks])
    pool(lambda e: e.memset(maskP[0:64, 64:128], 0.0), [r_masks], [r_masks])
    pool(lambda e: e.memset(Eseq[:], 1.0), [], [r_masks])
    pool(lambda e: e.affine_select(out=Eseq[:], in_=Eseq[:], pattern=[[1, 64]], compare_op=ALU.is_ge,
                                   fill=0.0, base=0, channel_multiplier=-4), [r_masks], [r_masks])
    pool(lambda e: e.affine_select(out=Eseq[:], in_=Eseq[:], pattern=[[-1, 64]], compare_op=ALU.is_ge,
                                   fill=0.0, base=3, channel_multiplier=4), [r_masks], [r_masks])
    pool(lambda e: e.memset(rowm[:], 1.0), [], [r_masks])
    pool(lambda e: e.affine_select(out=rowm[:], in_=rowm[:], pattern=[[-4, 16]], compare_op=ALU.is_ge,
                                   fill=0.0, base=0, channel_multiplier=1), [r_masks], [r_masks])
    pool(lambda e: e.affine_select(out=rowm[:], in_=rowm[:], pattern=[[4, 16]], compare_op=ALU.is_ge,
                                   fill=0.0, base=3, channel_multiplier=-1), [r_masks], [r_masks])
    mm_group(PB[0][0:64, 0:64], [(Eseq[:, :], Eseq[:, :])], [r_masks], [PR[0]])
    pool(lambda e: e.memset(mask0[:], 0.0), [r_masks], [r_masks])
    dve(lambda e: e.tensor_tensor(out=mask0[0:64, 0:64], in0=PB[0][0:64, 0:64], in1=maskP[0:64, 0:64], op=ALU.mult),
        [PR[0], r_masks], [r_masks])
    dve(lambda e: e.tensor_copy(out=mask0[64:80, 64:80], in_=maskP[64:80, 64:80]), [r_masks], [r_masks])
    pool(lambda e: e.memset(evod[:], 0.0), [], [r_misc])
    ev4 = sb("ev4", [128, 4])
    pool(lambda e: e.memset(ev4[:], 1.0), [], [r_misc])
    pool(lambda e: e.affine_select(out=ev4[:], in_=ev4[:], pattern=[[-32, 4]], compare_op=ALU.is_ge,
                                   fill=0.0, base=0, channel_multiplier=1), [r_misc], [r_misc])
    pool(lambda e: e.affine_select(out=ev4[:], in_=ev4[:], pattern=[[32, 4]], compare_op=ALU.is_ge,
                                   fill=0.0, base=15, channel_multiplier=-1), [r_misc], [r_misc])
    dve(lambda e: e.tensor_tensor(out=evod[:, 0:2], in0=ev4[:, 0:2], in1=ev4[:, 2:4], op=ALU.add), [r_misc], [r_misc])
    dve(lambda e: e.tensor_tensor(out=evod[:, 0:1], in0=evod[:, 0:1], in1=evod[:, 1:2], op=ALU.add), [r_misc], [r_misc])
    dve(lambda e: e.tensor_scalar(out=evod[:, 1:2], in0=evod[:, 0:1], scalar1=-1.0, scalar2=1.0, op0=ALU.mult,
                                  op1=ALU.add), [r_misc], [r_misc])
    dve(lambda e: e.tensor_scalar(out=evod[:, 2:4], in0=evod[:, 0:2], scalar1=-1.0, scalar2=None, op0=ALU.mult),
        [r_misc], [r_misc])
    pool(lambda e: e.memset(Sst[:], 0.0), [], [res("Sst")])
    pool(lambda e: e.memset(car[:], 0.0), [], [res("car")])

    r_PV = res("PV")
    for (src, dst, nrows_all) in ((pvec, PV, NPROW), (s5pv, S5P, DEPTH * 48)):
        for r0 in range(0, nrows_all, 128):
            nr = min(128, nrows_all - r0)
            kb.dma("sp", stg[0:nr, 0, 0:128], src[r0:r0 + nr, :], r_stg[0], [], [r_stg[0]])
            pe(lambda e, nr=nr: e.transpose(PB[0][:, 0:nr], stg[0:nr, 0, 0:128], identf[0:nr, 0:nr]),
               [r_stg[0], r_identf], [PR[0]])
            dve(lambda e, nr=nr, r0=r0, dst=dst: e.tensor_copy(out=dst[:, r0:r0 + nr], in_=PB[0][:, 0:nr]),
                [PR[0]], [r_PV])

    lbe = sb("lbe", [128, 4, 4])
    lbs = sb("lbs", [128, 4])
    act(lambda e: e.activation(out=lbe[:].rearrange("p a b -> p (a b)"), in_=PV[:, R_LBP:R_LBP + 16], func=AF.Exp),
        [r_PV], [r_misc])
    dve(lambda e: e.tensor_tensor(out=lbs[:], in0=lbe[:, 0, :], in1=lbe[:, 1, :], op=ALU.add), [r_misc], [r_misc])
    dve(lambda e: e.tensor_tensor(out=lbs[:], in0=lbs[:], in1=lbe[:, 2, :], op=ALU.add), [r_misc], [r_misc])
    dve(lambda e: e.tensor_tensor(out=lbs[:], in0=lbs[:], in1=lbe[:, 3, :], op=ALU.add), [r_misc], [r_misc])
    dve(lambda e: e.reciprocal(out=lbs[:], in_=lbs[:]), [r_misc], [r_misc])
    for l in range(4):
        dve(lambda e, l=l: e.tensor_tensor(out=lbe[:, l, :], in0=lbe[:, l, :], in1=lbs[:], op=ALU.mult),
            [r_misc], [r_misc])
    r_lb = res("lbT")
    dve(lambda e: e.memset(lbT[:, 0, :, 0:1], 0.0), [], [r_lb])
    for l in range(1, 4):
        dve(lambda e, l=l: e.tensor_tensor(out=lbT[:, l, :, 0:1], in0=lbT[:, l - 1, :, 0:1],
                                           in1=lbe[:, l, :].unsqueeze(2), op=ALU.add), [r_misc, r_lb], [r_lb])
    dve(lambda e: e.tensor_scalar(out=lbT[:, :, :, 1:2], in0=lbT[:, :, :, 0:1], scalar1=-1.0, scalar2=1.0,
                                  op0=ALU.mult, op1=ALU.add), [r_lb], [r_lb])

    class WStream:
        def __init__(self):
            self.plan = {"A": [], "B": []}
            self.issued = {"A": 0, "B": 0}
            self.consumed = {"A": 0, "B": 0}
            self.released = set()

        def add(self, cls, src_fn, hold=None):
            self.plan[cls].append((src_fn, hold))

        def release(self, key):
            self.released.add(key)
            self.pump()

        def pump(self):
            for cls, nslot, rlist in (("A", NRA, r_ringA), ("B", 11, r_wdh)):
                while (self.issued[cls] < len(self.plan[cls]) and
                       self.issued[cls] - self.consumed[cls] < nslot):
                    i = self.issued[cls]
                    src_fn, hold = self.plan[cls][i]
                    if hold is not None and hold not in self.released:
                        break
                    slot = i % nslot
                    dst, src = src_fn(slot)
                    kb.dma("pool", dst, src, rlist[slot], [], [rlist[slot]])
                    self.issued[cls] += 1

        def take(self, cls, n=1):
            self.pump()
            i = self.consumed[cls]
            nslot = NRA if cls == "A" else 11
            assert i + n <= self.issued[cls], (cls, i, n, self.issued[cls])
            return i % nslot

        def done(self, cls, n=1):
            self.consumed[cls] += n
            self.pump()

    WS = WStream()

    def colblock_src(w, l, c0, ncols):
        def f(slot):
            dst = ringA[:, slot, 0:8 * ncols].rearrange("p (k c) -> p k c", c=ncols)
            src = w[l, :, c0:c0 + ncols].rearrange("(k p) c -> p k c", p=128)
            return dst, src
        return f

    def rowblock_src(w, l, r0):
        def f(slot):
            return wdh[:, slot, :], w[l, r0:r0 + 128, :]
        return f

    def wout_src(l, kk):
        def f(slot):
            dst = ringA[:, slot, 0:2048].rearrange("p (k c) -> p k c", c=1024)
            src = w_out[l, kk * 256:(kk + 1) * 256, :].rearrange("(k p) c -> p k c", p=128)
            return dst, src
        return f

    def glu_src(l):
        def f(slot):
            dst = ringA[:, slot, 0:2048].rearrange("p (k c) -> p k c", c=512)
            src = w_glu[l].rearrange("(k p) c -> p k c", p=128)
            return dst, src
        return f

    groups = [dict(T=TG1, x0=0, tiles=[(0, 80), (80, 592), (592, 1104)]),
              dict(T=TG2, x0=TG1, tiles=[(0, 512), (512, 1024)])]
    WIN_ORDER = [2, 4, 6, 8, 3, 5, 7, 9, 0, 1]
    for G in groups:
        for l in range(depth):
            for (wg, wu, wd) in ((w_f1g, w_f1u, w_f1d), (w_f2g, w_f2u, w_f2d)):
                for half in range(2):
                    for hb in range(11):
                        c0 = (half * 11 + hb) * 128
                        WS.add("A", colblock_src(wg, l, c0, 128))
                        WS.add("A", colblock_src(wu, l, c0, 128))
                    for hb in range(11):
                        hold = None
                        if wd is w_f2d and half == 0 and hb == 0 and mixer:
                            hold = "f2_%d_%d" % (groups.index(G), l)
                        WS.add("B", rowblock_src(wd, l, (half * 11 + hb) * 128), hold=hold)
                if wg is w_f1g and mixer:
                    for b in WIN_ORDER:
                        WS.add("A", colblock_src(w_in, l, b * 256, 256))
                    if s5:
                        WS.add("A", glu_src(l))
                    for kk in range(4):
                        WS.add("A", wout_src(l, kk))

    def tile_idx(G, c0):
        return [t[0] for t in G["tiles"]].index(c0)

    sqi = [0]
    r_sq = [res("sq%d" % i) for i in range(3)]
    r_rt = [res("rt0"), res("rt1")]
    r_sg = [res("sg0"), res("sg1")]
    psrot = [0]

    def rmsnorm_to_xn(G, gcol, nfeat_inv=1.0 / D):
        for ti, (c0, c1) in enumerate(G["tiles"]):
            n = c1 - c0
            pbi = 6
            for k in range(8):
                s = sqi[0] % 3
                sqi[0] += 1
                act(lambda e, k=k, s=s: e.activation(out=sq[:, s, 0:n], in_=hT[:, k, c0:c1], func=AF.Square),
                    [r_h[k][ti]], [r_sq[s]])
                pe(lambda e, k=k, s=s: e.matmul(PB[pbi][:, 0:n], lhsT=onesb[:], rhs=sq[:, s, 0:n],
                                                start=(k == 0), stop=(k == 7)),
                   [r_sq[s], r_ones], [PR[pbi]])
            j = ti % 2
            act(lambda e, j=j: e.activation(out=rt[:, j, 0:n], in_=PB[pbi][:, 0:n], func=AF.Ln, bias=epsT[:, 0:1],
                                            scale=nfeat_inv), [PR[pbi], r_misc], [r_rt[j]])
            act(lambda e, j=j: e.activation(out=rt[:, j, 0:n], in_=rt[:, j, 0:n], func=AF.Exp, scale=-0.5),
                [r_rt[j]], [r_rt[j]])
            for k in range(8):
                dve(lambda e, k=k, j=j: e.scalar_tensor_tensor(out=xn[:, k, c0:c1], in0=hT[:, k, c0:c1],
                                                               scalar=PV[:, gcol + k:gcol + k + 1], op0=ALU.mult,
                                                               in1=rt[:, j, 0:n], op1=ALU.mult),
                    [r_h[k][ti], r_rt[j], r_PV], [r_xn[k][ti]])

    def ffn(G, l):
        ntile = len(G["tiles"])
        for half in range(2):
            for hb in range(11):
                sg_ = WS.take("A", 2)
                su_ = (sg_ + 1) % NRA
                wgv = ringA[:, sg_, 0:1024].rearrange("p (k c) -> p k c", c=128)
                wuv = ringA[:, su_, 0:1024].rearrange("p (k c) -> p k c", c=128)
                for ti, (c0, c1) in enumerate(G["tiles"]):
                    n = c1 - c0
                    pg = psrot[0] % 2
                    psrot[0] += 1
                    bg, bu = 2 * pg, 2 * pg + 1
                    mm_group(PB[bg][:, 0:n], [(wgv[:, k, :], xn[:, k, c0:c1]) for k in range(8)],
                             [r_ringA[sg_]] + [r_xn[k][ti] for k in range(8)], [PR[bg]])
                    mm_group(PB[bu][:, 0:n], [(wuv[:, k, :], xn[:, k, c0:c1]) for k in range(8)],
                             [r_ringA[su_]] + [r_xn[k][ti] for k in range(8)], [PR[bu]])
                    act(lambda e, pg=pg, bg=bg: e.activation(out=sgt[:, pg, 0:n], in_=PB[bg][:, 0:n], func=AF.Silu),
                        [PR[bg]], [r_sg[pg]])
                    dve(lambda e, pg=pg, bu=bu, hb=hb: e.tensor_tensor(out=hid[:, hb, c0:c1], in0=sgt[:, pg, 0:n],
                                                                      in1=PB[bu][:, 0:n], op=ALU.mult),
                        [r_sg[pg], PR[bu]], [r_hid[hb][ti]])
                WS.done("A", 2)
            s0 = WS.take("B", 11)
            for ti, (c0, c1) in enumerate(G["tiles"]):
                n = c1 - c0
                for ot in range(8):
                    pbi = 4 + (psrot[0] % 2)
                    psrot[0] += 1
                    mm_group(PB[pbi][:, 0:n],
                             [(wdh[:, (s0 + hb) % 11, ot * 128:(ot + 1) * 128], hid[:, hb, c0:c1]) for hb in range(11)],
                             r_wdh + [r_hid[hb][ti] for hb in range(11)], [PR[pbi]])
                    dve(lambda e, pbi=pbi, ot=ot: e.scalar_tensor_tensor(out=hT[:, ot, c0:c1], in0=PB[pbi][:, 0:n],
                                                                        scalar=0.5, op0=ALU.mult,
                                                                        in1=hT[:, ot, c0:c1], op1=ALU.add),
                        [PR[pbi], r_h[ot][ti]], [r_h[ot][ti]])
            WS.done("B", 11)

    def load_x(G):
        T = G["T"]
        nblk = (T + 127) // 128
        for b in range(nblk):
            c0 = b * 128
            nr = min(128, T - c0)
            ti = None
            s = b % 2
            kb.dma("sp", stg[0:nr, s, :], xin[G["x0"] + c0:G["x0"] + c0 + nr, :], r_stg[s], [], [r_stg[s]])
            for kk in range(2):
                pbi = psrot[0] % 2
                psrot[0] += 1

                def fn(e, kk=kk, pbi=pbi, s=s, nr=nr):
                    ins = None
                    for k4 in range(4):
                        k = kk * 4 + k4
                        ins = e.transpose(PB[pbi][:, k4 * 128:k4 * 128 + nr], stg[0:nr, s, k * 128:(k + 1) * 128],
                                          identf[0:nr, 0:nr])
                    return ins
                pe(fn, [r_stg[s], r_identf], [PR[pbi]])
                tis = sorted(set(ti for ti, (a, bb) in enumerate(G["tiles"]) if a < c0 + nr and bb > c0))
                wr = [r_h[kk * 4 + k4][ti] for k4 in range(4) for ti in tis]
                eng = act if kk == 0 else dve
                if kk == 0:
                    act(lambda e, kk=kk, pbi=pbi, nr=nr, c0=c0: e.activation(
                        out=hT[:, kk * 4:(kk + 1) * 4, c0:c0 + nr],
                        in_=PB[pbi][:, :].rearrange("p (a t) -> p a t", t=128)[:, :, 0:nr], func=AF.Copy),
                        [PR[pbi]], wr)
                else:
                    dve(lambda e, kk=kk, pbi=pbi, nr=nr, c0=c0: e.tensor_copy(
                        out=hT[:, kk * 4:(kk + 1) * 4, c0:c0 + nr],
                        in_=PB[pbi][:, :].rearrange("p (a t) -> p a t", t=128)[:, :, 0:nr]),
                        [PR[pbi]], wr)

    def store_y(G):
        T = G["T"]
        for ti, (c0, c1) in enumerate(G["tiles"]):
            n = c1 - c0
            pbi = 6
            for k in range(8):
                s = sqi[0] % 3
                sqi[0] += 1
                act(lambda e, k=k, s=s: e.activation(out=sq[:, s, 0:n], in_=hT[:, k, c0:c1], func=AF.Square),
                    [r_h[k][ti]], [r_sq[s]])
                pe(lambda e, k=k, s=s: e.matmul(PB[pbi][:, 0:n], lhsT=onesb[:], rhs=sq[:, s, 0:n],
                                                start=(k == 0), stop=(k == 7)), [r_sq[s], r_ones], [PR[pbi]])
            j = ti % 2
            act(lambda e, j=j: e.activation(out=rt[:, j, 0:n], in_=PB[pbi][:, 0:n], func=AF.Ln, bias=epsT[:, 0:1],
                                            scale=1.0 / D), [PR[pbi], r_misc], [r_rt[j]])
            act(lambda e, j=j: e.activation(out=rt[:, j, 0:n], in_=rt[:, j, 0:n], func=AF.Exp, scale=-0.5),
                [r_rt[j]], [r_rt[j]])
            for k in range(8):
                dve(lambda e, k=k, j=j: e.scalar_tensor_tensor(out=hT[:, k, c0:c1], in0=hT[:, k, c0:c1],
                                                               scalar=PV[:, R_NFIN + k:R_NFIN + k + 1], op0=ALU.mult,
                                                               in1=rt[:, j, 0:n], op1=ALU.mult),
                    [r_h[k][ti], r_rt[j], r_PV], [r_h[k][ti]])
        nblk = (T + 127) // 128
        for b in range(nblk):
            c0 = b * 128
            nr = min(128, T - c0)
            s = b % 2
            tis = sorted(set(ti for ti, (a, bb) in enumerate(G["tiles"]) if a < c0 + nr and bb > c0))
            for kk in range(2):
                pbi = psrot[0] % 2
                psrot[0] += 1

                def fn(e, kk=kk, pbi=pbi, nr=nr, c0=c0):
                    ins = None
                    for k4 in range(4):
                        k = kk * 4 + k4
                        ins = e.transpose(PB[pbi][0:nr, k4 * 128:(k4 + 1) * 128], hT[:, k, c0:c0 + nr], identf[:, :])
                    return ins
                pe(fn, [r_identf] + [r_h[kk * 4 + k4][ti] for k4 in range(4) for ti in tis], [PR[pbi]])
                if kk == 0:
                    act(lambda e, pbi=pbi, nr=nr, s=s: e.activation(out=stg[0:nr, s, 0:512], in_=PB[pbi][0:nr, :],
                                                                    func=AF.Copy), [PR[pbi]], [r_stg[s]])
                else:
                    dve(lambda e, pbi=pbi, nr=nr, s=s: e.tensor_copy(out=stg[0:nr, s, 512:1024], in_=PB[pbi][0:nr, :]),
                        [PR[pbi]], [r_stg[s]])
            kb.dma("sp", yout[G["x0"] + c0:G["x0"] + c0 + nr, :], stg[0:nr, s, :], r_stg[s], [r_stg[s]], [])

    r_q = [res("qT%d" % i) for i in range(4)]
    r_k = [res("kT%d" % i) for i in range(4)]
    r_gate = [res("gate%d" % i) for i in range(4)]
    r_u = [res("uT%d" % i) for i in range(4)]
    r_Vb = {(bi_, hf_): res("Vtok%d_%d" % (bi_, hf_)) for bi_ in range(9) for hf_ in range(2)}
    r_Vall = list(r_Vb.values())
    r_tmp = [res("tmpA"), res("tmpB"), res("tmpC"), res("tmpD")]
    r_eb = res("eb")
    r_S = [[res("S%d_%d" % (l, h)) for h in range(4)] for l in range(DEPTH)]
    r_Sd = [res("Sd%d" % i) for i in range(4)]
    r_Sdb = [res("Sdb%d" % i) for i in range(4)]
    r_ktok = [res("ktok%d" % i) for i in range(4)]
    r_kms = [res("kms%d" % i) for i in range(4)]
    r_Am = [res("Am%d" % i) for i in range(4)]
    r_hgst = [dres("hgst%d" % i) for i in range(4)]
    r_hgout = [dres("hgout%d" % i) for i in range(7)]
    hgo = [hgout[:, i, :] for i in range(3)] + [Sd[:, i, :] for i in range(4)]
    r_wo = res("wo")
    r_s5t = res("s5tables")
    r_s5sm = res("s5sm")
    r_car = res("car")
    r_carq = [res("carq%d" % i) for i in range(4)]
    cnt = dict(sd=0, kt=0, am=0, hgst=0, hgout=0, km=0)
    S5A1, S5A2 = [], []

    def mixer_fn(G, l, gi):
        T = G["T"]
        tiles = G["tiles"]
        ntile = len(tiles)
        mix_res = r_q + r_k + r_gate + r_u + r_Vall
        ffn_res = [x for a in r_hid for x in a] + r_wdh
        kb.alias(mix_res, ffn_res)
        rm_res = res("rmask")
        hg2 = r_tmp + [res("eq%d" % h) for h in range(4)] + [rm_res]
        kb.alias(hg2, S5A2)
        rmsnorm_to_xn(G, R_NMX + 8 * l)
        if True:
            dve(lambda e: e.memset(rmask[:, 0:T], 1.0), [], [rm_res])
            if gi == 0:
                dve(lambda e: e.memset(rmask[:, 0:64].rearrange("p (a b) -> p a b", b=4)[:, :, 0:1], 0.0), [rm_res], [rm_res])
                dve(lambda e: e.memset(rmask[:, 64:65], 0.0), [rm_res], [rm_res])
                dve(lambda e: e.memset(rmask[:, 80:TG1].rearrange("p (a b) -> p a b", b=64)[:, :, 0:1], 0.0), [rm_res], [rm_res])
            else:
                dve(lambda e: e.memset(rmask[:, 0:T].rearrange("p (a b) -> p a b", b=64)[:, :, 0:1], 0.0), [rm_res], [rm_res])
        if gi == 0:
            chunks = [(64, 80)] + [(80 + 64 * i, 144 + 64 * i) for i in range(16)]
            tokblocks = [(0, 80)] + [(80 + 128 * i, 208 + 128 * i) for i in range(8)]
        else:
            chunks = [(64 * i, 64 * i + 64) for i in range(16)]
            tokblocks = [(128 * i, 128 * i + 128) for i in range(8)]
        nchunk_all = len(chunks) + (16 if gi == 0 else 0)

        def xn_reads(ti):
            return [r_xn[k][ti] for k in range(8)]

        for b in WIN_ORDER:
            slot = WS.take("A")
            wv = ringA[:, slot, 0:2048].rearrange("p (k c) -> p k c", c=256)
            for j in range(2):
                zt = 2 * b + j
                hd = zt % 4
                kind = zt // 4
                if kind == 2:
                    continue
                for ti, (c0, c1) in enumerate(tiles):
                    n = c1 - c0
                    pbi = psrot[0] % 4
                    psrot[0] += 1
                    mm_group(PB[pbi][:, 0:n], [(wv[:, k, j * 128:(j + 1) * 128], xn[:, k, c0:c1]) for k in range(8)],
                             [r_ringA[slot]] + xn_reads(ti), [PR[pbi]])
                    if kind == 1:
                        act(lambda e, pbi=pbi: e.activation(out=tmpA[:, c0:c1], in_=PB[pbi][:, 0:n], func=AF.Sigmoid),
                            [PR[pbi]], [r_tmp[0]])
                        act(lambda e, pbi=pbi, hd=hd: e.activation(out=kT[:, hd, c0:c1], in_=PB[pbi][:, 0:n],
                                                                   func=AF.Sigmoid, scale=-1.0), [PR[pbi]], [r_k[hd]])
                    elif kind == 0:
                        act(lambda e, pbi=pbi, hd=hd: e.activation(out=qT[:, hd, c0:c1], in_=PB[pbi][:, 0:n],
                                                                   func=AF.Silu), [PR[pbi]], [r_q[hd]])
                    elif kind == 3:
                        act(lambda e, pbi=pbi, hd=hd: e.activation(out=gate[:, hd, c0:c1], in_=PB[pbi][:, 0:n],
                                                                   func=AF.Silu), [PR[pbi]], [r_gate[hd]])
                    else:
                        act(lambda e, pbi=pbi, hd=hd: e.activation(out=uT[:, hd, c0:c1], in_=PB[pbi][:, 0:n],
                                                                   func=AF.Copy), [PR[pbi]], [r_u[hd]])
                if kind == 0:
                    dve(lambda e, hd=hd: e.tensor_tensor(out=qT[:, hd, 0:T], in0=qT[:, hd, 0:T],
                                                         in1=arena2[:, (4 + hd) * TG1:(4 + hd) * TG1 + T], op=ALU.mult),
                        [r_q[hd], res("eq%d" % hd)], [r_q[hd]])
                if kind == 1:
                    lb0 = lbT[:, l, hd, 0:1]
                    lb1 = lbT[:, l, hd, 1:2]
                    dve(lambda e: e.tensor_scalar(out=tmpA[:, 0:T], in0=tmpA[:, 0:T], scalar1=lb1, scalar2=lb0,
                                                  op0=ALU.mult, op1=ALU.add), [r_tmp[0], r_lb], [r_tmp[0]])
                    act(lambda e: e.activation(out=tmpB[:, 0:T], in_=tmpA[:, 0:T], func=AF.Ln), [r_tmp[0]], [r_tmp[1]])
                    dve(lambda e: e.tensor_tensor_scan(out=tmpC[:, 0:T], data0=rmask[:, 0:T], data1=tmpB[:, 0:T],
                                                       initial=0.0, op0=ALU.mult, op1=ALU.add),
                        [r_tmp[1], rm_res], [r_tmp[2]])
                    segs = []
                    if gi == 0:
                        segs.append((0, 64, 4))
                        segs.append((64, 80, 16))
                        segs.append((80, TG1, 64))
                    else:
                        segs.append((0, T, 64))
                    ebo = 0
                    for (a0, a1, L) in segs:
                        nch = (a1 - a0) // L
                        v = tmpC[:, a0:a1].rearrange("p (a b) -> p a b", b=L)
                        dve(lambda e, v=v, a0=a0, a1=a1, L=L, nch=nch: e.tensor_tensor(
                            out=tmpB[:, a0:a1].rearrange("p (a b) -> p a b", b=L), in0=v,
                            in1=v[:, :, L - 1:L].to_broadcast([128, nch, L]), op=ALU.subtract),
                            [r_tmp[2]], [r_tmp[1]])
                        act(lambda e, v=v, L=L, nch=nch, ebo=ebo, hd=hd: e.activation(
                            out=eb[:, hd, ebo:ebo + nch].unsqueeze(2), in_=v[:, :, L - 1:L], func=AF.Exp),
                            [r_tmp[2]], [r_eb])
                        ebo += nch
                    act(lambda e: e.activation(out=tmpC[:, 0:T], in_=tmpB[:, 0:T], func=AF.Exp, scale=-1.0),
                        [r_tmp[1]], [r_tmp[2]])
                    act(lambda e, hd=hd: e.activation(out=arena2[:, (4 + hd) * TG1:(4 + hd) * TG1 + T],
                                                      in_=tmpB[:, 0:T], func=AF.Exp), [r_tmp[1]], [res("eq%d" % hd)])
                    dve(lambda e, hd=hd: e.scalar_tensor_tensor(out=kT[:, hd, 0:T], in0=kT[:, hd, 0:T], scalar=lb1,
                                                                op0=ALU.mult, in1=tmpC[:, 0:T], op1=ALU.mult),
                        [r_k[hd], r_tmp[2], r_lb], [r_k[hd]])
            if b in (4, 5):
                for bi, (t0, t1) in enumerate(tokblocks):
                    nt = t1 - t0
                    pbi = 4 + (psrot[0] % 2)
                    psrot[0] += 1
                    tis = sorted(set(ti for ti, (a, bb) in enumerate(tiles) if a < t1 and bb > t0))
                    mm_group(PB[pbi][0:nt, 0:256], [(xn[:, k, t0:t1], wv[:, k, :]) for k in range(8)],
                             [r_ringA[slot]] + [r_xn[k][ti] for k in range(8) for ti in tis], [PR[pbi]])
                    cb = (b - 4) * 256
                    if bi % 2 == 0:
                        act(lambda e, pbi=pbi, nt=nt, bi=bi, cb=cb: e.activation(out=Vtok[0:nt, bi, cb:cb + 256],
                                                                                in_=PB[pbi][0:nt, 0:256], func=AF.Copy),
                            [PR[pbi]], [r_Vb[(bi, b - 4)]])
                    else:
                        dve(lambda e, pbi=pbi, nt=nt, bi=bi, cb=cb: e.tensor_copy(out=Vtok[0:nt, bi, cb:cb + 256],
                                                                                 in_=PB[pbi][0:nt, 0:256]),
                            [PR[pbi]], [r_Vb[(bi, b - 4)]])
            WS.done("A", 1)

        if hgrn:
            hgrn_fn(G, l, gi, chunks, tokblocks)
        else:
            for hd in range(4):
                dve(lambda e, hd=hd: e.memset(xn[:, hd, 0:T], 0.0), [], [r_xn[hd][ti] for ti in range(ntile)])
        if s5:
            s5_fn(G, l, gi)
        else:
            for hd in range(4):
                dve(lambda e, hd=hd: e.memset(xn[:, 4 + hd, 0:T], 0.0), [], [r_xn[4 + hd][ti] for ti in range(ntile)])
            if True:
                WS.take("A") if False else None
        kb.alias(r_wdh, mix_res + S5A1)
        WS.release("f2_%d_%d" % (gi, l))
        sA = WS.take("A", 4)
        for ti, (c0, c1) in enumerate(tiles):
            n = c1 - c0
            for ot in range(8):
                pbi = 4 + (psrot[0] % 2)
                psrot[0] += 1
                mm_group(PB[pbi][:, 0:n],
                         [(ringA[:, (sA + k // 2) % NRA, (k % 2) * 1024 + ot * 128:(k % 2) * 1024 + (ot + 1) * 128],
                           xn[:, k, c0:c1]) for k in range(8)],
                         r_ringA + [r_xn[k][ti] for k in range(8)], [PR[pbi]])
                dve(lambda e, pbi=pbi, ot=ot: e.tensor_tensor(out=hT[:, ot, c0:c1], in0=PB[pbi][:, 0:n],
                                                              in1=hT[:, ot, c0:c1], op=ALU.add),
                    [PR[pbi], r_h[ot][ti]], [r_h[ot][ti]])
        WS.done("A", 4)
        kb.alias([x for a in r_hid for x in a], mix_res + S5A1)

    def hg_norm_out(G, l, hd, ti, c0, c1, pbo):
        n = c1 - c0
        s = sqi[0] % 3
        sqi[0] += 1
        act(lambda e: e.activation(out=sq[:, s, 0:n], in_=PB[pbo][:, 0:n], func=AF.Square), [PR[pbo]], [r_sq[s]])
        pe(lambda e: e.matmul(PB[6][:, 0:n], lhsT=onesb[:], rhs=sq[:, s, 0:n], start=True, stop=True),
           [r_sq[s], r_ones], [PR[6]])
        j = sqi[0] % 2
        act(lambda e: e.activation(out=rt[:, j, 0:n], in_=PB[6][:, 0:n], func=AF.Ln, bias=epsT[:, 0:1],
                                        scale=1.0 / 128), [PR[6], r_misc], [r_rt[j]])
        act(lambda e: e.activation(out=rt[:, j, 0:n], in_=rt[:, j, 0:n], func=AF.Exp, scale=-0.5),
            [r_rt[j]], [r_rt[j]])
        dve(lambda e: e.tensor_tensor(out=sgt[:, j, 0:n], in0=PB[pbo][:, 0:n], in1=rt[:, j, 0:n], op=ALU.mult),
            [PR[pbo], r_rt[j]], [r_sg[j]])
        gcol = R_HGN + 4 * l + hd
        dve(lambda e: e.scalar_tensor_tensor(out=xn[:, hd, c0:c1], in0=sgt[:, j, 0:n], scalar=PV[:, gcol:gcol + 1],
                                             op0=ALU.mult, in1=gate[:, hd, c0:c1], op1=ALU.mult),
            [r_sg[j], r_gate[hd], r_PV], [r_xn[hd][ti]])

    def hgrn_fn(G, l, gi, chunks, tokblocks):
        tiles = G["tiles"]
        r_pu = [PR[4 + h % 2] for h in range(4)]
        r_psc = [PR[4 + h % 2] for h in range(4)]
        r_ptr = [PRb for h in range(4)]
        pu = [PB[4 + h % 2][:, 0:128] for h in range(4)]
        psc = [PB[4 + h % 2][:, 0:128] for h in range(4)]
        ptr = [PBb[:, 0:128] for h in range(4)]
        for ti, (c0t, c1t) in enumerate(tiles):
            tch = [c for c in chunks if c[0] >= c0t and c[1] <= c1t]
            tbl = [(bi, tb) for bi, tb in enumerate(tokblocks) if tb[0] >= c0t and tb[1] <= c1t]
            for (bi, (t0, t1)) in tbl:
                nb = t1 - t0
                for hd in range(4):
                    mm_group(psc[hd][0:nb, 0:nb], [(kT[:, hd, t0:t1], qT[:, hd, t0:t1])], [r_k[hd], r_q[hd]],
                             [r_psc[hd]])
                    msk = mask0 if (gi == 0 and bi == 0) else maskP
                    dve(lambda e, hd=hd, msk=msk: e.tensor_tensor(out=Am[0:nb, hd, 0:nb], in0=psc[hd][0:nb, 0:nb],
                                                                  in1=msk[0:nb, 0:nb], op=ALU.mult),
                        [r_psc[hd], r_masks], [r_Am[hd]])
                    pe(lambda e, hd=hd: e.transpose(ptr[hd][0:nb, 0:128], kT[:, hd, t0:t1], identb[:, :]),
                       [r_k[hd], r_identb], [r_ptr[hd]])
                    act(lambda e, hd=hd: e.activation(out=ktok[0:nb, hd, :], in_=ptr[hd][0:nb, 0:128], func=AF.Copy),
                        [r_ptr[hd]], [r_ktok[hd]])
                if gi == 0 and bi == 0:
                    for sq_ in range(NSEQ):
                        hss = []
                        for hd in range(4):
                            hs = cnt["hgst"] % 4
                            cnt["hgst"] += 1
                            hss.append(hs)
                            kb.dma("sp", hgst[:, hs, :], st_hg[l, sq_, hd], r_hgst[hs], [], [r_hgst[hs]])
                        for hd in range(4):
                            dve(lambda e, hd=hd, sq_=sq_: e.tensor_scalar(out=kms[:, hd, :], in0=ktok[0:64, hd, :],
                                                                          scalar1=rowm[:, sq_:sq_ + 1], scalar2=None,
                                                                          op0=ALU.mult),
                                [r_ktok[hd], r_masks], [r_kms[hd]])
                        for hd in range(4):
                            act(lambda e, hd=hd, hs=hss[hd], sq_=sq_: e.activation(
                                out=Sdb[:, hd, :], in_=hgst[:, hs, :], func=AF.Copy, scale=eb[:, hd, sq_:sq_ + 1]),
                                [r_hgst[hss[hd]], r_eb], [r_Sdb[hd]])
                        for pr in range(2):
                            def fu(e, pr=pr):
                                ins = None
                                for hd in (pr, pr + 2):
                                    ins = e.matmul(PB[4 + pr][:, 128 * (hd // 2):128 * (hd // 2) + 128],
                                                   lhsT=kms[:, hd, :], rhs=Vtok[0:64, 0, hd * 128:(hd + 1) * 128],
                                                   start=True, stop=True)
                                return ins
                            pe(fu, [r_kms[pr], r_kms[pr + 2], r_Vb[(0, 0)], r_Vb[(0, 1)]], [PR[4 + pr]])
                        for hd in range(4):
                            pe(lambda e, hd=hd, sq_=sq_: e.matmul(PB[hd][:, 4 * sq_:4 * sq_ + 4], lhsT=Sdb[:, hd, :],
                                                                  rhs=qT[:, hd, 4 * sq_:4 * sq_ + 4],
                                                                  start=(sq_ == 0), stop=False),
                               [r_Sdb[hd], r_q[hd]], [PR[hd]])
                        for hd in range(4):
                            ho = cnt["hgout"] % 7
                            cnt["hgout"] += 1
                            pr = hd % 2
                            dve(lambda e, ho=ho, hd=hd, pr=pr, hs=hss[hd], sq_=sq_: e.scalar_tensor_tensor(
                                out=hgo[ho], in0=hgst[:, hs, :], scalar=eb[:, hd, sq_:sq_ + 1], op0=ALU.mult,
                                in1=PB[4 + pr][:, 128 * (hd // 2):128 * (hd // 2) + 128], op1=ALU.add),
                                [PR[4 + pr], r_hgst[hss[hd]], r_eb], [r_hgout[ho]])
                            kb.dma("act", o_hgs[l, sq_, hd], hgo[ho], r_hgout[ho], [r_hgout[ho]], [])
                    for hd in range(4):
                        pe(lambda e, hd=hd: e.matmul(PB[hd][:, 0:64], lhsT=Vtok[0:64, 0, hd * 128:(hd + 1) * 128],
                                                     rhs=Am[0:64, hd, 0:64], start=False, stop=True),
                           [r_Vb[(0, hd // 2)], r_Am[hd]], [PR[hd]])
                    blk_chunks = [(64, 80, 16)]
                else:
                    blk_chunks = [(c[0], c[1], (17 if gi == 0 else 0) + chunks.index(c) - (1 if gi == 0 else 0))
                                  for c in tch if c[0] >= t0 and c[1] <= t1]
                for (c0, c1, ecol_i) in blk_chunks:
                    L = c1 - c0
                    pb_ = c0 - t0
                    for pr in range(2):
                        def fu(e, pr=pr, pb_=pb_, L=L, bi=bi):
                            ins = None
                            for hd in (pr, pr + 2):
                                ins = e.matmul(PB[4 + pr][:, 128 * (hd // 2):128 * (hd // 2) + 128],
                                               lhsT=ktok[pb_:pb_ + L, hd, :],
                                               rhs=Vtok[pb_:pb_ + L, bi, hd * 128:(hd + 1) * 128], start=True, stop=True)
                            return ins
                        pe(fu, [r_ktok[pr], r_ktok[pr + 2], r_Vb[(bi, 0)], r_Vb[(bi, 1)]], [PR[4 + pr]])
                    for hd in range(4):
                        act(lambda e, hd=hd, ecol_i=ecol_i: e.activation(out=Sdb[:, hd, :], in_=Sst[:, l, hd, :],
                                                                         func=AF.Copy,
                                                                         scale=eb[:, hd, ecol_i:ecol_i + 1]),
                            [r_S[l][hd], r_eb], [r_Sdb[hd]])
                    for hd in range(4):
                        def fo(e, hd=hd, c0=c0, c1=c1, pb_=pb_, L=L, bi=bi):
                            e.matmul(PB[hd][:, c0 - c0t:c1 - c0t], lhsT=Sdb[:, hd, :], rhs=qT[:, hd, c0:c1],
                                     start=True, stop=False)
                            return e.matmul(PB[hd][:, c0 - c0t:c1 - c0t],
                                            lhsT=Vtok[pb_:pb_ + L, bi, hd * 128:(hd + 1) * 128],
                                            rhs=Am[pb_:pb_ + L, hd, pb_:pb_ + L], start=False, stop=True)
                        pe(fo, [r_Sdb[hd], r_q[hd], r_Vb[(bi, hd // 2)], r_Am[hd]], [PR[hd]])
                    for hd in range(4):
                        pr = hd % 2
                        dve(lambda e, hd=hd, pr=pr, ecol_i=ecol_i: e.scalar_tensor_tensor(
                            out=Sst[:, l, hd, :], in0=Sst[:, l, hd, :], scalar=eb[:, hd, ecol_i:ecol_i + 1], op0=ALU.mult,
                            in1=PB[4 + pr][:, 128 * (hd // 2):128 * (hd // 2) + 128], op1=ALU.add),
                            [PR[4 + pr], r_S[l][hd], r_eb], [r_S[l][hd]])
            for hd in range(4):
                hg_norm_out(G, l, hd, ti, c0t, c1t, hd)
        if gi == 1:
            for hd in range(4):
                ho = cnt["hgout"] % 7
                cnt["hgout"] += 1
                dve(lambda e, ho=ho, hd=hd: e.tensor_copy(out=hgo[ho], in_=Sst[:, l, hd, :]),
                    [r_S[l][hd]], [r_hgout[ho]])
                kb.dma("sp", o_hgp[l, hd], hgo[ho], r_hgout[ho], [r_hgout[ho]], [])

    def s5_tables(l):
        sm = lambda i: s5sm[:, i, :]
        LRE, LIM, LDT = (S5P[:, l * 48 + i * 16:l * 48 + (i + 1) * 16] for i in range(3))
        DT, LR, MAG, ANG, Q, RS, RC, SN, CS, AR, AI, DEN, CR, CI, T1, T2, M1 = (sm(i) for i in range(17))
        r = r_s5sm
        rp = res("S5Pv")
        act(lambda e: e.activation(out=DT, in_=LDT, func=AF.Exp), [r_PV], [r])
        dve(lambda e: e.tensor_scalar(out=LR, in0=LRE, scalar1=-1e-4, scalar2=None, op0=ALU.min), [r_PV], [r])
        dve(lambda e: e.tensor_tensor(out=T1, in0=LR, in1=DT, op=ALU.mult), [r], [r])
        act(lambda e: e.activation(out=MAG, in_=T1, func=AF.Exp), [r], [r])
        dve(lambda e: e.tensor_tensor(out=ANG, in0=LIM, in1=DT, op=ALU.mult), [r, r_PV], [r])

        T2b, Qb, M1b = sm(24), sm(25), sm(26)
        ch = [dict(dst=RS, shift=0.0, T2=T2, Q=Q, M1=M1, I=s5i[:, 0, :], res=res("s5redA")),
              dict(dst=RC, shift=math.pi / 2, T2=T2b, Q=Qb, M1=M1b, I=s5i[:, 1, :], res=res("s5redB"))]
        steps = [
            lambda e, c: e.tensor_scalar(out=c["T2"], in0=ANG, scalar1=c["shift"], scalar2=None, op0=ALU.add),
            lambda e, c: e.tensor_scalar(out=c["Q"], in0=c["T2"], scalar1=1.0 / TWO_PI, scalar2=None, op0=ALU.mult),
            lambda e, c: e.tensor_copy(out=c["I"], in_=c["Q"]),
            lambda e, c: e.tensor_copy(out=c["Q"], in_=c["I"]),
            lambda e, c: e.scalar_tensor_tensor(out=c["dst"], in0=c["Q"], scalar=-TWO_PI, op0=ALU.mult, in1=c["T2"],
                                                op1=ALU.add),
            lambda e, c: e.tensor_scalar(out=c["M1"], in0=c["dst"], scalar1=math.pi, scalar2=None, op0=ALU.is_gt),
            lambda e, c: e.scalar_tensor_tensor(out=c["dst"], in0=c["M1"], scalar=-TWO_PI, op0=ALU.mult, in1=c["dst"],
                                                op1=ALU.add),
            lambda e, c: e.tensor_scalar(out=c["M1"], in0=c["dst"], scalar1=-math.pi, scalar2=None, op0=ALU.is_lt),
            lambda e, c: e.scalar_tensor_tensor(out=c["dst"], in0=c["M1"], scalar=TWO_PI, op0=ALU.mult, in1=c["dst"],
                                                op1=ALU.add),
        ]
        for si, stp in enumerate(steps):
            for c in ch:
                rd = [r] if si == 0 else [c["res"]]
                dve(lambda e, stp=stp, c=c: stp(e, c), rd, [c["res"]])
        r_redA, r_redB = ch[0]["res"], ch[1]["res"]
        act(lambda e: e.activation(out=SN, in_=RS, func=AF.Sin), [r, r_redA], [r])
        act(lambda e: e.activation(out=CS, in_=RC, func=AF.Sin), [r, r_redB], [r])
        dve(lambda e: e.tensor_tensor(out=AR, in0=MAG, in1=CS, op=ALU.mult), [r], [r])
        dve(lambda e: e.tensor_tensor(out=AI, in0=MAG, in1=SN, op=ALU.mult), [r], [r])
        dve(lambda e: e.tensor_tensor(out=DEN, in0=LR, in1=LR, op=ALU.mult), [r], [r])
        dve(lambda e: e.tensor_tensor(out=T1, in0=LIM, in1=LIM, op=ALU.mult), [r, r_PV], [r])
        dve(lambda e: e.tensor_tensor(out=DEN, in0=DEN, in1=T1, op=ALU.add), [r], [r])
        dve(lambda e: e.reciprocal(out=DEN, in_=DEN), [r], [r])
        dve(lambda e: e.tensor_scalar(out=T1, in0=AR, scalar1=-1.0, scalar2=None, op0=ALU.add), [r], [r])
        dve(lambda e: e.tensor_tensor(out=CR, in0=T1, in1=LR, op=ALU.mult), [r], [r])
        dve(lambda e: e.tensor_tensor(out=T2, in0=AI, in1=LIM, op=ALU.mult), [r, r_PV], [r])
        dve(lambda e: e.tensor_tensor(out=CR, in0=CR, in1=T2, op=ALU.add), [r], [r])
        dve(lambda e: e.tensor_tensor(out=CR, in0=CR, in1=DEN, op=ALU.mult), [r], [r])
        dve(lambda e: e.tensor_tensor(out=CI, in0=AI, in1=LR, op=ALU.mult), [r], [r])
        dve(lambda e: e.tensor_tensor(out=T2, in0=T1, in1=LIM, op=ALU.mult), [r, r_PV], [r])
        dve(lambda e: e.tensor_tensor(out=CI, in0=CI, in1=T2, op=ALU.subtract), [r], [r])
        dve(lambda e: e.tensor_tensor(out=CI, in0=CI, in1=DEN, op=ALU.mult), [r], [r])
        return dict(MAG=MAG, SN=SN, CS=CS, CR=CR, CI=CI)

    r_Braw = dres("Braw")
    r_Craw = dres("Craw")
    r_BB = res("BB")
    r_srcB = res("srcB")
    r_srcBL = [[res("srcB%d_%d" % (a_, b_)) for b_ in range(2)] for a_ in range(2)]
    r_srcB_all = [x for a_ in r_srcBL for x in a_]
    r_BBL = [res("BB0"), res("BB1")]
    r_srcCL = [[res("srcC%d_%d" % (a_, b_)) for b_ in range(2)] for a_ in range(2)]
    r_srcC_all = [x for a_ in r_srcCL for x in a_]
    r_srcC = res("srcC")
    r_BtL = {(a_, ri_, eo_): res("Bt%d_%d_%d" % (a_, ri_, eo_)) for a_ in range(4) for ri_ in range(2) for eo_ in range(2)}
    r_CtL = {(st_, ri_): res("Ct%d_%d" % (st_, ri_)) for st_ in range(16) for ri_ in range(2)}
    r_Bt_all = list(r_BtL.values())
    r_Ct_all = list(r_CtL.values())
    r_X0 = res("X0")
    r_tq = [res("tq%d" % i) for i in range(4)]
    r_rin = [res("rin0"), res("rin1")]
    r_rr = [res("rr0"), res("rr1")]
    r_xTb = res("xTb")
    r_yfp = res("yfp")
    r_ygb = res("ygb")
    r_coef0 = res("coef0")
    r_scr = [res("scr%d" % i) for i in range(DEPTH)]
    r_scrS, r_scrL = dres("scrS"), dres("scrL")
    r_pq = [res("pq0"), res("pq1")]
    r_rr2 = [r_rr, [res("rrB0"), res("rrB1")]]
    r_rrs = [[[res("rrs%d_%d_%d" % (b_, ri_, s4_)) for s4_ in range(4)] for ri_ in range(2)] for b_ in range(2)]
    r_carq2 = [[res("carq%d_%d" % (i, ri_)) for ri_ in range(2)] for i in range(4)]
    r_xTb2 = [r_xTb, res("xTbB")]
    r_w1, r_w2 = res("w1f"), res("w2f")
    r_yfp2 = [r_yfp, res("yfpB")]

    S5A2.extend(r_tq + r_rin + r_rr + r_pq + [x for a_ in r_rrs[0] for x in a_] + [r_s5t, r_coef0, r_yfp, r_srcB, r_srcC, r_BB, r_Braw, r_Craw] + r_srcB_all + r_BBL + r_srcC_all)
    S5A1.extend([r_xTb, r_ygb] + r_Ct_all + [r_xTb2[1], r_w1, r_w2] + r_rr2[1] + [x for a_ in r_rrs[1] for x in a_])

    def s5_fn(G, l, gi):
        T = G["T"]
        tiles = G["tiles"]
        kb.alias(S5A2, r_tmp + [res("eq%d" % h) for h in range(4)] + [res("rmask")])
        kb.alias(S5A1, r_q + r_k + r_gate + r_Vall)
        kb.alias([r_yfp2[1]], r_sg)
        cached = (gi == 1)
        if cached:
            kb.defer = []
        dve(lambda e: e.memset(Ct[:], 0.0), [], r_Ct_all)
        dve(lambda e: e.memset(srcB[:], 0.0), [], r_srcB_all)
        kb.dma("sp", Braw[:, 0, :, :], b_re[l].rearrange("(a q) h -> q a h", q=128), r_Braw, [], [r_Braw])
        kb.dma("sp", Braw[:, 1, :, :], b_im[l].rearrange("(a q) h -> q a h", q=128), r_Braw, [], [r_Braw])
        kb.dma("sp", Craw[:, 0, :, :], c_re[l].rearrange("(a q) p -> q a p", q=128), r_Craw, [], [r_Craw])
        kb.dma("sp", Craw[:, 1, :, :], c_im[l].rearrange("(a q) p -> q a p", q=128), r_Craw, [], [r_Craw])
        tb = s5_tables(l)
        MAG, SN, CS, CR, CI = tb["MAG"], tb["SN"], tb["CS"], tb["CR"], tb["CI"]
        r = r_s5sm
        bc = lambda v: v.unsqueeze(2).to_broadcast([128, 16, 16])
        tmpX = srcC[:, :, :].rearrange("p a c -> p (a c)").rearrange("p (a c) -> p a c", c=16)
        rX, rY = res("bbX"), res("bbY")
        dve(lambda e: e.tensor_tensor(out=BB[:, 0], in0=Braw[:, 0], in1=bc(CR), op=ALU.mult), [r_Braw, r], [r_BBL[0]])
        dve(lambda e: e.tensor_tensor(out=BB[:, 1], in0=Braw[:, 1], in1=bc(CR), op=ALU.mult), [r_Braw, r], [r_BBL[1]])
        dve(lambda e: e.tensor_tensor(out=tmpX, in0=Braw[:, 1], in1=bc(CI), op=ALU.mult), [r_Braw, r] + r_srcC_all, [rX])
        dve(lambda e: e.tensor_tensor(out=Braw[:, 0], in0=Braw[:, 0], in1=bc(CI), op=ALU.mult), [r_Braw, r, r_BBL[0]],
            [rY])
        dve(lambda e: e.tensor_tensor(out=BB[:, 0], in0=BB[:, 0], in1=tmpX, op=ALU.subtract), [r_BBL[0], rX], [r_BBL[0]])
        dve(lambda e: e.tensor_tensor(out=BB[:, 1], in0=BB[:, 1], in1=Braw[:, 0], op=ALU.add), [r_BBL[1], rY], [r_BBL[1]])
        for h in range(2):
            for ri in range(2):
                for eo in range(2):
                    v_src = BB[64 * h:64 * h + 64, ri, :, :].rearrange("p (a q) c -> p a q c", q=4)[:, :, eo::2, :]
                    v_dst = srcB[64 * h:64 * h + 64, ri, eo, :, 16 * h:16 * h + 16].rearrange(
                        "p (a q) c -> p a q c", q=4)[:, :, eo::2, :]
                    dve(lambda e, v_dst=v_dst, v_src=v_src: e.tensor_copy(out=v_dst, in_=v_src),
                        [r_BBL[ri]], [r_srcBL[ri][eo]])
        for a in range(4):
            for ri in range(2):
                for eo, Bt in ((0, BtE), (1, BtO)):
                    pbt = 5 + (a * 4 + ri * 2 + eo) % 2
                    pe(lambda e, a=a, ri=ri, eo=eo, pbt=pbt: e.transpose(
                        PB[pbt][:, 0:128], srcB[:, ri, eo, 4 * a:4 * a + 4, :].rearrange("p a c -> p (a c)"), identf[:, :]),
                        [r_srcBL[ri][eo], r_identf], [PR[pbt]])
                    act(lambda e, a=a, ri=ri, Bt=Bt, pbt=pbt: e.activation(out=Bt[:, a, ri, :], in_=PB[pbt][:, 0:128],
                                                                        func=AF.Copy),
                        [PR[pbt]], [r_BtL[(a, ri, eo)]])
        for ot in range(4):
            for ri in range(2):
                m0 = evod[:, 0:1] if ri == 0 else evod[:, 2:3]
                m1 = evod[:, 1:2] if ri == 0 else evod[:, 3:4]
                dve(lambda e, ot=ot, ri=ri, m0=m0: e.tensor_scalar(out=srcC[:, ri, 0:64], in0=Craw[:, ri, ot, :],
                                                                   scalar1=m0, scalar2=None, op0=ALU.mult),
                    [r_Craw, r_misc, rX], [r_srcCL[ri][0]])
                dve(lambda e, ot=ot, ri=ri, m1=m1: e.tensor_scalar(out=srcC[:, ri, 64:128], in0=Craw[:, ri, ot, :],
                                                                   scalar1=m1, scalar2=None, op0=ALU.mult),
                    [r_Craw, r_misc, rX], [r_srcCL[ri][1]])
                pbt = 5 + (ot * 2 + ri) % 2
                pe(lambda e, ri=ri, pbt=pbt: e.transpose(PB[pbt][:, 0:128], srcC[:, ri, :], identf[:, :]),
                   r_srcCL[ri] + [r_identf], [PR[pbt]])
                for j in range(4):
                    act(lambda e, ot=ot, ri=ri, j=j, pbt=pbt: e.activation(out=Ct[:, 4 * ot + j, ri, 32 * j:32 * j + 32],
                                                                           in_=PB[pbt][:, 32 * j:32 * j + 32],
                                                                           func=AF.Copy),
                        [PR[pbt]], [r_CtL[(4 * ot + j, ri)]])
        rt_ = r_s5t
        r_cosT, r_sinT = res("cosTw"), res("sinTw")
        kb.alias([r_cosT, r_sinT], [rt_])
        dve(lambda e: e.tensor_copy(out=cosT[:, :, 0:1], in_=CS.unsqueeze(2)), [r, rt_], [r_cosT])
        dve(lambda e: e.tensor_copy(out=sinT[:, :, 0:1], in_=SN.unsqueeze(2)), [r, rt_], [r_sinT])
        m = 1
        while m < 128:
            parts = [(0, 16)] if 16 * m <= 512 else [(i * (512 // m), (i + 1) * (512 // m)) for i in range(16 * m // 512)]
            for (s0, s1) in parts:
                ns = s1 - s0
                c_lo, s_lo = cosT[:, s0:s1, 0:m], sinT[:, s0:s1, 0:m]
                cmb = cosT[:, s0:s1, m - 1:m].to_broadcast([128, ns, m])
                smb = sinT[:, s0:s1, m - 1:m].to_broadcast([128, ns, m])
                f1, f2, f3, f4 = (tq[i][:, :, :].rearrange("p a t -> p (a t)")[:, 0:ns * m].rearrange(
                    "p (a t) -> p a t", t=m) for i in range(4))
                dve(lambda e, c_lo=c_lo, cmb=cmb, f1=f1: e.tensor_tensor(out=f1, in0=c_lo, in1=cmb, op=ALU.mult),
                    [r_cosT], [r_tq[0]])
                dve(lambda e, s_lo=s_lo, smb=smb, f2=f2: e.tensor_tensor(out=f2, in0=s_lo, in1=smb, op=ALU.mult),
                    [r_sinT], [r_tq[1]])
                dve(lambda e, s_lo=s_lo, cmb=cmb, f3=f3: e.tensor_tensor(out=f3, in0=s_lo, in1=cmb, op=ALU.mult),
                    [r_sinT, r_cosT], [r_tq[2]])
                dve(lambda e, c_lo=c_lo, smb=smb, f4=f4: e.tensor_tensor(out=f4, in0=c_lo, in1=smb, op=ALU.mult),
                    [r_cosT, r_sinT], [r_tq[3]])
                dve(lambda e, s0=s0, s1=s1, f1=f1, f2=f2, m=m: e.tensor_tensor(out=cosT[:, s0:s1, m:2 * m], in0=f1,
                                                                              in1=f2, op=ALU.subtract),
                    [r_tq[0], r_tq[1]], [r_cosT])
                dve(lambda e, s0=s0, s1=s1, f3=f3, f4=f4, m=m: e.tensor_tensor(out=sinT[:, s0:s1, m:2 * m], in0=f3,
                                                                              in1=f4, op=ALU.add),
                    [r_tq[2], r_tq[3]], [r_sinT])
            m *= 2
        kb.alias([rt_], [r_cosT, r_sinT])
        smflat = s5sm[:, 0:17, :].rearrange("p a c -> p (a c)")
        if cached:
            kb.defer = None
            kb.dma("sp", smflat, scr_sm[l], r_scrL, [r_scr[l]], [r_s5sm])
            kb.dma("sp", BtE[:, :, :, :].rearrange("p a r c -> p (a r c)"), scr_bt[l, 0], r_scrL, [r_scr[l]],
                   [r_BtL[k_] for k_ in r_BtL if k_[2] == 0])
            kb.dma("sp", BtO[:, :, :, :].rearrange("p a r c -> p (a r c)"), scr_bt[l, 1], r_scrL, [r_scr[l]],
                   [r_BtL[k_] for k_ in r_BtL if k_[2] == 1])
            kb.dma("sp", arena2[:, 0:4096], scr_cs[l], r_scrL, [r_scr[l]], [rt_])
            kb.dma("sp", arena1[:, 4 * TG1:4 * TG1 + 4096], scr_ct[l], r_scrL, [r_scr[l]], r_Ct_all)
        if gi == 0:
            for ri, srcst in ((0, st_re), (1, st_im)):
                kb.dma("sp", s5stg[:, :], srcst[l], r_stg[0], [], r_stg)

                def ft(e):
                    ins = None
                    for st in range(16):
                        ins = e.transpose(PB[5][:, st * 16:(st + 1) * 16], s5stg[:, st * 128:(st + 1) * 128],
                                          identf[0:16, 0:16])
                    return ins
                pe(ft, r_stg + [r_identf], [PR[5]])
                dve(lambda e, ri=ri: e.tensor_copy(out=X0[:, ri, :, :].rearrange("p a b -> p (a b)"), in_=PB[5][:, 0:256]),
                    [PR[5]], [r_X0])
                dve(lambda e, ri=ri: e.tensor_tensor(out=inj[:, ri], in0=X0[:, ri], in1=bc(MAG), op=ALU.mult),
                    [r_X0, r], [r_X0])
            kb.alias([r_coef0, r_yfp], [r_Braw, r_Craw] + r_BBL + r_srcC_all)
            dve(lambda e: e.tensor_copy(out=coef0[:, :, :], in_=MAG.unsqueeze(2).to_broadcast([128, 16, 80])),
                [r], [r_coef0])
            dve(lambda e: e.memset(coef0[:, :, 0:64].rearrange("p a (s j) -> p a s j", j=4)[:, :, :, 0:1], 0.0),
                [r_coef0], [r_coef0])
            dve(lambda e: e.memset(coef0[:, :, 64:65], 0.0), [r_coef0], [r_coef0])

        if gi == 1:
            kb.alias([r_coef0, r_yfp], [r_Braw, r_Craw] + r_BBL + r_srcC_all)
        else:
            kb.dma("sp", scr_sm[l], smflat, r_scrS, [r_s5sm], [r_scr[l]])
            kb.dma("sp", scr_bt[l, 0], BtE[:, :, :, :].rearrange("p a r c -> p (a r c)"), r_scrS, r_Bt_all, [r_scr[l]])
            kb.dma("sp", scr_bt[l, 1], BtO[:, :, :, :].rearrange("p a r c -> p (a r c)"), r_scrS, r_Bt_all, [r_scr[l]])
            kb.dma("sp", scr_cs[l], arena2[:, 0:4096], r_scrS, [rt_], [r_scr[l]])
            kb.dma("sp", scr_ct[l], arena1[:, 4 * TG1:4 * TG1 + 4096], r_scrS, r_Ct_all, [r_scr[l]])
        kb.alias(r_rin + r_rr + [x for a_ in r_rrs[0] for x in a_], r_srcB_all)
        if gi == 0:
            sblocks = [(0, 80)] + [(80 + 256 * i, 336 + 256 * i) for i in range(4)]
        else:
            sblocks = [(256 * i, 256 * i + 256) for i in range(4)]
        GLs = WS.take("A")
        wgl = ringA[:, GLs, 0:2048].rearrange("p (k c) -> p k c", c=512)
        fcount = [0]
        pend_post = []
        for sbi, (c0, c1) in enumerate(sblocks):
            ncol = c1 - c0
            ybuf = yfp2[sbi % 2]
            r_ybuf = r_yfp2[sbi % 2]
            ti = [i for i, t in enumerate(tiles) if t[0] <= c0 and c1 <= t[1]][0]
            block0 = (gi == 0 and c0 == 0)
            frames = [(0, 80)] if block0 else [(f, f + 128) for f in range(c0, c1, 128)]
            nfr = len(frames)

            def emit_B(qd, fi):
                f0, f1 = frames[fi]
                L = f1 - f0
                fb_ = (fbase + qd * nfr + fi) % 2
                psF = PBall[:, fb_ * 1024:(fb_ + 1) * 1024].rearrange("p (s r c) -> p s r c", s=4, r=2)

                def fn(e):
                    ins = None
                    for s4 in range(4):
                        pb_ = 64 * (s4 // 2)
                        Bt = BtE if s4 % 2 == 0 else BtO
                        for ri in range(2):
                            ins = e.matmul(psF[:, s4, ri, 0:L], lhsT=Bt[pb_:pb_ + 64, qd, ri, :],
                                           rhs=uT[pb_:pb_ + 64, qd, f0:f1], start=True, stop=True)
                    return ins
                pe(fn, r_Bt_all + [r_u[qd]], [PR[2 * fb_], PR[2 * fb_ + 1]])
            pendC = []
            pend_carry = []

            def emit_C(qd):
                xq = xTb2[qd % 2]
                r_xq = r_xTb2[qd % 2]
                pby = 4
                mm_group(PB[pby][:, 0:ncol], [(Ct[:, 4 * qd + s4, ri, :], xq[:, s4, ri, 0:ncol])
                                              for s4 in range(4) for ri in range(2)], r_Ct_all + [r_xq], [PR[pby]])
                dcol = R_S5D + 4 * l + qd
                dve(lambda e, qd=qd, dcol=dcol: e.scalar_tensor_tensor(out=ybuf[:, qd, 0:ncol], in0=uT[:, qd, c0:c1],
                                                                       scalar=PV[:, dcol:dcol + 1], op0=ALU.mult,
                                                                       in1=PB[pby][:, 0:ncol], op1=ALU.add),
                    [r_u[qd], r_PV, PR[pby]], [r_ybuf])
            fbase = fcount[0]
            for fi in range(nfr):
                emit_B(0, fi)
            for qd in range(4):
                xq = xTb2[qd % 2]
                r_xq = r_xTb2[qd % 2]
                for fi, (f0, f1) in enumerate(frames):
                    L = f1 - f0
                    o0 = f0 - c0
                    stq = slice(4 * qd, 4 * qd + 4)
                    fb_ = fcount[0] % 2
                    fcount[0] += 1
                    psF = PBall[:, fb_ * 1024:(fb_ + 1) * 1024].rearrange("p (s r c) -> p s r c", s=4, r=2)
                    PRF = [PR[2 * fb_], PR[2 * fb_ + 1]]
                    rrb = rr2[fb_]
                    r_rrb = [r_rrs[fb_][0], r_rrs[fb_][1]]
                    if block0:
                        segs = [(0, 64, 0), (64, 80, 0)]
                    else:
                        segs = [(0, L, 0)]
                    for (a0, a1, _) in segs:
                        if block0 and a0 == 0:
                            tcv = cosT[:, stq, 0:4].unsqueeze(2).to_broadcast([128, 4, 16, 4])
                            tsv = sinT[:, stq, 0:4].unsqueeze(2).to_broadcast([128, 4, 16, 4])
                            shp = lambda v: v.rearrange("p a (s j) -> p a s j", j=4)
                        else:
                            tcv = cosT[:, stq, 0:a1 - a0]
                            tsv = sinT[:, stq, 0:a1 - a0]
                            shp = lambda v: v
                        bur = shp(psF[:, :, 0, a0:a1])
                        bui = shp(psF[:, :, 1, a0:a1])
                        t1, t2, t3, t4 = (shp(tq[i][:, :, a0:a1]) for i in range(4))
                        dve(lambda e, t1=t1, bur=bur, tcv=tcv: e.tensor_tensor(out=t1, in0=bur, in1=tcv, op=ALU.mult),
                            PRF + [rt_], [r_tq[0]])
                        dve(lambda e, t2=t2, bui=bui, tsv=tsv: e.tensor_tensor(out=t2, in0=bui, in1=tsv, op=ALU.mult),
                            PRF + [rt_], [r_tq[1]])
                        dve(lambda e, t3=t3, bui=bui, tcv=tcv: e.tensor_tensor(out=t3, in0=bui, in1=tcv, op=ALU.mult),
                            PRF + [rt_], [r_tq[2]])
                        dve(lambda e, t4=t4, bur=bur, tsv=tsv: e.tensor_tensor(out=t4, in0=bur, in1=tsv, op=ALU.mult),
                            PRF + [rt_], [r_tq[3]])
                    for cf in pend_carry:
                        cf(0)
                    if qd < 3:
                        emit_B(qd + 1, fi)
                    if fi == nfr - 1 and pendC:
                        emit_C(pendC.pop(0))
                    dve(lambda e: e.tensor_tensor(out=rin[0][:, :, 0:L], in0=tq[0][:, :, 0:L], in1=tq[1][:, :, 0:L],
                                                  op=ALU.add), [r_tq[0], r_tq[1]], [r_rin[0]])
                    dve(lambda e: e.tensor_tensor(out=rin[1][:, :, 0:L], in0=tq[2][:, :, 0:L], in1=tq[3][:, :, 0:L],
                                                  op=ALU.subtract), [r_tq[2], r_tq[3]], [r_rin[1]])
                    if block0:
                        for ri in range(2):
                            v = rin[ri][:, :, 0:64].rearrange("p a (s j) -> p a s j", j=4)[:, :, :, 0:1]
                            dve(lambda e, v=v, ri=ri: e.tensor_tensor(out=v, in0=v, in1=inj[:, ri, stq, :].unsqueeze(3),
                                                                      op=ALU.add), [r_rin[ri], r_X0], [r_rin[ri]])
                    while pend_carry:
                        pend_carry.pop(0)(1)
                    for s4 in range(4):
                        st = 4 * qd + s4
                        for ri in range(2):
                            if block0:
                                d0 = coef0[:, st, 0:L]
                                init = 0.0
                                rds = [r_rin[ri], r_coef0]
                            else:
                                d0 = MAG[:, st:st + 1].to_broadcast([128, L])
                                init = car[:, l, ri, st:st + 1]
                                rds = [r_rin[ri], r, r_carq2[qd][ri]]
                            dve(lambda e, s4=s4, ri=ri, d0=d0, init=init, rrb=rrb: e.tensor_tensor_scan(
                                out=rrb[ri][:, s4, 0:L], data0=d0, data1=rin[ri][:, s4, 0:L], initial=init,
                                op0=ALU.mult, op1=ALU.add), rds, [r_rrb[ri][s4]])
                    for _ in range(min(POST_DRAIN, len(pend_post))):
                        pend_post.pop(0)()
                    def carry_fn(phase, l=l, stq=stq, L=L, rrb=rrb, r_rrb=r_rrb, block0=block0, qd=qd,
                                 csb=(fcount[0] % 2) * 4):
                        cs = [s5sm[:, 17 + (csb + i) % 7, 4 * ((csb + i) // 7):4 * ((csb + i) // 7) + 4].unsqueeze(2)
                              for i in range(4)]
                        jl = (15 if block0 else L - 1)
                        tcl, tsl = cosT[:, stq, jl:jl + 1], sinT[:, stq, jl:jl + 1]
                        rrl, ril = rrb[0][:, :, L - 1:L], rrb[1][:, :, L - 1:L]
                        r_c4 = [res("s5cs%d" % (csb + i)) for i in range(4)]
                        if phase == 1:
                            dve(lambda e: e.tensor_tensor(out=car[:, l, 0, stq].unsqueeze(2), in0=cs[0], in1=cs[1],
                                                          op=ALU.subtract), [r_c4[0], r_c4[1]], [r_carq2[qd][0]])
                            dve(lambda e: e.tensor_tensor(out=car[:, l, 1, stq].unsqueeze(2), in0=cs[2], in1=cs[3],
                                                          op=ALU.add), [r_c4[2], r_c4[3]], [r_carq2[qd][1]])
                            return
                        dve(lambda e, rrl=rrl, tcl=tcl: e.tensor_tensor(out=cs[0], in0=rrl, in1=tcl, op=ALU.mult),
                            r_rrb[0] + [rt_], [r_c4[0]])
                        dve(lambda e, ril=ril, tsl=tsl: e.tensor_tensor(out=cs[1], in0=ril, in1=tsl, op=ALU.mult),
                            r_rrb[1] + [rt_], [r_c4[1]])
                        dve(lambda e, ril=ril, tcl=tcl: e.tensor_tensor(out=cs[2], in0=ril, in1=tcl, op=ALU.mult),
                            r_rrb[1] + [rt_], [r_c4[2]])
                        dve(lambda e, rrl=rrl, tsl=tsl: e.tensor_tensor(out=cs[3], in0=rrl, in1=tsl, op=ALU.mult),
                            r_rrb[0] + [rt_], [r_c4[3]])

                    pend_carry.append(carry_fn)
                    for (a0, a1, _) in segs:
                        if block0 and a0 == 0:
                            tcv = cosT[:, stq, 0:4].unsqueeze(2).to_broadcast([128, 4, 16, 4])
                            tsv = sinT[:, stq, 0:4].unsqueeze(2).to_broadcast([128, 4, 16, 4])
                            shp = lambda v: v.rearrange("p a (s j) -> p a s j", j=4)
                        else:
                            tcv = cosT[:, stq, 0:a1 - a0]
                            tsv = sinT[:, stq, 0:a1 - a0]
                            shp = lambda v: v
                        rrv, riv = shp(rrb[0][:, :, a0:a1]), shp(rrb[1][:, :, a0:a1])
                        p0, p1 = shp(pq[0][:, :, a0:a1]), shp(pq[1][:, :, a0:a1])
                        xr = shp(xq[:, :, 0, o0 + a0:o0 + a1])
                        xi = shp(xq[:, :, 1, o0 + a0:o0 + a1])
                        pool(lambda e, p0=p0, rrv=rrv, tcv=tcv: e.tensor_tensor(out=p0, in0=rrv, in1=tcv, op=ALU.mult),
                             r_rrb[0] + [rt_], [r_pq[0]])
                        pool(lambda e, p1=p1, riv=riv, tsv=tsv: e.tensor_tensor(out=p1, in0=riv, in1=tsv, op=ALU.mult),
                             r_rrb[1] + [rt_], [r_pq[1]])
                        pool(lambda e, p0=p0, p1=p1, xr=xr: e.tensor_tensor(out=xr, in0=p0, in1=p1, op=ALU.subtract),
                             [r_pq[0], r_pq[1]], [r_xq])
                        if block0 and a0 == 0:
                            pool(lambda e, p0=p0, p1=p1: e.tensor_tensor(out=X1[:, 0, stq, :].unsqueeze(3),
                                                                         in0=p0[:, :, :, 3:4], in1=p1[:, :, :, 3:4],
                                                                         op=ALU.subtract), [r_pq[0], r_pq[1]], [r_X0])
                        pool(lambda e, p0=p0, riv=riv, tcv=tcv: e.tensor_tensor(out=p0, in0=riv, in1=tcv, op=ALU.mult),
                             r_rrb[1] + [rt_], [r_pq[0]])
                        pool(lambda e, p1=p1, rrv=rrv, tsv=tsv: e.tensor_tensor(out=p1, in0=rrv, in1=tsv, op=ALU.mult),
                             r_rrb[0] + [rt_], [r_pq[1]])
                        pool(lambda e, p0=p0, p1=p1, xi=xi: e.tensor_tensor(out=xi, in0=p0, in1=p1, op=ALU.add),
                             [r_pq[0], r_pq[1]], [r_xq])
                        if block0 and a0 == 0:
                            pool(lambda e, p0=p0, p1=p1: e.tensor_tensor(out=X1[:, 1, stq, :].unsqueeze(3),
                                                                         in0=p0[:, :, :, 3:4], in1=p1[:, :, :, 3:4],
                                                                         op=ALU.add), [r_pq[0], r_pq[1]], [r_X0])
                pendC.append(qd)
            while pend_carry:
                cf = pend_carry.pop(0)
                cf(0)
                cf(1)
            while pendC:
                emit_C(pendC.pop(0))
            def emit_post(c0=c0, c1=c1, ncol=ncol, ti=ti, ybuf=ybuf, r_ybuf=r_ybuf):
                w1 = w1f[:, 0:4 * ncol].rearrange("p (a t) -> p a t", t=ncol)
                w2 = w2f[:, 0:4 * ncol].rearrange("p (a t) -> p a t", t=ncol)
                yv = ybuf[:, :, 0:ncol]
                pool(lambda e: e.tensor_tensor(out=w1, in0=yv, in1=yv, op=ALU.mult), [r_ybuf], [r_w1])
                pool(lambda e: e.tensor_scalar(out=w1, in0=w1, scalar1=0.044715, scalar2=1.0, op0=ALU.mult, op1=ALU.add),
                     [r_w1], [r_w1])
                pool(lambda e: e.tensor_tensor(out=w1, in0=w1, in1=yv, op=ALU.mult), [r_w1, r_ybuf],
                     [r_w1])
                act(lambda e: e.activation(out=w2, in_=w1, func=AF.Sigmoid, scale=2.0 * math.sqrt(2.0 / math.pi)),
                    [r_w1], [r_w2])
                dve(lambda e: e.tensor_tensor(out=yv, in0=yv, in1=w2, op=ALU.mult), [r_ybuf, r_w2], [r_ybuf])
                act(lambda e: e.activation(out=ygb[:, :, 0:ncol], in_=yv, func=AF.Copy), [r_ybuf], [r_ygb])
                for ot in range(4):
                    pbg = 5
                    mm_group(PB[pbg][:, 0:ncol], [(wgl[:, k, ot * 128:(ot + 1) * 128], ygb[:, k, 0:ncol]) for k in range(4)],
                             [r_ringA[GLs], r_ygb], [PR[pbg]])
                    bcol = R_BGLU + 4 * l + ot
                    act(lambda e, ot=ot, bcol=bcol: e.activation(out=w2[:, ot, :], in_=PB[pbg][:, 0:ncol], func=AF.Sigmoid,
                                                                 bias=PV[:, bcol:bcol + 1]), [PR[pbg], r_PV],
                        [r_w2])
                dve(lambda e: e.tensor_tensor(out=yv, in0=yv, in1=w2, op=ALU.mult), [r_ybuf, r_w2], [r_ybuf])
                for ot in range(4):
                    s = sqi[0] % 3
                    sqi[0] += 1
                    act(lambda e, ot=ot, s=s: e.activation(out=sq[:, s, 0:ncol], in_=ybuf[:, ot, 0:ncol], func=AF.Square),
                        [r_ybuf], [r_sq[s]])
                    pe(lambda e, ot=ot, s=s: e.matmul(PB[6][:, 0:ncol], lhsT=onesb[:], rhs=sq[:, s, 0:ncol],
                                                      start=(ot == 0), stop=(ot == 3)), [r_sq[s], r_ones], [PR[6]])
                j = sqi[0] % 2
                act(lambda e: e.activation(out=rt[:, j, 0:ncol], in_=PB[6][:, 0:ncol], func=AF.Ln, bias=epsT[:, 0:1],
                                           scale=1.0 / 512), [PR[6], r_misc], [r_rt[j]])
                act(lambda e: e.activation(out=rt[:, j, 0:ncol], in_=rt[:, j, 0:ncol], func=AF.Exp, scale=-0.5),
                    [r_rt[j]], [r_rt[j]])
                for ot in range(4):
                    gcol = R_S5N + 4 * l + ot
                    dve(lambda e, ot=ot, gcol=gcol: e.scalar_tensor_tensor(out=xn[:, 4 + ot, c0:c1], in0=ybuf[:, ot, 0:ncol],
                                                                           scalar=PV[:, gcol:gcol + 1], op0=ALU.mult,
                                                                           in1=rt[:, j, 0:ncol], op1=ALU.mult),
                        [r_ybuf, r_rt[j], r_PV], [r_xn[4 + ot][ti]])

            while pend_post:
                pend_post.pop(0)()
            kb.defer = []
            emit_post()
            pend_post.extend(kb.defer)
            kb.defer = None
        while pend_post:
            pend_post.pop(0)()
        kb.alias(r_sg, [r_yfp2[1]])
        WS.done("A", 1)
        if gi == 0:
            for ri, dst in ((0, o_s5s_re), (1, o_s5s_im)):
                for q4 in range(4):
                    def ft(e, ri=ri, q4=q4):
                        ins = None
                        for s4 in range(4):
                            st = 4 * q4 + s4
                            ins = e.transpose(PB[5][0:16, s4 * 128:(s4 + 1) * 128], X1[:, ri, st, :], identf[:, :])
                        return ins
                    pe(ft, [r_X0, r_identf], [PR[5]])
                    dve(lambda e, q4=q4: e.tensor_copy(out=s5stg[:, q4 * 512:(q4 + 1) * 512],
                                                       in_=PB[5][0:16, 0:512]), [PR[5]], r_stg)
                kb.dma("sp", dst[l], s5stg[:, :], r_stg[0], r_stg, [])
        else:
            for ri, dst in ((0, o_s5p_re), (1, o_s5p_im)):
                pe(lambda e, ri=ri: e.transpose(PB[5][0:16, 0:128], car[:, l, ri, :], identf[:, :]),
                   [x for a_ in r_carq2 for x in a_] + [r_identf], [PR[5]])
                dve(lambda e: e.tensor_copy(out=s5stg[:, 0:128], in_=PB[5][0:16, 0:128]), [PR[5]], r_stg)
                kb.dma("sp", dst[l], s5stg[:, 0:128], r_stg[0], r_stg, [])

    for gi, G in enumerate(groups):
        load_x(G)
        for l in range(depth):
            rmsnorm_to_xn(G, R_NF1 + 8 * l)
            ffn(G, l)
            if mixer:
                mixer_fn(G, l, gi)
            rmsnorm_to_xn(G, R_NF2 + 8 * l)
            ffn(G, l)
        store_y(G)

    for name, E in kb.eng.items():
        if E["obj"] is None and E["count"] > 0:
            nc.sync.wait_ge(E["sem"], E["count"])
    for name in ("pe", "act", "dve", "pool"):
        E = kb.eng[name]
        if E["count"] > 0:
            nc.sync.wait_ge(E["sem"], E["count"])
    es.close()
    return nc


def _pack_params(inp):
    rows = []
    rows.append(inp["norm_ffn1"].reshape(32, 128))
    rows.append(inp["norm_mix"].reshape(32, 128))
    rows.append(inp["norm_ffn2"].reshape(32, 128))
    rows.append(inp["norm_final"].reshape(8, 128))
    rows.append(inp["hgrn_norm"].reshape(16, 128))
    rows.append(inp["s5_norm"].reshape(16, 128))
    rows.append(inp["s5_b_glu"].reshape(16, 128))
    rows.append(inp["lb_param"].reshape(16, 128))
    rows.append(inp["s5_d"].reshape(16, 128))
    pv = np.ascontiguousarray(np.concatenate(rows, axis=0).astype(np.float32))
    assert pv.shape == (NPROW, 128)
    s5 = []
    for l in range(DEPTH):
        s5.append(inp["s5_lambda_re"][l].reshape(16, 128))
        s5.append(inp["s5_lambda_im"][l].reshape(16, 128))
        s5.append(np.repeat(inp["s5_log_dt"][l], 64).reshape(16, 128))
    s5 = np.ascontiguousarray(np.concatenate(s5, axis=0).astype(np.float32))
    return pv, s5


_NC_CACHE = {}


def make_in_maps(inp, cores):
    inp = {k: np.asarray(v) for k, v in inp.items()}
    pv, s5 = _pack_params(inp)
    shared = dict(
        pvec=pv, s5pv=s5,
        ffn1_w_gate=inp["ffn1_w_gate"], ffn1_w_up=inp["ffn1_w_up"], ffn1_w_down=inp["ffn1_w_down"],
        ffn2_w_gate=inp["ffn2_w_gate"], ffn2_w_up=inp["ffn2_w_up"], ffn2_w_down=inp["ffn2_w_down"],
        w_in=inp["w_in"], w_out=inp["w_out"], s5_w_glu=inp["s5_w_glu"],
        s5_b_re=inp["s5_b_re"].reshape(DEPTH, 2048, 16), s5_b_im=inp["s5_b_im"].reshape(DEPTH, 2048, 16),
        s5_c_re=inp["s5_c_re"].reshape(DEPTH, 512, 64), s5_c_im=inp["s5_c_im"].reshape(DEPTH, 512, 64),
    )
    maps = []
    for c in cores:
        xs = inp["x_sample"][NSEQ * c:NSEQ * (c + 1)].reshape(64, D)
        xp = inp["x_prompt"][c]
        x = np.ascontiguousarray(np.concatenate([xs, inp["meta_tokens"], xp], axis=0).astype(np.float32))
        m = dict(shared)
        m["xin"] = x
        m["st_hg"] = np.ascontiguousarray(inp["state_hgrn"][:, NSEQ * c:NSEQ * (c + 1)])
        m["st_re"] = np.ascontiguousarray(inp["state_s5_re"][:, NSEQ * c:NSEQ * (c + 1)].reshape(DEPTH, NSEQ, 2048))
        m["st_im"] = np.ascontiguousarray(inp["state_s5_im"][:, NSEQ * c:NSEQ * (c + 1)].reshape(DEPTH, NSEQ, 2048))
        maps.append(m)
    return maps


def gather(results, ncores):
    B = ncores
    y_prompt = np.zeros((B, 2048, D), np.float32)
    y_sample = np.zeros((NSEQ * B, 4, D), np.float32)
    hgp = np.zeros((DEPTH, B, 4, 128, 128), np.float32)
    s5pr = np.zeros((DEPTH, B, 32, 64), np.float32)
    s5pi = np.zeros((DEPTH, B, 32, 64), np.float32)
    hgs = np.zeros((DEPTH, NSEQ * B, 4, 128, 128), np.float32)
    s5sr = np.zeros((DEPTH, NSEQ * B, 32, 64), np.float32)
    s5si = np.zeros((DEPTH, NSEQ * B, 32, 64), np.float32)
    for c, r in enumerate(results):
        y = np.asarray(r["yout"])
        y_sample[NSEQ * c:NSEQ * (c + 1)] = y[0:64].reshape(NSEQ, 4, D)
        y_prompt[c] = y[80:]
        hgp[:, c] = np.asarray(r["o_hgp"])
        s5pr[:, c] = np.asarray(r["o_s5p_re"]).reshape(DEPTH, 32, 64)
        s5pi[:, c] = np.asarray(r["o_s5p_im"]).reshape(DEPTH, 32, 64)
        hgs[:, NSEQ * c:NSEQ * (c + 1)] = np.asarray(r["o_hgs"])
        s5sr[:, NSEQ * c:NSEQ * (c + 1)] = np.asarray(r["o_s5s_re"]).reshape(DEPTH, NSEQ, 32, 64)
        s5si[:, NSEQ * c:NSEQ * (c + 1)] = np.asarray(r["o_s5s_im"]).reshape(DEPTH, NSEQ, 32, 64)
    return (y_prompt, y_sample, hgp, s5pr, s5pi, hgs, s5sr, s5si)


def kernel(**inputs):
    if "nc" not in _NC_CACHE:
        _NC_CACHE["nc"] = build_nc()
    nc = _NC_CACHE["nc"]
    maps = make_in_maps(inputs, list(range(NCORE)))
    res = run_bass_kernel_spmd(nc, maps, core_ids=list(range(NCORE)))
    return gather(res.results, NCORE)
```

```python
import math
import numpy as np
import concourse.bass as bass
import concourse.mybir as mybir
from concourse.bass_utils import run_bass_kernel_spmd

F32 = mybir.dt.float32
BF16 = mybir.dt.bfloat16
I32 = mybir.dt.int32
AF = mybir.ActivationFunctionType
ALU = mybir.AluOpType

D = 1024
FF = 2816
NHT = 22
DEPTH = 4
NCORE = 8
TG1 = 1104
TG2 = 1024
TALL = TG1 + TG2
EPS = 1e-6
NSEQ = 16
POST_DRAIN = 5
TWO_PI = 2.0 * math.pi

R_NF1, R_NMX, R_NF2, R_NFIN, R_HGN, R_S5N, R_BGLU, R_LBP, R_S5D = 0, 32, 64, 96, 104, 120, 136, 152, 168
NPROW = 184


class Res:
    __slots__ = ("name", "w", "r", "a", "dsem")

    def __init__(self, name, dsem=None):
        self.name = name
        self.w = None
        self.r = {}
        self.a = {}
        self.dsem = dsem


class KB:
    def __init__(self, nc):
        self.nc = nc
        self.eng = {}
        self.nosame = set()
        self.defer = None

    def add_eng(self, name, obj, sem):
        self.eng[name] = dict(obj=obj, sem=sem, count=0, seen={})

    def _deps(self, reads, writes):
        need = {}
        for r in reads:
            if r.w is not None:
                n, c = r.w
                if need.get(n, 0) < c:
                    need[n] = c
            for n, c in r.a.items():
                if need.get(n, 0) < c:
                    need[n] = c
        for w in writes:
            if w.w is not None:
                n, c = w.w
                if need.get(n, 0) < c:
                    need[n] = c
            for n, c in w.r.items():
                if need.get(n, 0) < c:
                    need[n] = c
            for n, c in w.a.items():
                if need.get(n, 0) < c:
                    need[n] = c
        return need

    def _wait(self, ename, need):
        E = self.eng[ename]
        for n, c in need.items():
            if n == ename and ename in self.nosame:
                continue
            if E["seen"].get(n, 0) >= c:
                continue
            E["obj"].wait_ge(self.eng[n]["sem"], c)
            E["seen"][n] = c

    def op(self, ename, fn, reads=(), writes=()):
        if self.defer is not None:
            self.defer.append(lambda: self.op(ename, fn, reads, writes))
            return
        self._wait(ename, self._deps(reads, writes))
        E = self.eng[ename]
        ins = fn(E["obj"])
        E["count"] += 1
        ins.then_inc(E["sem"], 1)
        c = E["count"]
        for r in reads:
            r.r[ename] = c
        for w in writes:
            w.w = (ename, c)
            w.r = {}
            w.a = {}

    def dma(self, qname, out, in_, sres, reads=(), writes=(), **kw):
        if self.defer is not None:
            self.defer.append(lambda: self.dma(qname, out, in_, sres, reads, writes, **kw))
            return
        self._wait(qname, self._deps(reads, writes))
        d = self.eng[sres.dsem]
        ins = self.eng[qname]["obj"].dma_start(out=out, in_=in_, **kw)
        d["count"] += 16
        ins.then_inc(d["sem"], 16)
        for r in reads:
            r.r[sres.dsem] = d["count"]
        for w in writes:
            w.w = (sres.dsem, d["count"])
            w.r = {}
            w.a = {}

    def alias(self, new, old):
        m = {}
        for o in old:
            if o.w is not None:
                n, c = o.w
                m[n] = max(m.get(n, 0), c)
            for n, c in o.r.items():
                m[n] = max(m.get(n, 0), c)
            for n, c in o.a.items():
                m[n] = max(m.get(n, 0), c)
        for r in new:
            r.w = None
            r.r = {}
            r.a = dict(m)

    def wait_all(self, ename, ress):
        need = {}
        for r in ress:
            if r.w is not None:
                n, c = r.w
                need[n] = max(need.get(n, 0), c)
            for n, c in r.r.items():
                need[n] = max(need.get(n, 0), c)
        self._wait(ename, need)


def build_nc(depth=DEPTH, mixer=True, hgrn=True, s5=True):
    nc = bass.Bass("TRN2", target_bir_lowering=False)
    import contextlib
    es = contextlib.ExitStack()

    def dram(name, shape, kind, dt=F32):
        return nc.dram_tensor(name, list(shape), dt, kind=kind).ap()

    IN, OUT = "ExternalInput", "ExternalOutput"
    xin = dram("xin", [TALL, D], IN)
    st_hg = dram("st_hg", [DEPTH, NSEQ, 4, 128, 128], IN)
    st_re = dram("st_re", [DEPTH, NSEQ, 2048], IN)
    st_im = dram("st_im", [DEPTH, NSEQ, 2048], IN)
    pvec = dram("pvec", [NPROW, 128], IN)
    s5pv = dram("s5pv", [DEPTH * 48, 128], IN)
    w_f1g = dram("ffn1_w_gate", [DEPTH, D, FF], IN)
    w_f1u = dram("ffn1_w_up", [DEPTH, D, FF], IN)
    w_f1d = dram("ffn1_w_down", [DEPTH, FF, D], IN)
    w_f2g = dram("ffn2_w_gate", [DEPTH, D, FF], IN)
    w_f2u = dram("ffn2_w_up", [DEPTH, D, FF], IN)
    w_f2d = dram("ffn2_w_down", [DEPTH, FF, D], IN)
    w_in = dram("w_in", [DEPTH, D, 2560], IN)
    w_out = dram("w_out", [DEPTH, D, D], IN)
    w_glu = dram("s5_w_glu", [DEPTH, 512, 512], IN)
    b_re = dram("s5_b_re", [DEPTH, 2048, 16], IN)
    b_im = dram("s5_b_im", [DEPTH, 2048, 16], IN)
    c_re = dram("s5_c_re", [DEPTH, 512, 64], IN)
    c_im = dram("s5_c_im", [DEPTH, 512, 64], IN)
    INT = "Internal"
    scr_cs = dram("scr_cs", [DEPTH, 128, 4096], INT)
    scr_bt = dram("scr_bt", [DEPTH, 2, 128, 1024], INT, BF16)
    scr_ct = dram("scr_ct", [DEPTH, 128, 4096], INT, BF16)
    scr_sm = dram("scr_sm", [DEPTH, 128, 272], INT)
    yout = dram("yout", [TALL, D], OUT)
    o_hgp = dram("o_hgp", [DEPTH, 4, 128, 128], OUT)
    o_s5p_re = dram("o_s5p_re", [DEPTH, 16, 128], OUT)
    o_s5p_im = dram("o_s5p_im", [DEPTH, 16, 128], OUT)
    o_hgs = dram("o_hgs", [DEPTH, NSEQ, 4, 128, 128], OUT)
    o_s5s_re = dram("o_s5s_re", [DEPTH, NSEQ, 2048], OUT)
    o_s5s_im = dram("o_s5s_im", [DEPTH, NSEQ, 2048], OUT)

    def sb(name, shape, dt=F32):
        return es.enter_context(nc.sbuf_tensor(name, list(shape), dt))

    def ps(name, shape, dt=F32):
        return es.enter_context(nc.psum_tensor(name, list(shape), dt))

    kb = KB(nc)
    nsem = [0]

    def newsem(name):
        nsem[0] += 1
        return es.enter_context(nc.semaphore(name))

    kb.add_eng("pe", nc.tensor, newsem("s_pe"))
    kb.add_eng("act", nc.scalar, newsem("s_act"))
    kb.add_eng("dve", nc.vector, newsem("s_dve"))
    kb.add_eng("pool", nc.gpsimd, newsem("s_pool"))
    kb.add_eng("sp", nc.sync, newsem("s_sp"))
    kb.nosame.add("pe")
    kb.nosame.add("sp")

    def dres(name):
        sname = "d_" + name
        kb.add_eng(sname, None, newsem(sname))
        return Res(name, dsem=sname)

    hT = sb("hT", [128, 8, TG1])
    xn = sb("xn", [128, 8, TG1], BF16)
    Sst = sb("Sst", [128, DEPTH, 4, 128])
    car = sb("car", [128, DEPTH, 2, 16])
    NA1 = 11 * TG1 + 11 * 1024
    arena1 = sb("arena1", [128, NA1], BF16)
    NA2 = 10496 + 1024
    arena2 = sb("arena2", [128, NA2])
    NRA = 4
    ringA = sb("ringA", [128, NRA, 2048], BF16)
    stg = sb("stg", [128, 2, 1024])
    PV = sb("PV", [128, NPROW])
    S5P = sb("S5P", [128, DEPTH * 48])
    identf = sb("identf", [128, 128])
    identb = sb("identb", [128, 128], BF16)
    onesb = sb("onesb", [128, 128], BF16)
    maskP = sb("maskP", [128, 128])
    mask0 = sb("mask0", [128, 128])
    Eseq = sb("Eseq", [16, 64], BF16)
    rowm = sb("rowm", [64, 16])
    epsT = sb("epsT", [128, 1])
    lbT = sb("lbT", [128, DEPTH, 4, 2])
    sq = sb("sq", [128, 3, 512], BF16)
    rt = sb("rt", [128, 2, 512])
    sgt = sb("sgt", [128, 2, 512])
    BtE = sb("BtE", [128, 4, 2, 128], BF16)
    BtO = sb("BtO", [128, 4, 2, 128], BF16)
    s5sm = sb("s5sm", [128, 27, 16])
    s5i = sb("s5i", [128, 2, 16], I32)
    evod = sb("evod", [128, 4])
    X0 = sb("X0", [128, 2, 16, NSEQ])
    inj = X0
    X1 = X0
    hgst = sb("hgst", [128, 4, 128])
    hgout = sb("hgout", [128, 3, 128])
    Sd = sb("Sd", [128, 4, 128])
    Sdb = sb("Sdb", [128, 4, 128], BF16)
    ktok = sb("ktok", [128, 4, 128], BF16)
    kms = sb("kms", [64, 4, 128], BF16)
    Am = sb("Am", [128, 4, 128], BF16)
    eb = sb("eb", [128, 4, 34])

    PBall = ps("pball", [128, 7 * 512])
    PB = [PBall[:, 512 * i:512 * (i + 1)] for i in range(7)]
    psB4 = PBall[:, 0:2048].rearrange("p (s r c) -> p s r c", s=4, r=2)
    PBb = ps("pbb", [128, 1024], BF16)
    PR = [Res("pb%d" % i) for i in range(7)]
    PRb = Res("pbb")

    hid = arena1[:, 0:11 * TG1].rearrange("p (a t) -> p a t", t=TG1)
    wdh = arena1[:, 11 * TG1:NA1].rearrange("p (a t) -> p a t", t=1024)
    qT = arena1[:, 0:4 * TG1].rearrange("p (a t) -> p a t", t=TG1)
    kT = arena1[:, 4 * TG1:8 * TG1].rearrange("p (a t) -> p a t", t=TG1)
    gate = arena1[:, 8 * TG1:12 * TG1].rearrange("p (a t) -> p a t", t=TG1)
    uT = arena1[:, 12 * TG1:16 * TG1].rearrange("p (a t) -> p a t", t=TG1)
    Vtok = arena1[:, 16 * TG1:16 * TG1 + 9 * 512].rearrange("p (a t) -> p a t", t=512)
    assert 16 * TG1 + 9 * 512 <= NA1
    xTb = arena1[:, 0:2048].rearrange("p (a r t) -> p a r t", r=2, t=256)
    ygb = arena1[:, 2048:3072].rearrange("p (a t) -> p a t", t=256)
    Ct = arena1[:, 4 * TG1:4 * TG1 + 4096].rearrange("p (a r t) -> p a r t", r=2, t=128)
    w1f = arena1[:, 8 * TG1:8 * TG1 + 2048].bitcast(F32)
    rrB = [arena1[:, 8 * TG1 + 2048 + 1024 * i:8 * TG1 + 2048 + 1024 * (i + 1)].bitcast(F32).rearrange(
        "p (a t) -> p a t", t=128) for i in range(2)]
    w2f = arena1[:, 16 * TG1:16 * TG1 + 2048].bitcast(F32)
    xTbB = arena1[:, 16 * TG1 + 2048:16 * TG1 + 4096].rearrange("p (a r t) -> p a r t", r=2, t=256)
    tmpA = arena2[:, 0:TG1]
    tmpB = arena2[:, TG1:2 * TG1]
    tmpC = arena2[:, 2 * TG1:3 * TG1]
    tmpD = arena2[:, 3 * TG1:4 * TG1]
    rmask = arena2[:, 8 * TG1:9 * TG1]
    o = 0
    cosT = arena2[:, o:o + 2048].rearrange("p (a t) -> p a t", t=128); o += 2048
    sinT = arena2[:, o:o + 2048].rearrange("p (a t) -> p a t", t=128); o += 2048
    tq = [arena2[:, o + i * 512:o + (i + 1) * 512].rearrange("p (a t) -> p a t", t=128) for i in range(4)]; o += 2048
    srcB = arena2[:, o:o + 2048].rearrange("p (r e a c) -> p r e a c", r=2, e=2, c=32)
    rin = [arena2[:, o + i * 512:o + (i + 1) * 512].rearrange("p (a t) -> p a t", t=128) for i in range(2)]; o += 1024
    rr = [arena2[:, o + i * 512:o + (i + 1) * 512].rearrange("p (a t) -> p a t", t=128) for i in range(2)]; o += 1024
    Braw = arena2[:, o:o + 512].rearrange("p (r a c) -> p r a c", r=2, c=16)
    Craw = arena2[:, o + 512:o + 1024].rearrange("p (r a c) -> p r a c", r=2, c=64)
    BB = arena2[:, o + 1024:o + 1536].rearrange("p (r a c) -> p r a c", r=2, c=16)
    srcC = arena2[:, o + 1536:o + 1792].rearrange("p (r c) -> p r c", r=2)
    coef0 = arena2[:, o:o + 16 * 80].rearrange("p (a t) -> p a t", t=80); o += 1280
    yfp = arena2[:, o:o + 1024].rearrange("p (a t) -> p a t", t=256); o += 1024
    pq = [arena2[:, o + i * 512:o + (i + 1) * 512].rearrange("p (a t) -> p a t", t=128) for i in range(2)]; o += 1024
    assert o <= NA2
    rr2 = [rr, rrB]
    yfp2 = [yfp, sgt[:, :, :].rearrange("p a t -> p (a t)").rearrange("p (a t) -> p a t", t=256)]
    xTb2 = [xTb, xTbB]
    s5stg = stg[0:16, :, :].rearrange("p a t -> p (a t)")

    R = {}

    def res(name):
        if name not in R:
            R[name] = Res(name)
        return R[name]

    r_h = [[res("h%d_%d" % (k, t)) for t in range(3)] for k in range(8)]
    r_xn = [[res("xn%d_%d" % (k, t)) for t in range(3)] for k in range(8)]
    r_a1 = res("arena1_all")
    r_stg = [dres("stg0"), dres("stg1")]
    r_ringA = [dres("ringA%d" % i) for i in range(NRA)]
    r_wdh = [dres("wdh%d" % i) for i in range(11)]
    r_hid = [[res("hid%d_%d" % (a, t)) for t in range(3)] for a in range(11)]
    r_const = dres("const")
    r_misc = res("misc")

    def pe(fn, reads, writes):
        kb.op("pe", fn, reads, writes)

    def act(fn, reads, writes):
        kb.op("act", fn, reads, writes)

    def dve(fn, reads, writes):
        kb.op("dve", fn, reads, writes)

    def pool(fn, reads, writes):
        kb.op("pool", fn, reads, writes)

    def mm_group(out_ap, pairs, reads, writes):
        n = len(pairs)

        def fn(e):
            ins = None
            for i, (l, r) in enumerate(pairs):
                ins = e.matmul(out_ap, lhsT=l, rhs=r, start=(i == 0), stop=(i == n - 1))
            return ins
        pe(fn, reads, writes)

    r_identf, r_identb, r_ones, r_masks = res("identf"), res("identb"), res("onesb"), res("masks")
    pool(lambda e: e.memset(identf[:], 0.0), [], [r_identf])
    pool(lambda e: e.affine_select(out=identf[:], in_=identf[:], pattern=[[-1, 128]], compare_op=ALU.not_equal,
                                   fill=1.0, base=0, channel_multiplier=1), [r_identf], [r_identf])
    pool(lambda e: e.tensor_copy(out=identb[:], in_=identf[:]), [r_identf], [r_identb])
    pool(lambda e: e.memset(onesb[:], 1.0), [], [r_ones])
    pool(lambda e: e.memset(epsT[:], EPS), [], [r_misc])
    pool(lambda e: e.memset(maskP[:], 1.0), [], [r_masks])
    pool(lambda e: e.affine_select(out=maskP[:], in_=maskP[:], pattern=[[1, 128]], compare_op=ALU.is_ge,
                                   fill=0.0, base=0, channel_multiplier=-1), [r_masks], [r_masks])
    pool(lambda e: e.memset(maskP[0:64, 64:128], 0.0), [r_masks], [r_masks])
    pool(lambda e: e.memset(Eseq[:], 1.0), [], [r_masks])
    pool(lambda e: e.affine_select(out=Eseq[:], in_=Eseq[:], pattern=[[1, 64]], compare_op=ALU.is_ge,
                                   fill=0.0, base=0, channel_multiplier=-4), [r_masks], [r_masks])
    pool(lambda e: e.affine_select(out=Eseq[:], in_=Eseq[:], pattern=[[-1, 64]], compare_op=ALU.is_ge,
                                   fill=0.0, base=3, channel_multiplier=4), [r_masks], [r_masks])
    pool(lambda e: e.memset(rowm[:], 1.0), [], [r_masks])
    pool(lambda e: e.affine_select(out=rowm[:], in_=rowm[:], pattern=[[-4, 16]], compare_op=ALU.is_ge,
                                   fill=0.0, base=0, channel_multiplier=1), [r_masks], [r_masks])
    pool(lambda e: e.affine_select(out=rowm[:], in_=rowm[:], pattern=[[4, 16]], compare_op=ALU.is_ge,
                                   fill=0.0, base=3, channel_multiplier=-1), [r_masks], [r_masks])
    mm_group(PB[0][0:64, 0:64], [(Eseq[:, :], Eseq[:, :])], [r_masks], [PR[0]])
    pool(lambda e: e.memset(mask0[:], 0.0), [r_masks], [r_masks])
    dve(lambda e: e.tensor_tensor(out=mask0[0:64, 0:64], in0=PB[0][0:64, 0:64], in1=maskP[0:64, 0:64], op=ALU.mult),
        [PR[0], r_masks], [r_masks])
    dve(lambda e: e.tensor_copy(out=mask0[64:80, 64:80], in_=maskP[64:80, 64:80]), [r_masks], [r_masks])
    pool(lambda e: e.memset(evod[:], 0.0), [], [r_misc])
    ev4 = sb("ev4", [128, 4])
    pool(lambda e: e.memset(ev4[:], 1.0), [], [r_misc])
    pool(lambda e: e.affine_select(out=ev4[:], in_=ev4[:], pattern=[[-32, 4]], compare_op=ALU.is_ge,
                                   fill=0.0, base=0, channel_multiplier=1), [r_misc], [r_misc])
    pool(lambda e: e.affine_select(out=ev4[:], in_=ev4[:], pattern=[[32, 4]], compare_op=ALU.is_ge,
                                   fill=0.0, base=15, channel_multiplier=-1), [r_misc], [r_misc])
    dve(lambda e: e.tensor_tensor(out=evod[:, 0:2], in0=ev4[:, 0:2], in1=ev4[:, 2:4], op=ALU.add), [r_misc], [r_misc])
    dve(lambda e: e.tensor_tensor(out=evod[:, 0:1], in0=evod[:, 0:1], in1=evod[:, 1:2], op=ALU.add), [r_misc], [r_misc])
    dve(lambda e: e.tensor_scalar(out=evod[:, 1:2], in0=evod[:, 0:1], scalar1=-1.0, scalar2=1.0, op0=ALU.mult,
                                  op1=ALU.add), [r_misc], [r_misc])
    dve(lambda e: e.tensor_scalar(out=evod[:, 2:4], in0=evod[:, 0:2], scalar1=-1.0, scalar2=None, op0=ALU.mult),
        [r_misc], [r_misc])
    pool(lambda e: e.memset(Sst[:], 0.0), [], [res("Sst")])
    pool(lambda e: e.memset(car[:], 0.0), [], [res("car")])

    r_PV = res("PV")
    for (src, dst, nrows_all) in ((pvec, PV, NPROW), (s5pv, S5P, DEPTH * 48)):
        for r0 in range(0, nrows_all, 128):
            nr = min(128, nrows_all - r0)
            kb.dma("sp", stg[0:nr, 0, 0:128], src[r0:r0 + nr, :], r_stg[0], [], [r_stg[0]])
            pe(lambda e, nr=nr: e.transpose(PB[0][:, 0:nr], stg[0:nr, 0, 0:128], identf[0:nr, 0:nr]),
               [r_stg[0], r_identf], [PR[0]])
            dve(lambda e, nr=nr, r0=r0, dst=dst: e.tensor_copy(out=dst[:, r0:r0 + nr], in_=PB[0][:, 0:nr]),
                [PR[0]], [r_PV])

    lbe = sb("lbe", [128, 4, 4])
    lbs = sb("lbs", [128, 4])
    act(lambda e: e.activation(out=lbe[:].rearrange("p a b -> p (a b)"), in_=PV[:, R_LBP:R_LBP + 16], func=AF.Exp),
        [r_PV], [r_misc])
    dve(lambda e: e.tensor_tensor(out=lbs[:], in0=lbe[:, 0, :], in1=lbe[:, 1, :], op=ALU.add), [r_misc], [r_misc])
    dve(lambda e: e.tensor_tensor(out=lbs[:], in0=lbs[:], in1=lbe[:, 2, :], op=ALU.add), [r_misc], [r_misc])
    dve(lambda e: e.tensor_tensor(out=lbs[:], in0=lbs[:], in1=lbe[:, 3, :], op=ALU.add), [r_misc], [r_misc])
    dve(lambda e: e.reciprocal(out=lbs[:], in_=lbs[:]), [r_misc], [r_misc])
    for l in range(4):
        dve(lambda e, l=l: e.tensor_tensor(out=lbe[:, l, :], in0=lbe[:, l, :], in1=lbs[:], op=ALU.mult),
            [r_misc], [r_misc])
    r_lb = res("lbT")
    dve(lambda e: e.memset(lbT[:, 0, :, 0:1], 0.0), [], [r_lb])
    for l in range(1, 4):
        dve(lambda e, l=l: e.tensor_tensor(out=lbT[:, l, :, 0:1], in0=lbT[:, l - 1, :, 0:1],
                                           in1=lbe[:, l, :].unsqueeze(2), op=ALU.add), [r_misc, r_lb], [r_lb])
    dve(lambda e: e.tensor_scalar(out=lbT[:, :, :, 1:2], in0=lbT[:, :, :, 0:1], scalar1=-1.0, scalar2=1.0,
                                  op0=ALU.mult, op1=ALU.add), [r_lb], [r_lb])

    class WStream:
        def __init__(self):
            self.plan = {"A": [], "B": []}
            self.issued = {"A": 0, "B": 0}
            self.consumed = {"A": 0, "B": 0}
            self.released = set()

        def add(self, cls, src_fn, hold=None):
            self.plan[cls].append((src_fn, hold))

        def release(self, key):
            self.released.add(key)
            self.pump()

        def pump(self):
            for cls, nslot, rlist in (("A", NRA, r_ringA), ("B", 11, r_wdh)):
                while (self.issued[cls] < len(self.plan[cls]) and
                       self.issued[cls] - self.consumed[cls] < nslot):
                    i = self.issued[cls]
                    src_fn, hold = self.plan[cls][i]
                    if hold is not None and hold not in self.released:
                        break
                    slot = i % nslot
                    dst, src = src_fn(slot)
                    kb.dma("pool", dst, src, rlist[slot], [], [rlist[slot]])
                    self.issued[cls] += 1

        def take(self, cls, n=1):
            self.pump()
            i = self.consumed[cls]
            nslot = NRA if cls == "A" else 11
            assert i + n <= self.issued[cls], (cls, i, n, self.issued[cls])
            return i % nslot

        def done(self, cls, n=1):
            self.consumed[cls] += n
            self.pump()

    WS = WStream()

    def colblock_src(w, l, c0, ncols):
        def f(slot):
            dst = ringA[:, slot, 0:8 * ncols].rearrange("p (k c) -> p k c", c=ncols)
            src = w[l, :, c0:c0 + ncols].rearrange("(k p) c -> p k c", p=128)
            return dst, src
        return f

    def rowblock_src(w, l, r0):
        def f(slot):
            return wdh[:, slot, :], w[l, r0:r0 + 128, :]
        return f

    def wout_src(l, kk):
        def f(slot):
            dst = ringA[:, slot, 0:2048].rearrange("p (k c) -> p k c", c=1024)
            src = w_out[l, kk * 256:(kk + 1) * 256, :].rearrange("(k p) c -> p k c", p=128)
            return dst, src
        return f

    def glu_src(l):
        def f(slot):
            dst = ringA[:, slot, 0:2048].rearrange("p (k c) -> p k c", c=512)
            src = w_glu[l].rearrange("(k p) c -> p k c", p=128)
            return dst, src
        return f

    groups = [dict(T=TG1, x0=0, tiles=[(0, 80), (80, 592), (592, 1104)]),
              dict(T=TG2, x0=TG1, tiles=[(0, 512), (512, 1024)])]
    WIN_ORDER = [2, 4, 6, 8, 3, 5, 7, 9, 0, 1]
    for G in groups:
        for l in range(depth):
            for (wg, wu, wd) in ((w_f1g, w_f1u, w_f1d), (w_f2g, w_f2u, w_f2d)):
                for half in range(2):
                    for hb in range(11):
                        c0 = (half * 11 + hb) * 128
                        WS.add("A", colblock_src(wg, l, c0, 128))
                        WS.add("A", colblock_src(wu, l, c0, 128))
                    for hb in range(11):
                        hold = None
                        if wd is w_f2d and half == 0 and hb == 0 and mixer:
                            hold = "f2_%d_%d" % (groups.index(G), l)
                        WS.add("B", rowblock_src(wd, l, (half * 11 + hb) * 128), hold=hold)
                if wg is w_f1g and mixer:
                    for b in WIN_ORDER:
                        WS.add("A", colblock_src(w_in, l, b * 256, 256))
                    if s5:
                        WS.add("A", glu_src(l))
                    for kk in range(4):
                        WS.add("A", wout_src(l, kk))

    def tile_idx(G, c0):
        return [t[0] for t in G["tiles"]].index(c0)

    sqi = [0]
    r_sq = [res("sq%d" % i) for i in range(3)]
    r_rt = [res("rt0"), res("rt1")]
    r_sg = [res("sg0"), res("sg1")]
    psrot = [0]

    def rmsnorm_to_xn(G, gcol, nfeat_inv=1.0 / D):
        for ti, (c0, c1) in enumerate(G["tiles"]):
            n = c1 - c0
            pbi = 6
            for k in range(8):
                s = sqi[0] % 3
                sqi[0] += 1
                act(lambda e, k=k, s=s: e.activation(out=sq[:, s, 0:n], in_=hT[:, k, c0:c1], func=AF.Square),
                    [r_h[k][ti]], [r_sq[s]])
                pe(lambda e, k=k, s=s: e.matmul(PB[pbi][:, 0:n], lhsT=onesb[:], rhs=sq[:, s, 0:n],
                                                start=(k == 0), stop=(k == 7)),
                   [r_sq[s], r_ones], [PR[pbi]])
            j = ti % 2
            act(lambda e, j=j: e.activation(out=rt[:, j, 0:n], in_=PB[pbi][:, 0:n], func=AF.Ln, bias=epsT[:, 0:1],
                                            scale=nfeat_inv), [PR[pbi], r_misc], [r_rt[j]])
            act(lambda e, j=j: e.activation(out=rt[:, j, 0:n], in_=rt[:, j, 0:n], func=AF.Exp, scale=-0.5),
                [r_rt[j]], [r_rt[j]])
            for k in range(8):
                dve(lambda e, k=k, j=j: e.scalar_tensor_tensor(out=xn[:, k, c0:c1], in0=hT[:, k, c0:c1],
                                                               scalar=PV[:, gcol + k:gcol + k + 1], op0=ALU.mult,
                                                               in1=rt[:, j, 0:n], op1=ALU.mult),
                    [r_h[k][ti], r_rt[j], r_PV], [r_xn[k][ti]])

    def ffn(G, l):
        ntile = len(G["tiles"])
        for half in range(2):
            for hb in range(11):
                sg_ = WS.take("A", 2)
                su_ = (sg_ + 1) % NRA
                wgv = ringA[:, sg_, 0:1024].rearrange("p (k c) -> p k c", c=128)
                wuv = ringA[:, su_, 0:1024].rearrange("p (k c) -> p k c", c=128)
                for ti, (c0, c1) in enumerate(G["tiles"]):
                    n = c1 - c0
                    pg = psrot[0] % 2
                    psrot[0] += 1
                    bg, bu = 2 * pg, 2 * pg + 1
                    mm_group(PB[bg][:, 0:n], [(wgv[:, k, :], xn[:, k, c0:c1]) for k in range(8)],
                             [r_ringA[sg_]] + [r_xn[k][ti] for k in range(8)], [PR[bg]])
                    mm_group(PB[bu][:, 0:n], [(wuv[:, k, :], xn[:, k, c0:c1]) for k in range(8)],
                             [r_ringA[su_]] + [r_xn[k][ti] for k in range(8)], [PR[bu]])
                    act(lambda e, pg=pg, bg=bg: e.activation(out=sgt[:, pg, 0:n], in_=PB[bg][:, 0:n], func=AF.Silu),
                        [PR[bg]], [r_sg[pg]])
                    dve(lambda e, pg=pg, bu=bu, hb=hb: e.tensor_tensor(out=hid[:, hb, c0:c1], in0=sgt[:, pg, 0:n],
                                                                      in1=PB[bu][:, 0:n], op=ALU.mult),
                        [r_sg[pg], PR[bu]], [r_hid[hb][ti]])
                WS.done("A", 2)
            s0 = WS.take("B", 11)
            for ti, (c0, c1) in enumerate(G["tiles"]):
                n = c1 - c0
                for ot in range(8):
                    pbi = 4 + (psrot[0] % 2)
                    psrot[0] += 1
                    mm_group(PB[pbi][:, 0:n],
                             [(wdh[:, (s0 + hb) % 11, ot * 128:(ot + 1) * 128], hid[:, hb, c0:c1]) for hb in range(11)],
                             r_wdh + [r_hid[hb][ti] for hb in range(11)], [PR[pbi]])
                    dve(lambda e, pbi=pbi, ot=ot: e.scalar_tensor_tensor(out=hT[:, ot, c0:c1], in0=PB[pbi][:, 0:n],
                                                                        scalar=0.5, op0=ALU.mult,
                                                                        in1=hT[:, ot, c0:c1], op1=ALU.add),
                        [PR[pbi], r_h[ot][ti]], [r_h[ot][ti]])
            WS.done("B", 11)

    def load_x(G):
        T = G["T"]
        nblk = (T + 127) // 128
        for b in range(nblk):
            c0 = b * 128
            nr = min(128, T - c0)
            ti = None
            s = b % 2
            kb.dma("sp", stg[0:nr, s, :], xin[G["x0"] + c0:G["x0"] + c0 + nr, :], r_stg[s], [], [r_stg[s]])
            for kk in range(2):
                pbi = psrot[0] % 2
                psrot[0] += 1

                def fn(e, kk=kk, pbi=pbi, s=s, nr=nr):
                    ins = None
                    for k4 in range(4):
                        k = kk * 4 + k4
                        ins = e.transpose(PB[pbi][:, k4 * 128:k4 * 128 + nr], stg[0:nr, s, k * 128:(k + 1) * 128],
                                          identf[0:nr, 0:nr])
                    return ins
                pe(fn, [r_stg[s], r_identf], [PR[pbi]])
                tis = sorted(set(ti for ti, (a, bb) in enumerate(G["tiles"]) if a < c0 + nr and bb > c0))
                wr = [r_h[kk * 4 + k4][ti] for k4 in range(4) for ti in tis]
                eng = act if kk == 0 else dve
                if kk == 0:
                    act(lambda e, kk=kk, pbi=pbi, nr=nr, c0=c0: e.activation(
                        out=hT[:, kk * 4:(kk + 1) * 4, c0:c0 + nr],
                        in_=PB[pbi][:, :].rearrange("p (a t) -> p a t", t=128)[:, :, 0:nr], func=AF.Copy),
                        [PR[pbi]], wr)
                else:
                    dve(lambda e, kk=kk, pbi=pbi, nr=nr, c0=c0: e.tensor_copy(
                        out=hT[:, kk * 4:(kk + 1) * 4, c0:c0 + nr],
                        in_=PB[pbi][:, :].rearrange("p (a t) -> p a t", t=128)[:, :, 0:nr]),
                        [PR[pbi]], wr)

    def store_y(G):
        T = G["T"]
        for ti, (c0, c1) in enumerate(G["tiles"]):
            n = c1 - c0
            pbi = 6
            for k in range(8):
                s = sqi[0] % 3
                sqi[0] += 1
                act(lambda e, k=k, s=s: e.activation(out=sq[:, s, 0:n], in_=hT[:, k, c0:c1], func=AF.Square),
                    [r_h[k][ti]], [r_sq[s]])
                pe(lambda e, k=k, s=s: e.matmul(PB[pbi][:, 0:n], lhsT=onesb[:], rhs=sq[:, s, 0:n],
                                                start=(k == 0), stop=(k == 7)), [r_sq[s], r_ones], [PR[pbi]])
            j = ti % 2
            act(lambda e, j=j: e.activation(out=rt[:, j, 0:n], in_=PB[pbi][:, 0:n], func=AF.Ln, bias=epsT[:, 0:1],
                                            scale=1.0 / D), [PR[pbi], r_misc], [r_rt[j]])
            act(lambda e, j=j: e.activation(out=rt[:, j, 0:n], in_=rt[:, j, 0:n], func=AF.Exp, scale=-0.5),
                [r_rt[j]], [r_rt[j]])
            for k in range(8):
                dve(lambda e, k=k, j=j: e.scalar_tensor_tensor(out=hT[:, k, c0:c1], in0=hT[:, k, c0:c1],
                                                               scalar=PV[:, R_NFIN + k:R_NFIN + k + 1], op0=ALU.mult,
                                                               in1=rt[:, j, 0:n], op1=ALU.mult),
                    [r_h[k][ti], r_rt[j], r_PV], [r_h[k][ti]])
        nblk = (T + 127) // 128
        for b in range(nblk):
            c0 = b * 128
            nr = min(128, T - c0)
            s = b % 2
            tis = sorted(set(ti for ti, (a, bb) in enumerate(G["tiles"]) if a < c0 + nr and bb > c0))
            for kk in range(2):
                pbi = psrot[0] % 2
                psrot[0] += 1

                def fn(e, kk=kk, pbi=pbi, nr=nr, c0=c0):
                    ins = None
                    for k4 in range(4):
                        k = kk * 4 + k4
                        ins = e.transpose(PB[pbi][0:nr, k4 * 128:(k4 + 1) * 128], hT[:, k, c0:c0 + nr], identf[:, :])
                    return ins
                pe(fn, [r_identf] + [r_h[kk * 4 + k4][ti] for k4 in range(4) for ti in tis], [PR[pbi]])
                if kk == 0:
                    act(lambda e, pbi=pbi, nr=nr, s=s: e.activation(out=stg[0:nr, s, 0:512], in_=PB[pbi][0:nr, :],
                                                                    func=AF.Copy), [PR[pbi]], [r_stg[s]])
                else:
                    dve(lambda e, pbi=pbi, nr=nr, s=s: e.tensor_copy(out=stg[0:nr, s, 512:1024], in_=PB[pbi][0:nr, :]),
                        [PR[pbi]], [r_stg[s]])
            kb.dma("sp", yout[G["x0"] + c0:G["x0"] + c0 + nr, :], stg[0:nr, s, :], r_stg[s], [r_stg[s]], [])

    r_q = [res("qT%d" % i) for i in range(4)]
    r_k = [res("kT%d" % i) for i in range(4)]
    r_gate = [res("gate%d" % i) for i in range(4)]
    r_u = [res("uT%d" % i) for i in range(4)]
    r_Vb = {(bi_, hf_): res("Vtok%d_%d" % (bi_, hf_)) for bi_ in range(9) for hf_ in range(2)}
    r_Vall = list(r_Vb.values())
    r_tmp = [res("tmpA"), res("tmpB"), res("tmpC"), res("tmpD")]
    r_eb = res("eb")
    r_S = [[res("S%d_%d" % (l, h)) for h in range(4)] for l in range(DEPTH)]
    r_Sd = [res("Sd%d" % i) for i in range(4)]
    r_Sdb = [res("Sdb%d" % i) for i in range(4)]
    r_ktok = [res("ktok%d" % i) for i in range(4)]
    r_kms = [res("kms%d" % i) for i in range(4)]
    r_Am = [res("Am%d" % i) for i in range(4)]
    r_hgst = [dres("hgst%d" % i) for i in range(4)]
    r_hgout = [dres("hgout%d" % i) for i in range(7)]
    hgo = [hgout[:, i, :] for i in range(3)] + [Sd[:, i, :] for i in range(4)]
    r_wo = res("wo")
    r_s5t = res("s5tables")
    r_s5sm = res("s5sm")
    r_car = res("car")
    r_carq = [res("carq%d" % i) for i in range(4)]
    cnt = dict(sd=0, kt=0, am=0, hgst=0, hgout=0, km=0)
    S5A1, S5A2 = [], []

    def mixer_fn(G, l, gi):
        T = G["T"]
        tiles = G["tiles"]
        ntile = len(tiles)
        mix_res = r_q + r_k + r_gate + r_u + r_Vall
        ffn_res = [x for a in r_hid for x in a] + r_wdh
        kb.alias(mix_res, ffn_res)
        rm_res = res("rmask")
        hg2 = r_tmp + [res("eq%d" % h) for h in range(4)] + [rm_res]
        kb.alias(hg2, S5A2)
        rmsnorm_to_xn(G, R_NMX + 8 * l)
        if True:
            dve(lambda e: e.memset(rmask[:, 0:T], 1.0), [], [rm_res])
            if gi == 0:
                dve(lambda e: e.memset(rmask[:, 0:64].rearrange("p (a b) -> p a b", b=4)[:, :, 0:1], 0.0), [rm_res], [rm_res])
                dve(lambda e: e.memset(rmask[:, 64:65], 0.0), [rm_res], [rm_res])
                dve(lambda e: e.memset(rmask[:, 80:TG1].rearrange("p (a b) -> p a b", b=64)[:, :, 0:1], 0.0), [rm_res], [rm_res])
            else:
                dve(lambda e: e.memset(rmask[:, 0:T].rearrange("p (a b) -> p a b", b=64)[:, :, 0:1], 0.0), [rm_res], [rm_res])
        if gi == 0:
            chunks = [(64, 80)] + [(80 + 64 * i, 144 + 64 * i) for i in range(16)]
            tokblocks = [(0, 80)] + [(80 + 128 * i, 208 + 128 * i) for i in range(8)]
        else:
            chunks = [(64 * i, 64 * i + 64) for i in range(16)]
            tokblocks = [(128 * i, 128 * i + 128) for i in range(8)]
        nchunk_all = len(chunks) + (16 if gi == 0 else 0)

        def xn_reads(ti):
            return [r_xn[k][ti] for k in range(8)]

        for b in WIN_ORDER:
            slot = WS.take("A")
            wv = ringA[:, slot, 0:2048].rearrange("p (k c) -> p k c", c=256)
            for j in range(2):
                zt = 2 * b + j
                hd = zt % 4
                kind = zt // 4
                if kind == 2:
                    continue
                for ti, (c0, c1) in enumerate(tiles):
                    n = c1 - c0
                    pbi = psrot[0] % 4
                    psrot[0] += 1
                    mm_group(PB[pbi][:, 0:n], [(wv[:, k, j * 128:(j + 1) * 128], xn[:, k, c0:c1]) for k in range(8)],
                             [r_ringA[slot]] + xn_reads(ti), [PR[pbi]])
                    if kind == 1:
                        act(lambda e, pbi=pbi: e.activation(out=tmpA[:, c0:c1], in_=PB[pbi][:, 0:n], func=AF.Sigmoid),
                            [PR[pbi]], [r_tmp[0]])
                        act(lambda e, pbi=pbi: e.activation(out=tmpD[:, c0:c1], in_=PB[pbi][:, 0:n], func=AF.Sigmoid,
                                                            scale=-1.0), [PR[pbi]], [r_tmp[3]])
                    elif kind == 0:
                        act(lambda e, pbi=pbi, hd=hd: e.activation(out=qT[:, hd, c0:c1], in_=PB[pbi][:, 0:n],
                                                                   func=AF.Silu), [PR[pbi]], [r_q[hd]])
                    elif kind == 3:
                        act(lambda e, pbi=pbi, hd=hd: e.activation(out=gate[:, hd, c0:c1], in_=PB[pbi][:, 0:n],
                                                                   func=AF.Silu), [PR[pbi]], [r_gate[hd]])
                    else:
                        act(lambda e, pbi=pbi, hd=hd: e.activation(out=uT[:, hd, c0:c1], in_=PB[pbi][:, 0:n],
                                                                   func=AF.Copy), [PR[pbi]], [r_u[hd]])
                if kind == 0:
                    dve(lambda e, hd=hd: e.tensor_tensor(out=qT[:, hd, 0:T], in0=qT[:, hd, 0:T],
                                                         in1=arena2[:, (4 + hd) * TG1:(4 + hd) * TG1 + T], op=ALU.mult),
                        [r_q[hd], res("eq%d" % hd)], [r_q[hd]])
                if kind == 1:
                    lb0 = lbT[:, l, hd, 0:1]
                    lb1 = lbT[:, l, hd, 1:2]
                    dve(lambda e: e.tensor_scalar(out=tmpA[:, 0:T], in0=tmpA[:, 0:T], scalar1=lb1, scalar2=lb0,
                                                  op0=ALU.mult, op1=ALU.add), [r_tmp[0], r_lb], [r_tmp[0]])
                    act(lambda e: e.activation(out=tmpB[:, 0:T], in_=tmpA[:, 0:T], func=AF.Ln), [r_tmp[0]], [r_tmp[1]])
                    dve(lambda e: e.tensor_tensor_scan(out=tmpC[:, 0:T], data0=rmask[:, 0:T], data1=tmpB[:, 0:T],
                                                       initial=0.0, op0=ALU.mult, op1=ALU.add),
                        [r_tmp[1], rm_res], [r_tmp[2]])
                    segs = []
                    if gi == 0:
                        segs.append((0, 64, 4))
                        segs.append((64, 80, 16))
                        segs.append((80, TG1, 64))
                    else:
                        segs.append((0, T, 64))
                    ebo = 0
                    for (a0, a1, L) in segs:
                        nch = (a1 - a0) // L
                        v = tmpC[:, a0:a1].rearrange("p (a b) -> p a b", b=L)
                        dve(lambda e, v=v, a0=a0, a1=a1, L=L, nch=nch: e.tensor_tensor(
                            out=tmpB[:, a0:a1].rearrange("p (a b) -> p a b", b=L), in0=v,
                            in1=v[:, :, L - 1:L].to_broadcast([128, nch, L]), op=ALU.subtract),
                            [r_tmp[2]], [r_tmp[1]])
                        act(lambda e, v=v, L=L, nch=nch, ebo=ebo, hd=hd: e.activation(
                            out=eb[:, hd, ebo:ebo + nch].unsqueeze(2), in_=v[:, :, L - 1:L], func=AF.Exp),
                            [r_tmp[2]], [r_eb])
                        ebo += nch
                    act(lambda e: e.activation(out=tmpC[:, 0:T], in_=tmpB[:, 0:T], func=AF.Exp, scale=-1.0),
                        [r_tmp[1]], [r_tmp[2]])
                    act(lambda e, hd=hd: e.activation(out=arena2[:, (4 + hd) * TG1:(4 + hd) * TG1 + T],
                                                      in_=tmpB[:, 0:T], func=AF.Exp), [r_tmp[1]], [res("eq%d" % hd)])
                    dve(lambda e, hd=hd: e.scalar_tensor_tensor(out=kT[:, hd, 0:T], in0=tmpD[:, 0:T], scalar=lb1,
                                                                op0=ALU.mult, in1=tmpC[:, 0:T], op1=ALU.mult),
                        [r_tmp[3], r_tmp[2], r_lb], [r_k[hd]])
            if b in (4, 5):
                for bi, (t0, t1) in enumerate(tokblocks):
                    nt = t1 - t0
                    pbi = 4 + (psrot[0] % 2)
                    psrot[0] += 1
                    tis = sorted(set(ti for ti, (a, bb) in enumerate(tiles) if a < t1 and bb > t0))
                    mm_group(PB[pbi][0:nt, 0:256], [(xn[:, k, t0:t1], wv[:, k, :]) for k in range(8)],
                             [r_ringA[slot]] + [r_xn[k][ti] for k in range(8) for ti in tis], [PR[pbi]])
                    cb = (b - 4) * 256
                    if bi % 2 == 0:
                        act(lambda e, pbi=pbi, nt=nt, bi=bi, cb=cb: e.activation(out=Vtok[0:nt, bi, cb:cb + 256],
                                                                                in_=PB[pbi][0:nt, 0:256], func=AF.Copy),
                            [PR[pbi]], [r_Vb[(bi, b - 4)]])
                    else:
                        dve(lambda e, pbi=pbi, nt=nt, bi=bi, cb=cb: e.tensor_copy(out=Vtok[0:nt, bi, cb:cb + 256],
                                                                                 in_=PB[pbi][0:nt, 0:256]),
                            [PR[pbi]], [r_Vb[(bi, b - 4)]])
            WS.done("A", 1)

        if hgrn:
            hgrn_fn(G, l, gi, chunks, tokblocks)
        else:
            for hd in range(4):
                dve(lambda e, hd=hd: e.memset(xn[:, hd, 0:T], 0.0), [], [r_xn[hd][ti] for ti in range(ntile)])
        if s5:
            s5_fn(G, l, gi)
        else:
            for hd in range(4):
                dve(lambda e, hd=hd: e.memset(xn[:, 4 + hd, 0:T], 0.0), [], [r_xn[4 + hd][ti] for ti in range(ntile)])
            if True:
                WS.take("A") if False else None
        kb.alias(r_wdh, mix_res + S5A1)
        WS.release("f2_%d_%d" % (gi, l))
        sA = WS.take("A", 4)
        for ti, (c0, c1) in enumerate(tiles):
            n = c1 - c0
            for ot in range(8):
                pbi = 4 + (psrot[0] % 2)
                psrot[0] += 1
                mm_group(PB[pbi][:, 0:n],
                         [(ringA[:, (sA + k // 2) % NRA, (k % 2) * 1024 + ot * 128:(k % 2) * 1024 + (ot + 1) * 128],
                           xn[:, k, c0:c1]) for k in range(8)],
                         r_ringA + [r_xn[k][ti] for k in range(8)], [PR[pbi]])
                dve(lambda e, pbi=pbi, ot=ot: e.tensor_tensor(out=hT[:, ot, c0:c1], in0=PB[pbi][:, 0:n],
                                                              in1=hT[:, ot, c0:c1], op=ALU.add),
                    [PR[pbi], r_h[ot][ti]], [r_h[ot][ti]])
        WS.done("A", 4)
        kb.alias([x for a in r_hid for x in a], mix_res + S5A1)

    def hg_norm_out(G, l, hd, ti, c0, c1, pbo):
        n = c1 - c0
        s = sqi[0] % 3
        sqi[0] += 1
        act(lambda e: e.activation(out=sq[:, s, 0:n], in_=PB[pbo][:, 0:n], func=AF.Square), [PR[pbo]], [r_sq[s]])
        pe(lambda e: e.matmul(PB[6][:, 0:n], lhsT=onesb[:], rhs=sq[:, s, 0:n], start=True, stop=True),
           [r_sq[s], r_ones], [PR[6]])
        j = sqi[0] % 2
        act(lambda e: e.activation(out=rt[:, j, 0:n], in_=PB[6][:, 0:n], func=AF.Ln, bias=epsT[:, 0:1],
                                        scale=1.0 / 128), [PR[6], r_misc], [r_rt[j]])
        act(lambda e: e.activation(out=rt[:, j, 0:n], in_=rt[:, j, 0:n], func=AF.Exp, scale=-0.5),
            [r_rt[j]], [r_rt[j]])
        dve(lambda e: e.tensor_tensor(out=sgt[:, j, 0:n], in0=PB[pbo][:, 0:n], in1=rt[:, j, 0:n], op=ALU.mult),
            [PR[pbo], r_rt[j]], [r_sg[j]])
        gcol = R_HGN + 4 * l + hd
        dve(lambda e: e.scalar_tensor_tensor(out=xn[:, hd, c0:c1], in0=sgt[:, j, 0:n], scalar=PV[:, gcol:gcol + 1],
                                             op0=ALU.mult, in1=gate[:, hd, c0:c1], op1=ALU.mult),
            [r_sg[j], r_gate[hd], r_PV], [r_xn[hd][ti]])

    def hgrn_fn(G, l, gi, chunks, tokblocks):
        tiles = G["tiles"]
        r_pu = [PR[4 + h % 2] for h in range(4)]
        r_psc = [PR[4 + h % 2] for h in range(4)]
        r_ptr = [PRb for h in range(4)]
        pu = [PB[4 + h % 2][:, 0:128] for h in range(4)]
        psc = [PB[4 + h % 2][:, 0:128] for h in range(4)]
        ptr = [PBb[:, 0:128] for h in range(4)]
        for ti, (c0t, c1t) in enumerate(tiles):
            tch = [c for c in chunks if c[0] >= c0t and c[1] <= c1t]
            tbl = [(bi, tb) for bi, tb in enumerate(tokblocks) if tb[0] >= c0t and tb[1] <= c1t]
            for (bi, (t0, t1)) in tbl:
                nb = t1 - t0
                msk = mask0 if (gi == 0 and bi == 0) else maskP

                def fsc(e, nb=nb, t0=t0, t1=t1):
                    ins = None
                    for hd in range(4):
                        ins = e.matmul(PB[6][0:nb, 128 * hd:128 * hd + nb], lhsT=kT[:, hd, t0:t1], rhs=qT[:, hd, t0:t1],
                                       start=True, stop=True)
                    return ins
                pe(fsc, r_k + r_q, [PR[6]])
                dve(lambda e, msk=msk, nb=nb: e.tensor_tensor(
                    out=Am[0:nb, :, 0:nb], in0=PB[6][0:nb, :].rearrange("p (a t) -> p a t", t=128)[:, :, 0:nb],
                    in1=msk[0:nb, 0:nb].unsqueeze(1).to_broadcast([nb, 4, nb]), op=ALU.mult),
                    [PR[6], r_masks], r_Am)

                def ftr(e, nb=nb, t0=t0, t1=t1):
                    ins = None
                    for hd in range(4):
                        ins = e.transpose(PBb[0:nb, 128 * hd:128 * (hd + 1)], kT[:, hd, t0:t1], identb[:, :])
                    return ins
                pe(ftr, r_k + [r_identb], [PRb])
                act(lambda e, nb=nb: e.activation(out=ktok[0:nb, :, :].rearrange("p a t -> p (a t)"),
                                                  in_=PBb[0:nb, 0:512], func=AF.Copy), [PRb], r_ktok)
                if gi == 0 and bi == 0:
                    for sq_ in range(NSEQ):
                        hss = []
                        for hd in range(4):
                            hs = cnt["hgst"] % 4
                            cnt["hgst"] += 1
                            hss.append(hs)
                            kb.dma("sp", hgst[:, hs, :], st_hg[l, sq_, hd], r_hgst[hs], [], [r_hgst[hs]])
                        for hd in range(4):
                            dve(lambda e, hd=hd, sq_=sq_: e.tensor_scalar(out=kms[:, hd, :], in0=ktok[0:64, hd, :],
                                                                          scalar1=rowm[:, sq_:sq_ + 1], scalar2=None,
                                                                          op0=ALU.mult),
                                [r_ktok[hd], r_masks], [r_kms[hd]])
                        for hd in range(4):
                            act(lambda e, hd=hd, hs=hss[hd], sq_=sq_: e.activation(
                                out=Sdb[:, hd, :], in_=hgst[:, hs, :], func=AF.Copy, scale=eb[:, hd, sq_:sq_ + 1]),
                                [r_hgst[hss[hd]], r_eb], [r_Sdb[hd]])
                        for pr in range(2):
                            def fu(e, pr=pr):
                                ins = None
                                for hd in (pr, pr + 2):
                                    ins = e.matmul(PB[4 + pr][:, 128 * (hd // 2):128 * (hd // 2) + 128],
                                                   lhsT=kms[:, hd, :], rhs=Vtok[0:64, 0, hd * 128:(hd + 1) * 128],
                                                   start=True, stop=True)
                                return ins
                            pe(fu, [r_kms[pr], r_kms[pr + 2], r_Vb[(0, 0)], r_Vb[(0, 1)]], [PR[4 + pr]])
                        for hd in range(4):
                            pe(lambda e, hd=hd, sq_=sq_: e.matmul(PB[hd][:, 4 * sq_:4 * sq_ + 4], lhsT=Sdb[:, hd, :],
                                                                  rhs=qT[:, hd, 4 * sq_:4 * sq_ + 4],
                                                                  start=(sq_ == 0), stop=False),
                               [r_Sdb[hd], r_q[hd]], [PR[hd]])
                        for hd in range(4):
                            ho = cnt["hgout"] % 7
                            cnt["hgout"] += 1
                            pr = hd % 2
                            dve(lambda e, ho=ho, hd=hd, pr=pr, hs=hss[hd], sq_=sq_: e.scalar_tensor_tensor(
                                out=hgo[ho], in0=hgst[:, hs, :], scalar=eb[:, hd, sq_:sq_ + 1], op0=ALU.mult,
                                in1=PB[4 + pr][:, 128 * (hd // 2):128 * (hd // 2) + 128], op1=ALU.add),
                                [PR[4 + pr], r_hgst[hss[hd]], r_eb], [r_hgout[ho]])
                            kb.dma("act", o_hgs[l, sq_, hd], hgo[ho], r_hgout[ho], [r_hgout[ho]], [])
                    for hd in range(4):
                        pe(lambda e, hd=hd: e.matmul(PB[hd][:, 0:64], lhsT=Vtok[0:64, 0, hd * 128:(hd + 1) * 128],
                                                     rhs=Am[0:64, hd, 0:64], start=False, stop=True),
                           [r_Vb[(0, hd // 2)], r_Am[hd]], [PR[hd]])
                    blk_chunks = [(64, 80, 16)]
                else:
                    blk_chunks = [(c[0], c[1], (17 if gi == 0 else 0) + chunks.index(c) - (1 if gi == 0 else 0))
                                  for c in tch if c[0] >= t0 and c[1] <= t1]
                for (c0, c1, ecol_i) in blk_chunks:
                    L = c1 - c0
                    pb_ = c0 - t0
                    for pr in range(2):
                        def fu(e, pr=pr, pb_=pb_, L=L, bi=bi):
                            ins = None
                            for hd in (pr, pr + 2):
                                ins = e.matmul(PB[4 + pr][:, 128 * (hd // 2):128 * (hd // 2) + 128],
                                               lhsT=ktok[pb_:pb_ + L, hd, :],
                                               rhs=Vtok[pb_:pb_ + L, bi, hd * 128:(hd + 1) * 128], start=True, stop=True)
                            return ins
                        pe(fu, [r_ktok[pr], r_ktok[pr + 2], r_Vb[(bi, 0)], r_Vb[(bi, 1)]], [PR[4 + pr]])
                    for hd in range(4):
                        act(lambda e, hd=hd, ecol_i=ecol_i: e.activation(out=Sdb[:, hd, :], in_=Sst[:, l, hd, :],
                                                                         func=AF.Copy,
                                                                         scale=eb[:, hd, ecol_i:ecol_i + 1]),
                            [r_S[l][hd], r_eb], [r_Sdb[hd]])
                    for hd in range(4):
                        def fo(e, hd=hd, c0=c0, c1=c1, pb_=pb_, L=L, bi=bi):
                            e.matmul(PB[hd][:, c0 - c0t:c1 - c0t], lhsT=Sdb[:, hd, :], rhs=qT[:, hd, c0:c1],
                                     start=True, stop=False)
                            return e.matmul(PB[hd][:, c0 - c0t:c1 - c0t],
                                            lhsT=Vtok[pb_:pb_ + L, bi, hd * 128:(hd + 1) * 128],
                                            rhs=Am[pb_:pb_ + L, hd, pb_:pb_ + L], start=False, stop=True)
                        pe(fo, [r_Sdb[hd], r_q[hd], r_Vb[(bi, hd // 2)], r_Am[hd]], [PR[hd]])
                    for hd in range(4):
                        pr = hd % 2
                        dve(lambda e, hd=hd, pr=pr, ecol_i=ecol_i: e.scalar_tensor_tensor(
                            out=Sst[:, l, hd, :], in0=Sst[:, l, hd, :], scalar=eb[:, hd, ecol_i:ecol_i + 1], op0=ALU.mult,
                            in1=PB[4 + pr][:, 128 * (hd // 2):128 * (hd // 2) + 128], op1=ALU.add),
                            [PR[4 + pr], r_S[l][hd], r_eb], [r_S[l][hd]])
            for hd in range(4):
                hg_norm_out(G, l, hd, ti, c0t, c1t, hd)
        if gi == 1:
            for hd in range(4):
                ho = cnt["hgout"] % 7
                cnt["hgout"] += 1
                dve(lambda e, ho=ho, hd=hd: e.tensor_copy(out=hgo[ho], in_=Sst[:, l, hd, :]),
                    [r_S[l][hd]], [r_hgout[ho]])
                kb.dma("sp", o_hgp[l, hd], hgo[ho], r_hgout[ho], [r_hgout[ho]], [])

    def s5_tables(l):
        sm = lambda i: s5sm[:, i, :]
        LRE, LIM, LDT = (S5P[:, l * 48 + i * 16:l * 48 + (i + 1) * 16] for i in range(3))
        DT, LR, MAG, ANG, Q, RS, RC, SN, CS, AR, AI, DEN, CR, CI, T1, T2, M1 = (sm(i) for i in range(17))
        r = r_s5sm
        rp = res("S5Pv")
        act(lambda e: e.activation(out=DT, in_=LDT, func=AF.Exp), [r_PV], [r])
        dve(lambda e: e.tensor_scalar(out=LR, in0=LRE, scalar1=-1e-4, scalar2=None, op0=ALU.min), [r_PV], [r])
        dve(lambda e: e.tensor_tensor(out=T1, in0=LR, in1=DT, op=ALU.mult), [r], [r])
        act(lambda e: e.activation(out=MAG, in_=T1, func=AF.Exp), [r], [r])
        dve(lambda e: e.tensor_tensor(out=ANG, in0=LIM, in1=DT, op=ALU.mult), [r, r_PV], [r])

        T2b, Qb, M1b = sm(24), sm(25), sm(26)
        ch = [dict(dst=RS, shift=0.0, T2=T2, Q=Q, M1=M1, I=s5i[:, 0, :], res=res("s5redA")),
              dict(dst=RC, shift=math.pi / 2, T2=T2b, Q=Qb, M1=M1b, I=s5i[:, 1, :], res=res("s5redB"))]
        steps = [
            lambda e, c: e.tensor_scalar(out=c["T2"], in0=ANG, scalar1=c["shift"], scalar2=None, op0=ALU.add),
            lambda e, c: e.tensor_scalar(out=c["Q"], in0=c["T2"], scalar1=1.0 / TWO_PI, scalar2=None, op0=ALU.mult),
            lambda e, c: e.tensor_copy(out=c["I"], in_=c["Q"]),
            lambda e, c: e.tensor_copy(out=c["Q"], in_=c["I"]),
            lambda e, c: e.scalar_tensor_tensor(out=c["dst"], in0=c["Q"], scalar=-TWO_PI, op0=ALU.mult, in1=c["T2"],
                                                op1=ALU.add),
            lambda e, c: e.tensor_scalar(out=c["M1"], in0=c["dst"], scalar1=math.pi, scalar2=None, op0=ALU.is_gt),
            lambda e, c: e.scalar_tensor_tensor(out=c["dst"], in0=c["M1"], scalar=-TWO_PI, op0=ALU.mult, in1=c["dst"],
                                                op1=ALU.add),
            lambda e, c: e.tensor_scalar(out=c["M1"], in0=c["dst"], scalar1=-math.pi, scalar2=None, op0=ALU.is_lt),
            lambda e, c: e.scalar_tensor_tensor(out=c["dst"], in0=c["M1"], scalar=TWO_PI, op0=ALU.mult, in1=c["dst"],
                                                op1=ALU.add),
        ]
        for si, stp in enumerate(steps):
            for c in ch:
                rd = [r] if si == 0 else [c["res"]]
                dve(lambda e, stp=stp, c=c: stp(e, c), rd, [c["res"]])
        r_redA, r_redB = ch[0]["res"], ch[1]["res"]
        act(lambda e: e.activation(out=SN, in_=RS, func=AF.Sin), [r, r_redA], [r])
        act(lambda e: e.activation(out=CS, in_=RC, func=AF.Sin), [r, r_redB], [r])
        dve(lambda e: e.tensor_tensor(out=AR, in0=MAG, in1=CS, op=ALU.mult), [r], [r])
        dve(lambda e: e.tensor_tensor(out=AI, in0=MAG, in1=SN, op=ALU.mult), [r], [r])
        dve(lambda e: e.tensor_tensor(out=DEN, in0=LR, in1=LR, op=ALU.mult), [r], [r])
        dve(lambda e: e.tensor_tensor(out=T1, in0=LIM, in1=LIM, op=ALU.mult), [r, r_PV], [r])
        dve(lambda e: e.tensor_tensor(out=DEN, in0=DEN, in1=T1, op=ALU.add), [r], [r])
        dve(lambda e: e.reciprocal(out=DEN, in_=DEN), [r], [r])
        dve(lambda e: e.tensor_scalar(out=T1, in0=AR, scalar1=-1.0, scalar2=None, op0=ALU.add), [r], [r])
        dve(lambda e: e.tensor_tensor(out=CR, in0=T1, in1=LR, op=ALU.mult), [r], [r])
        dve(lambda e: e.tensor_tensor(out=T2, in0=AI, in1=LIM, op=ALU.mult), [r, r_PV], [r])
        dve(lambda e: e.tensor_tensor(out=CR, in0=CR, in1=T2, op=ALU.add), [r], [r])
        dve(lambda e: e.tensor_tensor(out=CR, in0=CR, in1=DEN, op=ALU.mult), [r], [r])
        dve(lambda e: e.tensor_tensor(out=CI, in0=AI, in1=LR, op=ALU.mult), [r], [r])
        dve(lambda e: e.tensor_tensor(out=T2, in0=T1, in1=LIM, op=ALU.mult), [r, r_PV], [r])
        dve(lambda e: e.tensor_tensor(out=CI, in0=CI, in1=T2, op=ALU.subtract), [r], [r])
        dve(lambda e: e.tensor_tensor(out=CI, in0=CI, in1=DEN, op=ALU.mult), [r], [r])
        return dict(MAG=MAG, SN=SN, CS=CS, CR=CR, CI=CI)

    r_Braw = dres("Braw")
    r_Craw = dres("Craw")
    r_BB = res("BB")
    r_srcB = res("srcB")
    r_srcBL = [[res("srcB%d_%d" % (a_, b_)) for b_ in range(2)] for a_ in range(2)]
    r_srcB_all = [x for a_ in r_srcBL for x in a_]
    r_BBL = [res("BB0"), res("BB1")]
    r_srcCL = [[res("srcC%d_%d" % (a_, b_)) for b_ in range(2)] for a_ in range(2)]
    r_srcC_all = [x for a_ in r_srcCL for x in a_]
    r_srcC = res("srcC")
    r_BtL = {(a_, ri_, eo_): res("Bt%d_%d_%d" % (a_, ri_, eo_)) for a_ in range(4) for ri_ in range(2) for eo_ in range(2)}
    r_CtL = {(st_, ri_): res("Ct%d_%d" % (st_, ri_)) for st_ in range(16) for ri_ in range(2)}
    r_Bt_all = list(r_BtL.values())
    r_Ct_all = list(r_CtL.values())
    r_X0 = res("X0")
    r_tq = [res("tq%d" % i) for i in range(4)]
    r_rin = [res("rin0"), res("rin1")]
    r_rr = [res("rr0"), res("rr1")]
    r_xTb = res("xTb")
    r_yfp = res("yfp")
    r_ygb = res("ygb")
    r_coef0 = res("coef0")
    r_scr = [res("scr%d" % i) for i in range(DEPTH)]
    r_scrS, r_scrL = dres("scrS"), dres("scrL")
    r_pq = [res("pq0"), res("pq1")]
    r_rr2 = [r_rr, [res("rrB0"), res("rrB1")]]
    r_rrs = [[[res("rrs%d_%d_%d" % (b_, ri_, s4_)) for s4_ in range(4)] for ri_ in range(2)] for b_ in range(2)]
    r_carq2 = [[res("carq%d_%d" % (i, ri_)) for ri_ in range(2)] for i in range(4)]
    r_xTb2 = [r_xTb, res("xTbB")]
    r_w1, r_w2 = res("w1f"), res("w2f")
    r_yfp2 = [r_yfp, res("yfpB")]

    S5A2.extend(r_tq + r_rin + r_rr + r_pq + [x for a_ in r_rrs[0] for x in a_] + [r_s5t, r_coef0, r_yfp, r_srcB, r_srcC, r_BB, r_Braw, r_Craw] + r_srcB_all + r_BBL + r_srcC_all)
    S5A1.extend([r_xTb, r_ygb] + r_Ct_all + [r_xTb2[1], r_w1, r_w2] + r_rr2[1] + [x for a_ in r_rrs[1] for x in a_])

    def s5_fn(G, l, gi):
        T = G["T"]
        tiles = G["tiles"]
        kb.alias(S5A2, r_tmp + [res("eq%d" % h) for h in range(4)] + [res("rmask")])
        kb.alias(S5A1, r_q + r_k + r_gate + r_Vall)
        kb.alias([r_yfp2[1]], r_sg)
        cached = (gi == 1)
        if cached:
            kb.defer = []
        dve(lambda e: e.memset(Ct[:], 0.0), [], r_Ct_all)
        dve(lambda e: e.memset(srcB[:], 0.0), [], r_srcB_all)
        kb.dma("sp", Braw[:, 0, :, :], b_re[l].rearrange("(a q) h -> q a h", q=128), r_Braw, [], [r_Braw])
        kb.dma("sp", Braw[:, 1, :, :], b_im[l].rearrange("(a q) h -> q a h", q=128), r_Braw, [], [r_Braw])
        kb.dma("sp", Craw[:, 0, :, :], c_re[l].rearrange("(a q) p -> q a p", q=128), r_Craw, [], [r_Craw])
        kb.dma("sp", Craw[:, 1, :, :], c_im[l].rearrange("(a q) p -> q a p", q=128), r_Craw, [], [r_Craw])
        tb = s5_tables(l)
        MAG, SN, CS, CR, CI = tb["MAG"], tb["SN"], tb["CS"], tb["CR"], tb["CI"]
        r = r_s5sm
        bc = lambda v: v.unsqueeze(2).to_broadcast([128, 16, 16])
        tmpX = srcC[:, :, :].rearrange("p a c -> p (a c)").rearrange("p (a c) -> p a c", c=16)
        rX, rY = res("bbX"), res("bbY")
        dve(lambda e: e.tensor_tensor(out=BB[:, 0], in0=Braw[:, 0], in1=bc(CR), op=ALU.mult), [r_Braw, r], [r_BBL[0]])
        dve(lambda e: e.tensor_tensor(out=BB[:, 1], in0=Braw[:, 1], in1=bc(CR), op=ALU.mult), [r_Braw, r], [r_BBL[1]])
        dve(lambda e: e.tensor_tensor(out=tmpX, in0=Braw[:, 1], in1=bc(CI), op=ALU.mult), [r_Braw, r] + r_srcC_all, [rX])
        dve(lambda e: e.tensor_tensor(out=Braw[:, 0], in0=Braw[:, 0], in1=bc(CI), op=ALU.mult), [r_Braw, r, r_BBL[0]],
            [rY])
        dve(lambda e: e.tensor_tensor(out=BB[:, 0], in0=BB[:, 0], in1=tmpX, op=ALU.subtract), [r_BBL[0], rX], [r_BBL[0]])
        dve(lambda e: e.tensor_tensor(out=BB[:, 1], in0=BB[:, 1], in1=Braw[:, 0], op=ALU.add), [r_BBL[1], rY], [r_BBL[1]])
        for h in range(2):
            for ri in range(2):
                for eo in range(2):
                    v_src = BB[64 * h:64 * h + 64, ri, :, :].rearrange("p (a q) c -> p a q c", q=4)[:, :, eo::2, :]
                    v_dst = srcB[64 * h:64 * h + 64, ri, eo, :, 16 * h:16 * h + 16].rearrange(
                        "p (a q) c -> p a q c", q=4)[:, :, eo::2, :]
                    dve(lambda e, v_dst=v_dst, v_src=v_src: e.tensor_copy(out=v_dst, in_=v_src),
                        [r_BBL[ri]], [r_srcBL[ri][eo]])
        for a in range(4):
            for ri in range(2):
                for eo, Bt in ((0, BtE), (1, BtO)):
                    pbt = 5 + (a * 4 + ri * 2 + eo) % 2
                    pe(lambda e, a=a, ri=ri, eo=eo, pbt=pbt: e.transpose(
                        PB[pbt][:, 0:128], srcB[:, ri, eo, 4 * a:4 * a + 4, :].rearrange("p a c -> p (a c)"), identf[:, :]),
                        [r_srcBL[ri][eo], r_identf], [PR[pbt]])
                    act(lambda e, a=a, ri=ri, Bt=Bt, pbt=pbt: e.activation(out=Bt[:, a, ri, :], in_=PB[pbt][:, 0:128],
                                                                        func=AF.Copy),
                        [PR[pbt]], [r_BtL[(a, ri, eo)]])
        for ot in range(4):
            for ri in range(2):
                m0 = evod[:, 0:1] if ri == 0 else evod[:, 2:3]
                m1 = evod[:, 1:2] if ri == 0 else evod[:, 3:4]
                dve(lambda e, ot=ot, ri=ri, m0=m0: e.tensor_scalar(out=srcC[:, ri, 0:64], in0=Craw[:, ri, ot, :],
                                                                   scalar1=m0, scalar2=None, op0=ALU.mult),
                    [r_Craw, r_misc, rX], [r_srcCL[ri][0]])
                dve(lambda e, ot=ot, ri=ri, m1=m1: e.tensor_scalar(out=srcC[:, ri, 64:128], in0=Craw[:, ri, ot, :],
                                                                   scalar1=m1, scalar2=None, op0=ALU.mult),
                    [r_Craw, r_misc, rX], [r_srcCL[ri][1]])
                pbt = 5 + (ot * 2 + ri) % 2
                pe(lambda e, ri=ri, pbt=pbt: e.transpose(PB[pbt][:, 0:128], srcC[:, ri, :], identf[:, :]),
                   r_srcCL[ri] + [r_identf], [PR[pbt]])
                for j in range(4):
                    act(lambda e, ot=ot, ri=ri, j=j, pbt=pbt: e.activation(out=Ct[:, 4 * ot + j, ri, 32 * j:32 * j + 32],
                                                                           in_=PB[pbt][:, 32 * j:32 * j + 32],
                                                                           func=AF.Copy),
                        [PR[pbt]], [r_CtL[(4 * ot + j, ri)]])
        rt_ = r_s5t
        r_cosT, r_sinT = res("cosTw"), res("sinTw")
        kb.alias([r_cosT, r_sinT], [rt_])
        dve(lambda e: e.tensor_copy(out=cosT[:, :, 0:1], in_=CS.unsqueeze(2)), [r, rt_], [r_cosT])
        dve(lambda e: e.tensor_copy(out=sinT[:, :, 0:1], in_=SN.unsqueeze(2)), [r, rt_], [r_sinT])
        m = 1
        while m < 128:
            parts = [(0, 16)] if 16 * m <= 512 else [(i * (512 // m), (i + 1) * (512 // m)) for i in range(16 * m // 512)]
            for (s0, s1) in parts:
                ns = s1 - s0
                c_lo, s_lo = cosT[:, s0:s1, 0:m], sinT[:, s0:s1, 0:m]
                cmb = cosT[:, s0:s1, m - 1:m].to_broadcast([128, ns, m])
                smb = sinT[:, s0:s1, m - 1:m].to_broadcast([128, ns, m])
                f1, f2, f3, f4 = (tq[i][:, :, :].rearrange("p a t -> p (a t)")[:, 0:ns * m].rearrange(
                    "p (a t) -> p a t", t=m) for i in range(4))
                dve(lambda e, c_lo=c_lo, cmb=cmb, f1=f1: e.tensor_tensor(out=f1, in0=c_lo, in1=cmb, op=ALU.mult),
                    [r_cosT], [r_tq[0]])
                dve(lambda e, s_lo=s_lo, smb=smb, f2=f2: e.tensor_tensor(out=f2, in0=s_lo, in1=smb, op=ALU.mult),
                    [r_sinT], [r_tq[1]])
                dve(lambda e, s_lo=s_lo, cmb=cmb, f3=f3: e.tensor_tensor(out=f3, in0=s_lo, in1=cmb, op=ALU.mult),
                    [r_sinT, r_cosT], [r_tq[2]])
                dve(lambda e, c_lo=c_lo, smb=smb, f4=f4: e.tensor_tensor(out=f4, in0=c_lo, in1=smb, op=ALU.mult),
                    [r_cosT, r_sinT], [r_tq[3]])
                dve(lambda e, s0=s0, s1=s1, f1=f1, f2=f2, m=m: e.tensor_tensor(out=cosT[:, s0:s1, m:2 * m], in0=f1,
                                                                              in1=f2, op=ALU.subtract),
                    [r_tq[0], r_tq[1]], [r_cosT])
                dve(lambda e, s0=s0, s1=s1, f3=f3, f4=f4, m=m: e.tensor_tensor(out=sinT[:, s0:s1, m:2 * m], in0=f3,
                                                                              in1=f4, op=ALU.add),
                    [r_tq[2], r_tq[3]], [r_sinT])
            m *= 2
        kb.alias([rt_], [r_cosT, r_sinT])
        smflat = s5sm[:, 0:17, :].rearrange("p a c -> p (a c)")
        if cached:
            kb.defer = None
            kb.dma("sp", smflat, scr_sm[l], r_scrL, [r_scr[l]], [r_s5sm])
            kb.dma("sp", BtE[:, :, :, :].rearrange("p a r c -> p (a r c)"), scr_bt[l, 0], r_scrL, [r_scr[l]],
                   [r_BtL[k_] for k_ in r_BtL if k_[2] == 0])
            kb.dma("sp", BtO[:, :, :, :].rearrange("p a r c -> p (a r c)"), scr_bt[l, 1], r_scrL, [r_scr[l]],
                   [r_BtL[k_] for k_ in r_BtL if k_[2] == 1])
            kb.dma("sp", arena2[:, 0:4096], scr_cs[l], r_scrL, [r_scr[l]], [rt_])
            kb.dma("sp", arena1[:, 4 * TG1:4 * TG1 + 4096], scr_ct[l], r_scrL, [r_scr[l]], r_Ct_all)
        if gi == 0:
            for ri, srcst in ((0, st_re), (1, st_im)):
                kb.dma("sp", s5stg[:, :], srcst[l], r_stg[0], [], r_stg)

                def ft(e):
                    ins = None
                    for st in range(16):
                        ins = e.transpose(PB[5][:, st * 16:(st + 1) * 16], s5stg[:, st * 128:(st + 1) * 128],
                                          identf[0:16, 0:16])
                    return ins
                pe(ft, r_stg + [r_identf], [PR[5]])
                dve(lambda e, ri=ri: e.tensor_copy(out=X0[:, ri, :, :].rearrange("p a b -> p (a b)"), in_=PB[5][:, 0:256]),
                    [PR[5]], [r_X0])
                dve(lambda e, ri=ri: e.tensor_tensor(out=inj[:, ri], in0=X0[:, ri], in1=bc(MAG), op=ALU.mult),
                    [r_X0, r], [r_X0])
            kb.alias([r_coef0, r_yfp], [r_Braw, r_Craw] + r_BBL + r_srcC_all)
            dve(lambda e: e.tensor_copy(out=coef0[:, :, :], in_=MAG.unsqueeze(2).to_broadcast([128, 16, 80])),
                [r], [r_coef0])
            dve(lambda e: e.memset(coef0[:, :, 0:64].rearrange("p a (s j) -> p a s j", j=4)[:, :, :, 0:1], 0.0),
                [r_coef0], [r_coef0])
            dve(lambda e: e.memset(coef0[:, :, 64:65], 0.0), [r_coef0], [r_coef0])

        if gi == 1:
            kb.alias([r_coef0, r_yfp], [r_Braw, r_Craw] + r_BBL + r_srcC_all)
        else:
            kb.dma("sp", scr_sm[l], smflat, r_scrS, [r_s5sm], [r_scr[l]])
            kb.dma("sp", scr_bt[l, 0], BtE[:, :, :, :].rearrange("p a r c -> p (a r c)"), r_scrS, r_Bt_all, [r_scr[l]])
            kb.dma("sp", scr_bt[l, 1], BtO[:, :, :, :].rearrange("p a r c -> p (a r c)"), r_scrS, r_Bt_all, [r_scr[l]])
            kb.dma("sp", scr_cs[l], arena2[:, 0:4096], r_scrS, [rt_], [r_scr[l]])
            kb.dma("sp", scr_ct[l], arena1[:, 4 * TG1:4 * TG1 + 4096], r_scrS, r_Ct_all, [r_scr[l]])
        kb.alias(r_rin + r_rr + [x for a_ in r_rrs[0] for x in a_], r_srcB_all)
        if gi == 0:
            sblocks = [(0, 80)] + [(80 + 256 * i, 336 + 256 * i) for i in range(4)]
        else:
            sblocks = [(256 * i, 256 * i + 256) for i in range(4)]
        GLs = WS.take("A")
        wgl = ringA[:, GLs, 0:2048].rearrange("p (k c) -> p k c", c=512)
        fcount = [0]
        pend_post = []
        for sbi, (c0, c1) in enumerate(sblocks):
            ncol = c1 - c0
            ybuf = yfp2[sbi % 2]
            r_ybuf = r_yfp2[sbi % 2]
            ti = [i for i, t in enumerate(tiles) if t[0] <= c0 and c1 <= t[1]][0]
            block0 = (gi == 0 and c0 == 0)
            frames = [(0, 80)] if block0 else [(f, f + 128) for f in range(c0, c1, 128)]
            nfr = len(frames)

            def emit_B(qd, fi):
                f0, f1 = frames[fi]
                L = f1 - f0
                fb_ = (fbase + qd * nfr + fi) % 2
                psF = PBall[:, fb_ * 1024:(fb_ + 1) * 1024].rearrange("p (s r c) -> p s r c", s=4, r=2)

                def fn(e):
                    ins = None
                    for s4 in range(4):
                        pb_ = 64 * (s4 // 2)
                        Bt = BtE if s4 % 2 == 0 else BtO
                        for ri in range(2):
                            ins = e.matmul(psF[:, s4, ri, 0:L], lhsT=Bt[pb_:pb_ + 64, qd, ri, :],
                                           rhs=uT[pb_:pb_ + 64, qd, f0:f1], start=True, stop=True)
                    return ins
                pe(fn, r_Bt_all + [r_u[qd]], [PR[2 * fb_], PR[2 * fb_ + 1]])
            pendC = []
            pend_carry = []

            def emit_C(qd):
                xq = xTb2[qd % 2]
                r_xq = r_xTb2[qd % 2]
                pby = 4
                mm_group(PB[pby][:, 0:ncol], [(Ct[:, 4 * qd + s4, ri, :], xq[:, s4, ri, 0:ncol])
                                              for s4 in range(4) for ri in range(2)], r_Ct_all + [r_xq], [PR[pby]])
                dcol = R_S5D + 4 * l + qd
                dve(lambda e, qd=qd, dcol=dcol: e.scalar_tensor_tensor(out=ybuf[:, qd, 0:ncol], in0=uT[:, qd, c0:c1],
                                                                       scalar=PV[:, dcol:dcol + 1], op0=ALU.mult,
                                                                       in1=PB[pby][:, 0:ncol], op1=ALU.add),
                    [r_u[qd], r_PV, PR[pby]], [r_ybuf])
            fbase = fcount[0]
            for fi in range(nfr):
                emit_B(0, fi)
            for qd in range(4):
                xq = xTb2[qd % 2]
                r_xq = r_xTb2[qd % 2]
                for fi, (f0, f1) in enumerate(frames):
                    L = f1 - f0
                    o0 = f0 - c0
                    stq = slice(4 * qd, 4 * qd + 4)
                    fb_ = fcount[0] % 2
                    fcount[0] += 1
                    psF = PBall[:, fb_ * 1024:(fb_ + 1) * 1024].rearrange("p (s r c) -> p s r c", s=4, r=2)
                    PRF = [PR[2 * fb_], PR[2 * fb_ + 1]]
                    rrb = rr2[fb_]
                    r_rrb = [r_rrs[fb_][0], r_rrs[fb_][1]]
                    if block0:
                        segs = [(0, 64, 0), (64, 80, 0)]
                    else:
                        segs = [(0, L, 0)]
                    for (a0, a1, _) in segs:
                        if block0 and a0 == 0:
                            tcv = cosT[:, stq, 0:4].unsqueeze(2).to_broadcast([128, 4, 16, 4])
                            tsv = sinT[:, stq, 0:4].unsqueeze(2).to_broadcast([128, 4, 16, 4])
                            shp = lambda v: v.rearrange("p a (s j) -> p a s j", j=4)
                        else:
                            tcv = cosT[:, stq, 0:a1 - a0]
                            tsv = sinT[:, stq, 0:a1 - a0]
                            shp = lambda v: v
                        bur = shp(psF[:, :, 0, a0:a1])
                        bui = shp(psF[:, :, 1, a0:a1])
                        t1, t2, t3, t4 = (shp(tq[i][:, :, a0:a1]) for i in range(4))
                        dve(lambda e, t1=t1, bur=bur, tcv=tcv: e.tensor_tensor(out=t1, in0=bur, in1=tcv, op=ALU.mult),
                            PRF + [rt_], [r_tq[0]])
                        dve(lambda e, t2=t2, bui=bui, tsv=tsv: e.tensor_tensor(out=t2, in0=bui, in1=tsv, op=ALU.mult),
                            PRF + [rt_], [r_tq[1]])
                        dve(lambda e, t3=t3, bui=bui, tcv=tcv: e.tensor_tensor(out=t3, in0=bui, in1=tcv, op=ALU.mult),
                            PRF + [rt_], [r_tq[2]])
                        dve(lambda e, t4=t4, bur=bur, tsv=tsv: e.tensor_tensor(out=t4, in0=bur, in1=tsv, op=ALU.mult),
                            PRF + [rt_], [r_tq[3]])
                    for cf in pend_carry:
                        cf(0)
                    if qd < 3:
                        emit_B(qd + 1, fi)
                    if fi == nfr - 1 and pendC:
                        emit_C(pendC.pop(0))
                    dve(lambda e: e.tensor_tensor(out=rin[0][:, :, 0:L], in0=tq[0][:, :, 0:L], in1=tq[1][:, :, 0:L],
                                                  op=ALU.add), [r_tq[0], r_tq[1]], [r_rin[0]])
                    dve(lambda e: e.tensor_tensor(out=rin[1][:, :, 0:L], in0=tq[2][:, :, 0:L], in1=tq[3][:, :, 0:L],
                                                  op=ALU.subtract), [r_tq[2], r_tq[3]], [r_rin[1]])
                    if block0:
                        for ri in range(2):
                            v = rin[ri][:, :, 0:64].rearrange("p a (s j) -> p a s j", j=4)[:, :, :, 0:1]
                            dve(lambda e, v=v, ri=ri: e.tensor_tensor(out=v, in0=v, in1=inj[:, ri, stq, :].unsqueeze(3),
                                                                      op=ALU.add), [r_rin[ri], r_X0], [r_rin[ri]])
                    while pend_carry:
                        pend_carry.pop(0)(1)
                    for s4 in range(4):
                        st = 4 * qd + s4
                        for ri in range(2):
                            if block0:
                                d0 = coef0[:, st, 0:L]
                                init = 0.0
                                rds = [r_rin[ri], r_coef0]
                            else:
                                d0 = MAG[:, st:st + 1].to_broadcast([128, L])
                                init = car[:, l, ri, st:st + 1]
                                rds = [r_rin[ri], r, r_carq2[qd][ri]]
                            dve(lambda e, s4=s4, ri=ri, d0=d0, init=init, rrb=rrb: e.tensor_tensor_scan(
                                out=rrb[ri][:, s4, 0:L], data0=d0, data1=rin[ri][:, s4, 0:L], initial=init,
                                op0=ALU.mult, op1=ALU.add), rds, [r_rrb[ri][s4]])
                    for _ in range(min(POST_DRAIN, len(pend_post))):
                        pend_post.pop(0)()
                    def carry_fn(phase, l=l, stq=stq, L=L, rrb=rrb, r_rrb=r_rrb, block0=block0, qd=qd,
                                 csb=(fcount[0] % 2) * 4):
                        cs = [s5sm[:, 17 + (csb + i) % 7, 4 * ((csb + i) // 7):4 * ((csb + i) // 7) + 4].unsqueeze(2)
                              for i in range(4)]
                        jl = (15 if block0 else L - 1)
                        tcl, tsl = cosT[:, stq, jl:jl + 1], sinT[:, stq, jl:jl + 1]
                        rrl, ril = rrb[0][:, :, L - 1:L], rrb[1][:, :, L - 1:L]
                        r_c4 = [res("s5cs%d" % (csb + i)) for i in range(4)]
                        if phase == 1:
                            dve(lambda e: e.tensor_tensor(out=car[:, l, 0, stq].unsqueeze(2), in0=cs[0], in1=cs[1],
                                                          op=ALU.subtract), [r_c4[0], r_c4[1]], [r_carq2[qd][0]])
                            dve(lambda e: e.tensor_tensor(out=car[:, l, 1, stq].unsqueeze(2), in0=cs[2], in1=cs[3],
                                                          op=ALU.add), [r_c4[2], r_c4[3]], [r_carq2[qd][1]])
                            return
                        dve(lambda e, rrl=rrl, tcl=tcl: e.tensor_tensor(out=cs[0], in0=rrl, in1=tcl, op=ALU.mult),
                            r_rrb[0] + [rt_], [r_c4[0]])
                        dve(lambda e, ril=ril, tsl=tsl: e.tensor_tensor(out=cs[1], in0=ril, in1=tsl, op=ALU.mult),
                            r_rrb[1] + [rt_], [r_c4[1]])
                        dve(lambda e, ril=ril, tcl=tcl: e.tensor_tensor(out=cs[2], in0=ril, in1=tcl, op=ALU.mult),
                            r_rrb[1] + [rt_], [r_c4[2]])
                        dve(lambda e, rrl=rrl, tsl=tsl: e.tensor_tensor(out=cs[3], in0=rrl, in1=tsl, op=ALU.mult),
                            r_rrb[0] + [rt_], [r_c4[3]])

                    pend_carry.append(carry_fn)
                    for (a0, a1, _) in segs:
                        if block0 and a0 == 0:
                            tcv = cosT[:, stq, 0:4].unsqueeze(2).to_broadcast([128, 4, 16, 4])
                            tsv = sinT[:, stq, 0:4].unsqueeze(2).to_broadcast([128, 4, 16, 4])
                            shp = lambda v: v.rearrange("p a (s j) -> p a s j", j=4)
                        else:
                            tcv = cosT[:, stq, 0:a1 - a0]
                            tsv = sinT[:, stq, 0:a1 - a0]
                            shp = lambda v: v
                        rrv, riv = shp(rrb[0][:, :, a0:a1]), shp(rrb[1][:, :, a0:a1])
                        p0, p1 = shp(pq[0][:, :, a0:a1]), shp(pq[1][:, :, a0:a1])
                        xr = shp(xq[:, :, 0, o0 + a0:o0 + a1])
                        xi = shp(xq[:, :, 1, o0 + a0:o0 + a1])
                        pool(lambda e, p0=p0, rrv=rrv, tcv=tcv: e.tensor_tensor(out=p0, in0=rrv, in1=tcv, op=ALU.mult),
                             r_rrb[0] + [rt_], [r_pq[0]])
                        pool(lambda e, p1=p1, riv=riv, tsv=tsv: e.tensor_tensor(out=p1, in0=riv, in1=tsv, op=ALU.mult),
                             r_rrb[1] + [rt_], [r_pq[1]])
                        pool(lambda e, p0=p0, p1=p1, xr=xr: e.tensor_tensor(out=xr, in0=p0, in1=p1, op=ALU.subtract),
                             [r_pq[0], r_pq[1]], [r_xq])
                        if block0 and a0 == 0:
                            pool(lambda e, p0=p0, p1=p1: e.tensor_tensor(out=X1[:, 0, stq, :].unsqueeze(3),
                                                                         in0=p0[:, :, :, 3:4], in1=p1[:, :, :, 3:4],
                                                                         op=ALU.subtract), [r_pq[0], r_pq[1]], [r_X0])
                        pool(lambda e, p0=p0, riv=riv, tcv=tcv: e.tensor_tensor(out=p0, in0=riv, in1=tcv, op=ALU.mult),
                             r_rrb[1] + [rt_], [r_pq[0]])
                        pool(lambda e, p1=p1, rrv=rrv, tsv=tsv: e.tensor_tensor(out=p1, in0=rrv, in1=tsv, op=ALU.mult),
                             r_rrb[0] + [rt_], [r_pq[1]])
                        pool(lambda e, p0=p0, p1=p1, xi=xi: e.tensor_tensor(out=xi, in0=p0, in1=p1, op=ALU.add),
                             [r_pq[0], r_pq[1]], [r_xq])
                        if block0 and a0 == 0:
                            pool(lambda e, p0=p0, p1=p1: e.tensor_tensor(out=X1[:, 1, stq, :].unsqueeze(3),
                                                                         in0=p0[:, :, :, 3:4], in1=p1[:, :, :, 3:4],
                                                                         op=ALU.add), [r_pq[0], r_pq[1]], [r_X0])
                pendC.append(qd)
            while pend_carry:
                cf = pend_carry.pop(0)
                cf(0)
                cf(1)
            while pendC:
                emit_C(pendC.pop(0))
            def emit_post(c0=c0, c1=c1, ncol=ncol, ti=ti, ybuf=ybuf, r_ybuf=r_ybuf):
                w1 = w1f[:, 0:4 * ncol].rearrange("p (a t) -> p a t", t=ncol)
                w2 = w2f[:, 0:4 * ncol].rearrange("p (a t) -> p a t", t=ncol)
                yv = ybuf[:, :, 0:ncol]
                pool(lambda e: e.tensor_tensor(out=w1, in0=yv, in1=yv, op=ALU.mult), [r_ybuf], [r_w1])
                pool(lambda e: e.tensor_scalar(out=w1, in0=w1, scalar1=0.044715, scalar2=1.0, op0=ALU.mult, op1=ALU.add),
                     [r_w1], [r_w1])
                pool(lambda e: e.tensor_tensor(out=w1, in0=w1, in1=yv, op=ALU.mult), [r_w1, r_ybuf],
                     [r_w1])
                act(lambda e: e.activation(out=w2, in_=w1, func=AF.Sigmoid, scale=2.0 * math.sqrt(2.0 / math.pi)),
                    [r_w1], [r_w2])
                dve(lambda e: e.tensor_tensor(out=yv, in0=yv, in1=w2, op=ALU.mult), [r_ybuf, r_w2], [r_ybuf])
                act(lambda e: e.activation(out=ygb[:, :, 0:ncol], in_=yv, func=AF.Copy), [r_ybuf], [r_ygb])
                for ot in range(4):
                    pbg = 5
                    mm_group(PB[pbg][:, 0:ncol], [(wgl[:, k, ot * 128:(ot + 1) * 128], ygb[:, k, 0:ncol]) for k in range(4)],
                             [r_ringA[GLs], r_ygb], [PR[pbg]])
                    bcol = R_BGLU + 4 * l + ot
                    act(lambda e, ot=ot, bcol=bcol: e.activation(out=w2[:, ot, :], in_=PB[pbg][:, 0:ncol], func=AF.Sigmoid,
                                                                 bias=PV[:, bcol:bcol + 1]), [PR[pbg], r_PV],
                        [r_w2])
                dve(lambda e: e.tensor_tensor(out=yv, in0=yv, in1=w2, op=ALU.mult), [r_ybuf, r_w2], [r_ybuf])
                for ot in range(4):
                    s = sqi[0] % 3
                    sqi[0] += 1
                    act(lambda e, ot=ot, s=s: e.activation(out=sq[:, s, 0:ncol], in_=ybuf[:, ot, 0:ncol], func=AF.Square),
                        [r_ybuf], [r_sq[s]])
                    pe(lambda e, ot=ot, s=s: e.matmul(PB[6][:, 0:ncol], lhsT=onesb[:], rhs=sq[:, s, 0:ncol],
                                                      start=(ot == 0), stop=(ot == 3)), [r_sq[s], r_ones], [PR[6]])
                j = sqi[0] % 2
                act(lambda e: e.activation(out=rt[:, j, 0:ncol], in_=PB[6][:, 0:ncol], func=AF.Ln, bias=epsT[:, 0:1],
                                           scale=1.0 / 512), [PR[6], r_misc], [r_rt[j]])
                act(lambda e: e.activation(out=rt[:, j, 0:ncol], in_=rt[:, j, 0:ncol], func=AF.Exp, scale=-0.5),
                    [r_rt[j]], [r_rt[j]])
                for ot in range(4):
                    gcol = R_S5N + 4 * l + ot
                    dve(lambda e, ot=ot, gcol=gcol: e.scalar_tensor_tensor(out=xn[:, 4 + ot, c0:c1], in0=ybuf[:, ot, 0:ncol],
                                                                           scalar=PV[:, gcol:gcol + 1], op0=ALU.mult,
                                                                           in1=rt[:, j, 0:ncol], op1=ALU.mult),
                        [r_ybuf, r_rt[j], r_PV], [r_xn[4 + ot][ti]])

            while pend_post:
                pend_post.pop(0)()
            kb.defer = []
            emit_post()
            pend_post.extend(kb.defer)
            kb.defer = None
        while pend_post:
            pend_post.pop(0)()
        kb.alias(r_sg, [r_yfp2[1]])
        WS.done("A", 1)
        if gi == 0:
            for ri, dst in ((0, o_s5s_re), (1, o_s5s_im)):
                for q4 in range(4):
                    def ft(e, ri=ri, q4=q4):
                        ins = None
                        for s4 in range(4):
                            st = 4 * q4 + s4
                            ins = e.transpose(PB[5][0:16, s4 * 128:(s4 + 1) * 128], X1[:, ri, st, :], identf[:, :])
                        return ins
                    pe(ft, [r_X0, r_identf], [PR[5]])
                    dve(lambda e, q4=q4: e.tensor_copy(out=s5stg[:, q4 * 512:(q4 + 1) * 512],
                                                       in_=PB[5][0:16, 0:512]), [PR[5]], r_stg)
                kb.dma("sp", dst[l], s5stg[:, :], r_stg[0], r_stg, [])
        else:
            for ri, dst in ((0, o_s5p_re), (1, o_s5p_im)):
                pe(lambda e, ri=ri: e.transpose(PB[5][0:16, 0:128], car[:, l, ri, :], identf[:, :]),
                   [x for a_ in r_carq2 for x in a_] + [r_identf], [PR[5]])
                dve(lambda e: e.tensor_copy(out=s5stg[:, 0:128], in_=PB[5][0:16, 0:128]), [PR[5]], r_stg)
                kb.dma("sp", dst[l], s5stg[:, 0:128], r_stg[0], r_stg, [])

    for gi, G in enumerate(groups):
        load_x(G)
        for l in range(depth):
            rmsnorm_to_xn(G, R_NF1 + 8 * l)
            ffn(G, l)
            if mixer:
                mixer_fn(G, l, gi)
            rmsnorm_to_xn(G, R_NF2 + 8 * l)
            ffn(G, l)
        store_y(G)

    for name, E in kb.eng.items():
        if E["obj"] is None and E["count"] > 0:
            nc.sync.wait_ge(E["sem"], E["count"])
    for name in ("pe", "act", "dve", "pool"):
        E = kb.eng[name]
        if E["count"] > 0:
            nc.sync.wait_ge(E["sem"], E["count"])
    es.close()
    return nc


def _pack_params(inp):
    rows = []
    rows.append(inp["norm_ffn1"].reshape(32, 128))
    rows.append(inp["norm_mix"].reshape(32, 128))
    rows.append(inp["norm_ffn2"].reshape(32, 128))
    rows.append(inp["norm_final"].reshape(8, 128))
    rows.append(inp["hgrn_norm"].reshape(16, 128))
    rows.append(inp["s5_norm"].reshape(16, 128))
    rows.append(inp["s5_b_glu"].reshape(16, 128))
    rows.append(inp["lb_param"].reshape(16, 128))
    rows.append(inp["s5_d"].reshape(16, 128))
    pv = np.ascontiguousarray(np.concatenate(rows, axis=0).astype(np.float32))
    assert pv.shape == (NPROW, 128)
    s5 = []
    for l in range(DEPTH):
        s5.append(inp["s5_lambda_re"][l].reshape(16, 128))
        s5.append(inp["s5_lambda_im"][l].reshape(16, 128))
        s5.append(np.repeat(inp["s5_log_dt"][l], 64).reshape(16, 128))
    s5 = np.ascontiguousarray(np.concatenate(s5, axis=0).astype(np.float32))
    return pv, s5


_NC_CACHE = {}


def make_in_maps(inp, cores):
    inp = {k: np.asarray(v) for k, v in inp.items()}
    pv, s5 = _pack_params(inp)
    shared = dict(
        pvec=pv, s5pv=s5,
        ffn1_w_gate=inp["ffn1_w_gate"], ffn1_w_up=inp["ffn1_w_up"], ffn1_w_down=inp["ffn1_w_down"],
        ffn2_w_gate=inp["ffn2_w_gate"], ffn2_w_up=inp["ffn2_w_up"], ffn2_w_down=inp["ffn2_w_down"],
        w_in=inp["w_in"], w_out=inp["w_out"], s5_w_glu=inp["s5_w_glu"],
        s5_b_re=inp["s5_b_re"].reshape(DEPTH, 2048, 16), s5_b_im=inp["s5_b_im"].reshape(DEPTH, 2048, 16),
        s5_c_re=inp["s5_c_re"].reshape(DEPTH, 512, 64), s5_c_im=inp["s5_c_im"].reshape(DEPTH, 512, 64),
    )
    maps = []
    for c in cores:
        xs = inp["x_sample"][NSEQ * c:NSEQ * (c + 1)].reshape(64, D)
        xp = inp["x_prompt"][c]
        x = np.ascontiguousarray(np.concatenate([xs, inp["meta_tokens"], xp], axis=0).astype(np.float32))
        m = dict(shared)
        m["xin"] = x
        m["st_hg"] = np.ascontiguousarray(inp["state_hgrn"][:, NSEQ * c:NSEQ * (c + 1)])
        m["st_re"] = np.ascontiguousarray(inp["state_s5_re"][:, NSEQ * c:NSEQ * (c + 1)].reshape(DEPTH, NSEQ, 2048))
        m["st_im"] = np.ascontiguousarray(inp["state_s5_im"][:, NSEQ * c:NSEQ * (c + 1)].reshape(DEPTH, NSEQ, 2048))
        maps.append(m)
    return maps


def gather(results, ncores):
    B = ncores
    y_prompt = np.zeros((B, 2048, D), np.float32)
    y_sample = np.zeros((NSEQ * B, 4, D), np.float32)
    hgp = np.zeros((DEPTH, B, 4, 128, 128), np.float32)
    s5pr = np.zeros((DEPTH, B, 32, 64), np.float32)
    s5pi = np.zeros((DEPTH, B, 32, 64), np.float32)
    hgs = np.zeros((DEPTH, NSEQ * B, 4, 128, 128), np.float32)
    s5sr = np.zeros((DEPTH, NSEQ * B, 32, 64), np.float32)
    s5si = np.zeros((DEPTH, NSEQ * B, 32, 64), np.float32)
    for c, r in enumerate(results):
        y = np.asarray(r["yout"])
        y_sample[NSEQ * c:NSEQ * (c + 1)] = y[0:64].reshape(NSEQ, 4, D)
        y_prompt[c] = y[80:]
        hgp[:, c] = np.asarray(r["o_hgp"])
        s5pr[:, c] = np.asarray(r["o_s5p_re"]).reshape(DEPTH, 32, 64)
        s5pi[:, c] = np.asarray(r["o_s5p_im"]).reshape(DEPTH, 32, 64)
        hgs[:, NSEQ * c:NSEQ * (c + 1)] = np.asarray(r["o_hgs"])
        s5sr[:, NSEQ * c:NSEQ * (c + 1)] = np.asarray(r["o_s5s_re"]).reshape(DEPTH, NSEQ, 32, 64)
        s5si[:, NSEQ * c:NSEQ * (c + 1)] = np.asarray(r["o_s5s_im"]).reshape(DEPTH, NSEQ, 32, 64)
    return (y_prompt, y_sample, hgp, s5pr, s5pi, hgs, s5sr, s5si)


def kernel(**inputs):
    if "nc" not in _NC_CACHE:
        _NC_CACHE["nc"] = build_nc()
    nc = _NC_CACHE["nc"]
    maps = make_in_maps(inputs, list(range(NCORE)))
    res = run_bass_kernel_spmd(nc, maps, core_ids=list(range(NCORE)))
    return gather(res.results, NCORE)
```

```python
import math
import numpy as np
import concourse.bass as bass
import concourse.mybir as mybir
from concourse.bass_utils import run_bass_kernel_spmd

F32 = mybir.dt.float32
BF16 = mybir.dt.bfloat16
I32 = mybir.dt.int32
AF = mybir.ActivationFunctionType
ALU = mybir.AluOpType

D = 1024
FF = 2816
NHT = 22
DEPTH = 4
NCORE = 8
TG1 = 1104
TG2 = 1024
TALL = TG1 + TG2
EPS = 1e-6
NSEQ = 16
POST_DRAIN = 5
TWO_PI = 2.0 * math.pi

R_NF1, R_NMX, R_NF2, R_NFIN, R_HGN, R_S5N, R_BGLU, R_LBP, R_S5D = 0, 32, 64, 96, 104, 120, 136, 152, 168
NPROW = 184


class Res:
    __slots__ = ("name", "w", "r", "a", "dsem")

    def __init__(self, name, dsem=None):
        self.name = name
        self.w = None
        self.r = {}
        self.a = {}
        self.dsem = dsem


class KB:
    def __init__(self, nc):
        self.nc = nc
        self.eng = {}
        self.nosame = set()
        self.defer = None

    def add_eng(self, name, obj, sem):
        self.eng[name] = dict(obj=obj, sem=sem, count=0, seen={})

    def _deps(self, reads, writes):
        need = {}
        for r in reads:
            if r.w is not None:
                n, c = r.w
                if need.get(n, 0) < c:
                    need[n] = c
            for n, c in r.a.items():
                if need.get(n, 0) < c:
                    need[n] = c
        for w in writes:
            if w.w is not None:
                n, c = w.w
                if need.get(n, 0) < c:
                    need[n] = c
            for n, c in w.r.items():
                if need.get(n, 0) < c:
                    need[n] = c
            for n, c in w.a.items():
                if need.get(n, 0) < c:
                    need[n] = c
        return need

    def _wait(self, ename, need):
        E = self.eng[ename]
        for n, c in need.items():
            if n == ename and ename in self.nosame:
                continue
            if E["seen"].get(n, 0) >= c:
                continue
            E["obj"].wait_ge(self.eng[n]["sem"], c)
            E["seen"][n] = c

    def op(self, ename, fn, reads=(), writes=()):
        if self.defer is not None:
            self.defer.append(lambda: self.op(ename, fn, reads, writes))
            return
        self._wait(ename, self._deps(reads, writes))
        E = self.eng[ename]
        ins = fn(E["obj"])
        E["count"] += 1
        ins.then_inc(E["sem"], 1)
        c = E["count"]
        for r in reads:
            r.r[ename] = c
        for w in writes:
            w.w = (ename, c)
            w.r = {}
            w.a = {}

    def dma(self, qname, out, in_, sres, reads=(), writes=(), **kw):
        if self.defer is not None:
            self.defer.append(lambda: self.dma(qname, out, in_, sres, reads, writes, **kw))
            return
        self._wait(qname, self._deps(reads, writes))
        d = self.eng[sres.dsem]
        ins = self.eng[qname]["obj"].dma_start(out=out, in_=in_, **kw)
        d["count"] += 16
        ins.then_inc(d["sem"], 16)
        for r in reads:
            r.r[sres.dsem] = d["count"]
        for w in writes:
            w.w = (sres.dsem, d["count"])
            w.r = {}
            w.a = {}

    def alias(self, new, old):
        m = {}
        for o in old:
            if o.w is not None:
                n, c = o.w
                m[n] = max(m.get(n, 0), c)
            for n, c in o.r.items():
                m[n] = max(m.get(n, 0), c)
            for n, c in o.a.items():
                m[n] = max(m.get(n, 0), c)
        for r in new:
            r.w = None
            r.r = {}
            r.a = dict(m)

    def wait_all(self, ename, ress):
        need = {}
        for r in ress:
            if r.w is not None:
                n, c = r.w
                need[n] = max(need.get(n, 0), c)
            for n, c in r.r.items():
                need[n] = max(need.get(n, 0), c)
        self._wait(ename, need)


def build_nc(depth=DEPTH, mixer=True, hgrn=True, s5=True):
    nc = bass.Bass("TRN2", target_bir_lowering=False)
    import contextlib
    es = contextlib.ExitStack()

    def dram(name, shape, kind, dt=F32):
        return nc.dram_tensor(name, list(shape), dt, kind=kind).ap()

    IN, OUT = "ExternalInput", "ExternalOutput"
    xin = dram("xin", [TALL, D], IN)
    st_hg = dram("st_hg", [DEPTH, NSEQ, 4, 128, 128], IN)
    st_re = dram("st_re", [DEPTH, NSEQ, 2048], IN)
    st_im = dram("st_im", [DEPTH, NSEQ, 2048], IN)
    pvec = dram("pvec", [NPROW, 128], IN)
    s5pv = dram("s5pv", [DEPTH * 48, 128], IN)
    w_f1g = dram("ffn1_w_gate", [DEPTH, D, FF], IN)
    w_f1u = dram("ffn1_w_up", [DEPTH, D, FF], IN)
    w_f1d = dram("ffn1_w_down", [DEPTH, FF, D], IN)
    w_f2g = dram("ffn2_w_gate", [DEPTH, D, FF], IN)
    w_f2u = dram("ffn2_w_up", [DEPTH, D, FF], IN)
    w_f2d = dram("ffn2_w_down", [DEPTH, FF, D], IN)
    w_in = dram("w_in", [DEPTH, D, 2560], IN)
    w_out = dram("w_out", [DEPTH, D, D], IN)
    w_glu = dram("s5_w_glu", [DEPTH, 512, 512], IN)
    b_re = dram("s5_b_re", [DEPTH, 2048, 16], IN)
    b_im = dram("s5_b_im", [DEPTH, 2048, 16], IN)
    c_re = dram("s5_c_re", [DEPTH, 512, 64], IN)
    c_im = dram("s5_c_im", [DEPTH, 512, 64], IN)
    INT = "Internal"
    scr_cs = dram("scr_cs", [DEPTH, 128, 4096], INT)
    scr_bt = dram("scr_bt", [DEPTH, 2, 128, 1024], INT, BF16)
    scr_ct = dram("scr_ct", [DEPTH, 128, 4096], INT, BF16)
    scr_sm = dram("scr_sm", [DEPTH, 128, 272], INT)
    yout = dram("yout", [TALL, D], OUT)
    o_hgp = dram("o_hgp", [DEPTH, 4, 128, 128], OUT)
    o_s5p_re = dram("o_s5p_re", [DEPTH, 16, 128], OUT)
    o_s5p_im = dram("o_s5p_im", [DEPTH, 16, 128], OUT)
    o_hgs = dram("o_hgs", [DEPTH, NSEQ, 4, 128, 128], OUT)
    o_s5s_re = dram("o_s5s_re", [DEPTH, NSEQ, 2048], OUT)
    o_s5s_im = dram("o_s5s_im", [DEPTH, NSEQ, 2048], OUT)

    def sb(name, shape, dt=F32):
        return es.enter_context(nc.sbuf_tensor(name, list(shape), dt))

    def ps(name, shape, dt=F32):
        return es.enter_context(nc.psum_tensor(name, list(shape), dt))

    kb = KB(nc)
    nsem = [0]

    def newsem(name):
        nsem[0] += 1
        return es.enter_context(nc.semaphore(name))

    kb.add_eng("pe", nc.tensor, newsem("s_pe"))
    kb.add_eng("act", nc.scalar, newsem("s_act"))
    kb.add_eng("dve", nc.vector, newsem("s_dve"))
    kb.add_eng("pool", nc.gpsimd, newsem("s_pool"))
    kb.add_eng("sp", nc.sync, newsem("s_sp"))
    kb.nosame.add("pe")
    kb.nosame.add("sp")

    def dres(name):
        sname = "d_" + name
        kb.add_eng(sname, None, newsem(sname))
        return Res(name, dsem=sname)

    hT = sb("hT", [128, 8, TG1])
    xn = sb("xn", [128, 8, TG1], BF16)
    Sst = sb("Sst", [128, DEPTH, 4, 128])
    car = sb("car", [128, DEPTH, 2, 16])
    NA1 = 11 * TG1 + 11 * 1024
    arena1 = sb("arena1", [128, NA1], BF16)
    NA2 = 10496 + 1024
    arena2 = sb("arena2", [128, NA2])
    NRA = 4
    ringA = sb("ringA", [128, NRA, 2048], BF16)
    stg = sb("stg", [128, 2, 1024])
    PV = sb("PV", [128, NPROW])
    S5P = sb("S5P", [128, DEPTH * 48])
    identf = sb("identf", [128, 128])
    identb = sb("identb", [128, 128], BF16)
    onesb = sb("onesb", [128, 128], BF16)
    maskP = sb("maskP", [128, 128])
    mask0 = sb("mask0", [128, 128])
    Eseq = sb("Eseq", [16, 64], BF16)
    rowm = sb("rowm", [64, 16])
    epsT = sb("epsT", [128, 1])
    lbT = sb("lbT", [128, DEPTH, 4, 2])
    sq = sb("sq", [128, 3, 512], BF16)
    rt = sb("rt", [128, 2, 512])
    sgt = sb("sgt", [128, 2, 512])
    BtE = sb("BtE", [128, 4, 2, 128], BF16)
    BtO = sb("BtO", [128, 4, 2, 128], BF16)
    s5sm = sb("s5sm", [128, 27, 16])
    s5i = sb("s5i", [128, 2, 16], I32)
    evod = sb("evod", [128, 4])
    X0 = sb("X0", [128, 2, 16, NSEQ])
    inj = X0
    X1 = X0
    hgst = sb("hgst", [128, 4, 128])
    hgout = sb("hgout", [128, 3, 128])
    Sd = sb("Sd", [128, 4, 128])
    Sdb = sb("Sdb", [128, 4, 128], BF16)
    ktok = sb("ktok", [128, 4, 128], BF16)
    kms = sb("kms", [64, 4, 128], BF16)
    Am = sb("Am", [128, 4, 128], BF16)
    eb = sb("eb", [128, 4, 34])

    PBall = ps("pball", [128, 7 * 512])
    PB = [PBall[:, 512 * i:512 * (i + 1)] for i in range(7)]
    psB4 = PBall[:, 0:2048].rearrange("p (s r c) -> p s r c", s=4, r=2)
    PBb = ps("pbb", [128, 1024], BF16)
    PR = [Res("pb%d" % i) for i in range(7)]
    PRb = Res("pbb")

    hid = arena1[:, 0:11 * TG1].rearrange("p (a t) -> p a t", t=TG1)
    wdh = arena1[:, 11 * TG1:NA1].rearrange("p (a t) -> p a t", t=1024)
    qT = arena1[:, 0:4 * TG1].rearrange("p (a t) -> p a t", t=TG1)
    kT = arena1[:, 4 * TG1:8 * TG1].rearrange("p (a t) -> p a t", t=TG1)
    gate = arena1[:, 8 * TG1:12 * TG1].rearrange("p (a t) -> p a t", t=TG1)
    uT = arena1[:, 12 * TG1:16 * TG1].rearrange("p (a t) -> p a t", t=TG1)
    Vtok = arena1[:, 16 * TG1:16 * TG1 + 9 * 512].rearrange("p (a t) -> p a t", t=512)
    assert 16 * TG1 + 9 * 512 <= NA1
    xTb = arena1[:, 0:2048].rearrange("p (a r t) -> p a r t", r=2, t=256)
    ygb = arena1[:, 2048:3072].rearrange("p (a t) -> p a t", t=256)
    Ct = arena1[:, 4 * TG1:4 * TG1 + 4096].rearrange("p (a r t) -> p a r t", r=2, t=128)
    w1f = arena1[:, 8 * TG1:8 * TG1 + 2048].bitcast(F32)
    rrB = [arena1[:, 8 * TG1 + 2048 + 1024 * i:8 * TG1 + 2048 + 1024 * (i + 1)].bitcast(F32).rearrange(
        "p (a t) -> p a t", t=128) for i in range(2)]
    w2f = arena1[:, 16 * TG1:16 * TG1 + 2048].bitcast(F32)
    xTbB = arena1[:, 16 * TG1 + 2048:16 * TG1 + 4096].rearrange("p (a r t) -> p a r t", r=2, t=256)
    tmpA = arena2[:, 0:TG1]
    tmpB = arena2[:, TG1:2 * TG1]
    tmpC = arena2[:, 2 * TG1:3 * TG1]
    tmpD = arena2[:, 3 * TG1:4 * TG1]
    rmask = arena2[:, 8 * TG1:9 * TG1]
    o = 0
    cosT = arena2[:, o:o + 2048].rearrange("p (a t) -> p a t", t=128); o += 2048
    sinT = arena2[:, o:o + 2048].rearrange("p (a t) -> p a t", t=128); o += 2048
    tq = [arena2[:, o + i * 512:o + (i + 1) * 512].rearrange("p (a t) -> p a t", t=128) for i in range(4)]; o += 2048
    srcB = arena2[:, o:o + 2048].rearrange("p (r e a c) -> p r e a c", r=2, e=2, c=32)
    rin = [arena2[:, o + i * 512:o + (i + 1) * 512].rearrange("p (a t) -> p a t", t=128) for i in range(2)]; o += 1024
    rr = [arena2[:, o + i * 512:o + (i + 1) * 512].rearrange("p (a t) -> p a t", t=128) for i in range(2)]
    rrP0 = arena2[:, o:o + 1024].rearrange("p (r a t) -> p r a t", r=2, t=128); o += 1024
    Braw = arena2[:, o:o + 512].rearrange("p (r a c) -> p r a c", r=2, c=16)
    Craw = arena2[:, o + 512:o + 1024].rearrange("p (r a c) -> p r a c", r=2, c=64)
    BB = arena2[:, o + 1024:o + 1536].rearrange("p (r a c) -> p r a c", r=2, c=16)
    srcC = arena2[:, o + 1536:o + 1792].rearrange("p (r c) -> p r c", r=2)
    coef0 = arena2[:, o:o + 16 * 80].rearrange("p (a t) -> p a t", t=80); o += 1280
    yfp = arena2[:, o:o + 1024].rearrange("p (a t) -> p a t", t=256); o += 1024
    pq = [arena2[:, o + i * 512:o + (i + 1) * 512].rearrange("p (a t) -> p a t", t=128) for i in range(2)]; o += 1024
    assert o <= NA2
    rr2 = [rr, rrB]
    rrPB = arena1[:, 8 * TG1 + 2048:8 * TG1 + 4096].bitcast(F32).rearrange("p (r a t) -> p r a t", r=2, t=128)
    rrP2 = [rrP0, rrPB]
    yfp2 = [yfp, sgt[:, :, :].rearrange("p a t -> p (a t)").rearrange("p (a t) -> p a t", t=256)]
    xTb2 = [xTb, xTbB]
    s5stg = stg[0:16, :, :].rearrange("p a t -> p (a t)")

    R = {}

    def res(name):
        if name not in R:
            R[name] = Res(name)
        return R[name]

    r_h = [[res("h%d_%d" % (k, t)) for t in range(3)] for k in range(8)]
    r_xn = [[res("xn%d_%d" % (k, t)) for t in range(3)] for k in range(8)]
    r_a1 = res("arena1_all")
    r_stg = [dres("stg0"), dres("stg1")]
    r_ringA = [dres("ringA%d" % i) for i in range(NRA)]
    r_wdh = [dres("wdh%d" % i) for i in range(11)]
    r_hid = [[res("hid%d_%d" % (a, t)) for t in range(3)] for a in range(11)]
    r_const = dres("const")
    r_misc = res("misc")

    def pe(fn, reads, writes):
        kb.op("pe", fn, reads, writes)

    def act(fn, reads, writes):
        kb.op("act", fn, reads, writes)

    def dve(fn, reads, writes):
        kb.op("dve", fn, reads, writes)

    def pool(fn, reads, writes):
        kb.op("pool", fn, reads, writes)

    def mm_group(out_ap, pairs, reads, writes):
        n = len(pairs)

        def fn(e):
            ins = None
            for i, (l, r) in enumerate(pairs):
                ins = e.matmul(out_ap, lhsT=l, rhs=r, start=(i == 0), stop=(i == n - 1))
            return ins
        pe(fn, reads, writes)

    r_identf, r_identb, r_ones, r_masks = res("identf"), res("identb"), res("onesb"), res("masks")
    pool(lambda e: e.memset(identf[:], 0.0), [], [r_identf])
    pool(lambda e: e.affine_select(out=identf[:], in_=identf[:], pattern=[[-1, 128]], compare_op=ALU.not_equal,
                                   fill=1.0, base=0, channel_multiplier=1), [r_identf], [r_identf])
    pool(lambda e: e.tensor_copy(out=identb[:], in_=identf[:]), [r_identf], [r_identb])
    pool(lambda e: e.memset(onesb[:], 1.0), [], [r_ones])
    pool(lambda e: e.memset(epsT[:], EPS), [], [r_misc])
    pool(lambda e: e.memset(maskP[:], 1.0), [], [r_masks])
    pool(lambda e: e.affine_select(out=maskP[:], in_=maskP[:], pattern=[[1, 128]], compare_op=ALU.is_ge,
                                   fill=0.0, base=0, channel_multiplier=-1), [r_masks], [r_masks])
    pool(lambda e: e.memset(maskP[0:64, 64:128], 0.0), [r_masks], [r_masks])
    pool(lambda e: e.memset(Eseq[:], 1.0), [], [r_masks])
    pool(lambda e: e.affine_select(out=Eseq[:], in_=Eseq[:], pattern=[[1, 64]], compare_op=ALU.is_ge,
                                   fill=0.0, base=0, channel_multiplier=-4), [r_masks], [r_masks])
    pool(lambda e: e.affine_select(out=Eseq[:], in_=Eseq[:], pattern=[[-1, 64]], compare_op=ALU.is_ge,
                                   fill=0.0, base=3, channel_multiplier=4), [r_masks], [r_masks])
    pool(lambda e: e.memset(rowm[:], 1.0), [], [r_masks])
    pool(lambda e: e.affine_select(out=rowm[:], in_=rowm[:], pattern=[[-4, 16]], compare_op=ALU.is_ge,
                                   fill=0.0, base=0, channel_multiplier=1), [r_masks], [r_masks])
    pool(lambda e: e.affine_select(out=rowm[:], in_=rowm[:], pattern=[[4, 16]], compare_op=ALU.is_ge,
                                   fill=0.0, base=3, channel_multiplier=-1), [r_masks], [r_masks])
    mm_group(PB[0][0:64, 0:64], [(Eseq[:, :], Eseq[:, :])], [r_masks], [PR[0]])
    pool(lambda e: e.memset(mask0[:], 0.0), [r_masks], [r_masks])
    dve(lambda e: e.tensor_tensor(out=mask0[0:64, 0:64], in0=PB[0][0:64, 0:64], in1=maskP[0:64, 0:64], op=ALU.mult),
        [PR[0], r_masks], [r_masks])
    dve(lambda e: e.tensor_copy(out=mask0[64:80, 64:80], in_=maskP[64:80, 64:80]), [r_masks], [r_masks])
    pool(lambda e: e.memset(evod[:], 0.0), [], [r_misc])
    ev4 = sb("ev4", [128, 4])
    pool(lambda e: e.memset(ev4[:], 1.0), [], [r_misc])
    pool(lambda e: e.affine_select(out=ev4[:], in_=ev4[:], pattern=[[-32, 4]], compare_op=ALU.is_ge,
                                   fill=0.0, base=0, channel_multiplier=1), [r_misc], [r_misc])
    pool(lambda e: e.affine_select(out=ev4[:], in_=ev4[:], pattern=[[32, 4]], compare_op=ALU.is_ge,
                                   fill=0.0, base=15, channel_multiplier=-1), [r_misc], [r_misc])
    dve(lambda e: e.tensor_tensor(out=evod[:, 0:2], in0=ev4[:, 0:2], in1=ev4[:, 2:4], op=ALU.add), [r_misc], [r_misc])
    dve(lambda e: e.tensor_tensor(out=evod[:, 0:1], in0=evod[:, 0:1], in1=evod[:, 1:2], op=ALU.add), [r_misc], [r_misc])
    dve(lambda e: e.tensor_scalar(out=evod[:, 1:2], in0=evod[:, 0:1], scalar1=-1.0, scalar2=1.0, op0=ALU.mult,
                                  op1=ALU.add), [r_misc], [r_misc])
    dve(lambda e: e.tensor_scalar(out=evod[:, 2:4], in0=evod[:, 0:2], scalar1=-1.0, scalar2=None, op0=ALU.mult),
        [r_misc], [r_misc])
    pool(lambda e: e.memset(Sst[:], 0.0), [], [res("Sst")])
    pool(lambda e: e.memset(car[:], 0.0), [], [res("car")])

    r_PV = res("PV")
    for (src, dst, nrows_all) in ((pvec, PV, NPROW), (s5pv, S5P, DEPTH * 48)):
        for r0 in range(0, nrows_all, 128):
            nr = min(128, nrows_all - r0)
            kb.dma("sp", stg[0:nr, 0, 0:128], src[r0:r0 + nr, :], r_stg[0], [], [r_stg[0]])
            pe(lambda e, nr=nr: e.transpose(PB[0][:, 0:nr], stg[0:nr, 0, 0:128], identf[0:nr, 0:nr]),
               [r_stg[0], r_identf], [PR[0]])
            dve(lambda e, nr=nr, r0=r0, dst=dst: e.tensor_copy(out=dst[:, r0:r0 + nr], in_=PB[0][:, 0:nr]),
                [PR[0]], [r_PV])

    lbe = sb("lbe", [128, 4, 4])
    lbs = sb("lbs", [128, 4])
    act(lambda e: e.activation(out=lbe[:].rearrange("p a b -> p (a b)"), in_=PV[:, R_LBP:R_LBP + 16], func=AF.Exp),
        [r_PV], [r_misc])
    dve(lambda e: e.tensor_tensor(out=lbs[:], in0=lbe[:, 0, :], in1=lbe[:, 1, :], op=ALU.add), [r_misc], [r_misc])
    dve(lambda e: e.tensor_tensor(out=lbs[:], in0=lbs[:], in1=lbe[:, 2, :], op=ALU.add), [r_misc], [r_misc])
    dve(lambda e: e.tensor_tensor(out=lbs[:], in0=lbs[:], in1=lbe[:, 3, :], op=ALU.add), [r_misc], [r_misc])
    dve(lambda e: e.reciprocal(out=lbs[:], in_=lbs[:]), [r_misc], [r_misc])
    for l in range(4):
        dve(lambda e, l=l: e.tensor_tensor(out=lbe[:, l, :], in0=lbe[:, l, :], in1=lbs[:], op=ALU.mult),
            [r_misc], [r_misc])
    r_lb = res("lbT")
    dve(lambda e: e.memset(lbT[:, 0, :, 0:1], 0.0), [], [r_lb])
    for l in range(1, 4):
        dve(lambda e, l=l: e.tensor_tensor(out=lbT[:, l, :, 0:1], in0=lbT[:, l - 1, :, 0:1],
                                           in1=lbe[:, l, :].unsqueeze(2), op=ALU.add), [r_misc, r_lb], [r_lb])
    dve(lambda e: e.tensor_scalar(out=lbT[:, :, :, 1:2], in0=lbT[:, :, :, 0:1], scalar1=-1.0, scalar2=1.0,
                                  op0=ALU.mult, op1=ALU.add), [r_lb], [r_lb])

    class WStream:
        def __init__(self):
            self.plan = {"A": [], "B": []}
            self.issued = {"A": 0, "B": 0}
            self.consumed = {"A": 0, "B": 0}
            self.released = set()

        def add(self, cls, src_fn, hold=None):
            self.plan[cls].append((src_fn, hold))

        def release(self, key):
            self.released.add(key)
            self.pump()

        def pump(self):
            for cls, nslot, rlist in (("A", NRA, r_ringA), ("B", 11, r_wdh)):
                while (self.issued[cls] < len(self.plan[cls]) and
                       self.issued[cls] - self.consumed[cls] < nslot):
                    i = self.issued[cls]
                    src_fn, hold = self.plan[cls][i]
                    if hold is not None and hold not in self.released:
                        break
                    slot = i % nslot
                    dst, src = src_fn(slot)
                    kb.dma("pool", dst, src, rlist[slot], [], [rlist[slot]])
                    self.issued[cls] += 1

        def take(self, cls, n=1):
            self.pump()
            i = self.consumed[cls]
            nslot = NRA if cls == "A" else 11
            assert i + n <= self.issued[cls], (cls, i, n, self.issued[cls])
            return i % nslot

        def done(self, cls, n=1):
            self.consumed[cls] += n
            self.pump()

    WS = WStream()

    def colblock_src(w, l, c0, ncols):
        def f(slot):
            dst = ringA[:, slot, 0:8 * ncols].rearrange("p (k c) -> p k c", c=ncols)
            src = w[l, :, c0:c0 + ncols].rearrange("(k p) c -> p k c", p=128)
            return dst, src
        return f

    def rowblock_src(w, l, r0):
        def f(slot):
            return wdh[:, slot, :], w[l, r0:r0 + 128, :]
        return f

    def wout_src(l, kk):
        def f(slot):
            dst = ringA[:, slot, 0:2048].rearrange("p (k c) -> p k c", c=1024)
            src = w_out[l, kk * 256:(kk + 1) * 256, :].rearrange("(k p) c -> p k c", p=128)
            return dst, src
        return f

    def glu_src(l):
        def f(slot):
            dst = ringA[:, slot, 0:2048].rearrange("p (k c) -> p k c", c=512)
            src = w_glu[l].rearrange("(k p) c -> p k c", p=128)
            return dst, src
        return f

    groups = [dict(T=TG1, x0=0, tiles=[(0, 80), (80, 592), (592, 1104)]),
              dict(T=TG2, x0=TG1, tiles=[(0, 512), (512, 1024)])]
    WIN_ORDER = [2, 4, 6, 8, 3, 5, 7, 9, 0, 1]
    for G in groups:
        for l in range(depth):
            for (wg, wu, wd) in ((w_f1g, w_f1u, w_f1d), (w_f2g, w_f2u, w_f2d)):
                for half in range(2):
                    for hb in range(11):
                        c0 = (half * 11 + hb) * 128
                        WS.add("A", colblock_src(wg, l, c0, 128))
                        WS.add("A", colblock_src(wu, l, c0, 128))
                    for hb in range(11):
                        hold = None
                        if wd is w_f2d and half == 0 and hb == 0 and mixer:
                            hold = "f2_%d_%d" % (groups.index(G), l)
                        WS.add("B", rowblock_src(wd, l, (half * 11 + hb) * 128), hold=hold)
                if wg is w_f1g and mixer:
                    for b in WIN_ORDER:
                        WS.add("A", colblock_src(w_in, l, b * 256, 256))
                    if s5:
                        WS.add("A", glu_src(l))
                    for kk in range(4):
                        WS.add("A", wout_src(l, kk))

    def tile_idx(G, c0):
        return [t[0] for t in G["tiles"]].index(c0)

    sqi = [0]
    r_sq = [res("sq%d" % i) for i in range(3)]
    r_rt = [res("rt0"), res("rt1")]
    r_sg = [res("sg0"), res("sg1")]
    psrot = [0]

    def rmsnorm_to_xn(G, gcol, nfeat_inv=1.0 / D):
        for ti, (c0, c1) in enumerate(G["tiles"]):
            n = c1 - c0
            pbi = 6
            for k in range(8):
                s = sqi[0] % 3
                sqi[0] += 1
                act(lambda e, k=k, s=s: e.activation(out=sq[:, s, 0:n], in_=hT[:, k, c0:c1], func=AF.Square),
                    [r_h[k][ti]], [r_sq[s]])
                pe(lambda e, k=k, s=s: e.matmul(PB[pbi][:, 0:n], lhsT=onesb[:], rhs=sq[:, s, 0:n],
                                                start=(k == 0), stop=(k == 7)),
                   [r_sq[s], r_ones], [PR[pbi]])
            j = ti % 2
            act(lambda e, j=j: e.activation(out=rt[:, j, 0:n], in_=PB[pbi][:, 0:n], func=AF.Ln, bias=epsT[:, 0:1],
                                            scale=nfeat_inv), [PR[pbi], r_misc], [r_rt[j]])
            act(lambda e, j=j: e.activation(out=rt[:, j, 0:n], in_=rt[:, j, 0:n], func=AF.Exp, scale=-0.5),
                [r_rt[j]], [r_rt[j]])
            for k in range(8):
                dve(lambda e, k=k, j=j: e.scalar_tensor_tensor(out=xn[:, k, c0:c1], in0=hT[:, k, c0:c1],
                                                               scalar=PV[:, gcol + k:gcol + k + 1], op0=ALU.mult,
                                                               in1=rt[:, j, 0:n], op1=ALU.mult),
                    [r_h[k][ti], r_rt[j], r_PV], [r_xn[k][ti]])

    def ffn(G, l):
        ntile = len(G["tiles"])
        for half in range(2):
            for hb in range(11):
                sg_ = WS.take("A", 2)
                su_ = (sg_ + 1) % NRA
                wgv = ringA[:, sg_, 0:1024].rearrange("p (k c) -> p k c", c=128)
                wuv = ringA[:, su_, 0:1024].rearrange("p (k c) -> p k c", c=128)
                for ti, (c0, c1) in enumerate(G["tiles"]):
                    n = c1 - c0
                    pg = psrot[0] % 2
                    psrot[0] += 1
                    bg, bu = 2 * pg, 2 * pg + 1
                    mm_group(PB[bg][:, 0:n], [(wgv[:, k, :], xn[:, k, c0:c1]) for k in range(8)],
                             [r_ringA[sg_]] + [r_xn[k][ti] for k in range(8)], [PR[bg]])
                    mm_group(PB[bu][:, 0:n], [(wuv[:, k, :], xn[:, k, c0:c1]) for k in range(8)],
                             [r_ringA[su_]] + [r_xn[k][ti] for k in range(8)], [PR[bu]])
                    act(lambda e, pg=pg, bg=bg: e.activation(out=sgt[:, pg, 0:n], in_=PB[bg][:, 0:n], func=AF.Silu),
                        [PR[bg]], [r_sg[pg]])
                    dve(lambda e, pg=pg, bu=bu, hb=hb: e.tensor_tensor(out=hid[:, hb, c0:c1], in0=sgt[:, pg, 0:n],
                                                                      in1=PB[bu][:, 0:n], op=ALU.mult),
                        [r_sg[pg], PR[bu]], [r_hid[hb][ti]])
                WS.done("A", 2)
            s0 = WS.take("B", 11)
            for ti, (c0, c1) in enumerate(G["tiles"]):
                n = c1 - c0
                for ot in range(8):
                    pbi = 4 + (psrot[0] % 2)
                    psrot[0] += 1
                    mm_group(PB[pbi][:, 0:n],
                             [(wdh[:, (s0 + hb) % 11, ot * 128:(ot + 1) * 128], hid[:, hb, c0:c1]) for hb in range(11)],
                             r_wdh + [r_hid[hb][ti] for hb in range(11)], [PR[pbi]])
                    dve(lambda e, pbi=pbi, ot=ot: e.scalar_tensor_tensor(out=hT[:, ot, c0:c1], in0=PB[pbi][:, 0:n],
                                                                        scalar=0.5, op0=ALU.mult,
                                                                        in1=hT[:, ot, c0:c1], op1=ALU.add),
                        [PR[pbi], r_h[ot][ti]], [r_h[ot][ti]])
            WS.done("B", 11)

    def load_x(G):
        T = G["T"]
        nblk = (T + 127) // 128
        for b in range(nblk):
            c0 = b * 128
            nr = min(128, T - c0)
            ti = None
            s = b % 2
            kb.dma("sp", stg[0:nr, s, :], xin[G["x0"] + c0:G["x0"] + c0 + nr, :], r_stg[s], [], [r_stg[s]])
            for kk in range(2):
                pbi = psrot[0] % 2
                psrot[0] += 1

                def fn(e, kk=kk, pbi=pbi, s=s, nr=nr):
                    ins = None
                    for k4 in range(4):
                        k = kk * 4 + k4
                        ins = e.transpose(PB[pbi][:, k4 * 128:k4 * 128 + nr], stg[0:nr, s, k * 128:(k + 1) * 128],
                                          identf[0:nr, 0:nr])
                    return ins
                pe(fn, [r_stg[s], r_identf], [PR[pbi]])
                tis = sorted(set(ti for ti, (a, bb) in enumerate(G["tiles"]) if a < c0 + nr and bb > c0))
                wr = [r_h[kk * 4 + k4][ti] for k4 in range(4) for ti in tis]
                eng = act if kk == 0 else dve
                if kk == 0:
                    act(lambda e, kk=kk, pbi=pbi, nr=nr, c0=c0: e.activation(
                        out=hT[:, kk * 4:(kk + 1) * 4, c0:c0 + nr],
                        in_=PB[pbi][:, :].rearrange("p (a t) -> p a t", t=128)[:, :, 0:nr], func=AF.Copy),
                        [PR[pbi]], wr)
                else:
                    dve(lambda e, kk=kk, pbi=pbi, nr=nr, c0=c0: e.tensor_copy(
                        out=hT[:, kk * 4:(kk + 1) * 4, c0:c0 + nr],
                        in_=PB[pbi][:, :].rearrange("p (a t) -> p a t", t=128)[:, :, 0:nr]),
                        [PR[pbi]], wr)

    def store_y(G):
        T = G["T"]
        for ti, (c0, c1) in enumerate(G["tiles"]):
            n = c1 - c0
            pbi = 6
            for k in range(8):
                s = sqi[0] % 3
                sqi[0] += 1
                act(lambda e, k=k, s=s: e.activation(out=sq[:, s, 0:n], in_=hT[:, k, c0:c1], func=AF.Square),
                    [r_h[k][ti]], [r_sq[s]])
                pe(lambda e, k=k, s=s: e.matmul(PB[pbi][:, 0:n], lhsT=onesb[:], rhs=sq[:, s, 0:n],
                                                start=(k == 0), stop=(k == 7)), [r_sq[s], r_ones], [PR[pbi]])
            j = ti % 2
            act(lambda e, j=j: e.activation(out=rt[:, j, 0:n], in_=PB[pbi][:, 0:n], func=AF.Ln, bias=epsT[:, 0:1],
                                            scale=1.0 / D), [PR[pbi], r_misc], [r_rt[j]])
            act(lambda e, j=j: e.activation(out=rt[:, j, 0:n], in_=rt[:, j, 0:n], func=AF.Exp, scale=-0.5),
                [r_rt[j]], [r_rt[j]])
            for k in range(8):
                dve(lambda e, k=k, j=j: e.scalar_tensor_tensor(out=hT[:, k, c0:c1], in0=hT[:, k, c0:c1],
                                                               scalar=PV[:, R_NFIN + k:R_NFIN + k + 1], op0=ALU.mult,
                                                               in1=rt[:, j, 0:n], op1=ALU.mult),
                    [r_h[k][ti], r_rt[j], r_PV], [r_h[k][ti]])
        nblk = (T + 127) // 128
        for b in range(nblk):
            c0 = b * 128
            nr = min(128, T - c0)
            s = b % 2
            tis = sorted(set(ti for ti, (a, bb) in enumerate(G["tiles"]) if a < c0 + nr and bb > c0))
            for kk in range(2):
                pbi = psrot[0] % 2
                psrot[0] += 1

                def fn(e, kk=kk, pbi=pbi, nr=nr, c0=c0):
                    ins = None
                    for k4 in range(4):
                        k = kk * 4 + k4
                        ins = e.transpose(PB[pbi][0:nr, k4 * 128:(k4 + 1) * 128], hT[:, k, c0:c0 + nr], identf[:, :])
                    return ins
                pe(fn, [r_identf] + [r_h[kk * 4 + k4][ti] for k4 in range(4) for ti in tis], [PR[pbi]])
                if kk == 0:
                    act(lambda e, pbi=pbi, nr=nr, s=s: e.activation(out=stg[0:nr, s, 0:512], in_=PB[pbi][0:nr, :],
                                                                    func=AF.Copy), [PR[pbi]], [r_stg[s]])
                else:
                    dve(lambda e, pbi=pbi, nr=nr, s=s: e.tensor_copy(out=stg[0:nr, s, 512:1024], in_=PB[pbi][0:nr, :]),
                        [PR[pbi]], [r_stg[s]])
            kb.dma("sp", yout[G["x0"] + c0:G["x0"] + c0 + nr, :], stg[0:nr, s, :], r_stg[s], [r_stg[s]], [])

    r_q = [res("qT%d" % i) for i in range(4)]
    r_k = [res("kT%d" % i) for i in range(4)]
    r_gate = [res("gate%d" % i) for i in range(4)]
    r_u = [res("uT%d" % i) for i in range(4)]
    r_Vb = {(bi_, hf_): res("Vtok%d_%d" % (bi_, hf_)) for bi_ in range(9) for hf_ in range(2)}
    r_Vall = list(r_Vb.values())
    r_tmp = [res("tmpA"), res("tmpB"), res("tmpC"), res("tmpD")]
    r_eb = res("eb")
    r_S = [[res("S%d_%d" % (l, h)) for h in range(4)] for l in range(DEPTH)]
    r_Sd = [res("Sd%d" % i) for i in range(4)]
    r_Sdb = [res("Sdb%d" % i) for i in range(4)]
    r_ktok = [res("ktok%d" % i) for i in range(4)]
    r_kms = [res("kms%d" % i) for i in range(4)]
    r_Am = [res("Am%d" % i) for i in range(4)]
    r_hgst = [dres("hgst%d" % i) for i in range(4)]
    r_hgout = [dres("hgout%d" % i) for i in range(7)]
    hgo = [hgout[:, i, :] for i in range(3)] + [Sd[:, i, :] for i in range(4)]
    r_wo = res("wo")
    r_s5t = res("s5tables")
    r_s5sm = res("s5sm")
    r_car = res("car")
    r_carq = [res("carq%d" % i) for i in range(4)]
    cnt = dict(sd=0, kt=0, am=0, hgst=0, hgout=0, km=0)
    S5A1, S5A2 = [], []

    def mixer_fn(G, l, gi):
        T = G["T"]
        tiles = G["tiles"]
        ntile = len(tiles)
        mix_res = r_q + r_k + r_gate + r_u + r_Vall
        ffn_res = [x for a in r_hid for x in a] + r_wdh
        kb.alias(mix_res, ffn_res)
        rm_res = res("rmask")
        hg2 = r_tmp + [res("eq%d" % h) for h in range(4)] + [rm_res]
        kb.alias(hg2, S5A2)
        rmsnorm_to_xn(G, R_NMX + 8 * l)
        if True:
            dve(lambda e: e.memset(rmask[:, 0:T], 1.0), [], [rm_res])
            if gi == 0:
                dve(lambda e: e.memset(rmask[:, 0:64].rearrange("p (a b) -> p a b", b=4)[:, :, 0:1], 0.0), [rm_res], [rm_res])
                dve(lambda e: e.memset(rmask[:, 64:65], 0.0), [rm_res], [rm_res])
                dve(lambda e: e.memset(rmask[:, 80:TG1].rearrange("p (a b) -> p a b", b=64)[:, :, 0:1], 0.0), [rm_res], [rm_res])
            else:
                dve(lambda e: e.memset(rmask[:, 0:T].rearrange("p (a b) -> p a b", b=64)[:, :, 0:1], 0.0), [rm_res], [rm_res])
        if gi == 0:
            chunks = [(64, 80)] + [(80 + 64 * i, 144 + 64 * i) for i in range(16)]
            tokblocks = [(0, 80)] + [(80 + 128 * i, 208 + 128 * i) for i in range(8)]
        else:
            chunks = [(64 * i, 64 * i + 64) for i in range(16)]
            tokblocks = [(128 * i, 128 * i + 128) for i in range(8)]
        nchunk_all = len(chunks) + (16 if gi == 0 else 0)

        def xn_reads(ti):
            return [r_xn[k][ti] for k in range(8)]

        for b in WIN_ORDER:
            slot = WS.take("A")
            wv = ringA[:, slot, 0:2048].rearrange("p (k c) -> p k c", c=256)
            for j in range(2):
                zt = 2 * b + j
                hd = zt % 4
                kind = zt // 4
                if kind == 2:
                    continue
                for ti, (c0, c1) in enumerate(tiles):
                    n = c1 - c0
                    pbi = psrot[0] % 4
                    psrot[0] += 1
                    mm_group(PB[pbi][:, 0:n], [(wv[:, k, j * 128:(j + 1) * 128], xn[:, k, c0:c1]) for k in range(8)],
                             [r_ringA[slot]] + xn_reads(ti), [PR[pbi]])
                    if kind == 1:
                        act(lambda e, pbi=pbi: e.activation(out=tmpA[:, c0:c1], in_=PB[pbi][:, 0:n], func=AF.Sigmoid),
                            [PR[pbi]], [r_tmp[0]])
                        act(lambda e, pbi=pbi: e.activation(out=tmpD[:, c0:c1], in_=PB[pbi][:, 0:n], func=AF.Sigmoid,
                                                            scale=-1.0), [PR[pbi]], [r_tmp[3]])
                    elif kind == 0:
                        act(lambda e, pbi=pbi, hd=hd: e.activation(out=qT[:, hd, c0:c1], in_=PB[pbi][:, 0:n],
                                                                   func=AF.Silu), [PR[pbi]], [r_q[hd]])
                    elif kind == 3:
                        act(lambda e, pbi=pbi, hd=hd: e.activation(out=gate[:, hd, c0:c1], in_=PB[pbi][:, 0:n],
                                                                   func=AF.Silu), [PR[pbi]], [r_gate[hd]])
                    else:
                        act(lambda e, pbi=pbi, hd=hd: e.activation(out=uT[:, hd, c0:c1], in_=PB[pbi][:, 0:n],
                                                                   func=AF.Copy), [PR[pbi]], [r_u[hd]])
                if kind == 0:
                    dve(lambda e, hd=hd: e.tensor_tensor(out=qT[:, hd, 0:T], in0=qT[:, hd, 0:T],
                                                         in1=arena2[:, (4 + hd) * TG1:(4 + hd) * TG1 + T], op=ALU.mult),
                        [r_q[hd], res("eq%d" % hd)], [r_q[hd]])
                if kind == 1:
                    lb0 = lbT[:, l, hd, 0:1]
                    lb1 = lbT[:, l, hd, 1:2]
                    dve(lambda e: e.tensor_scalar(out=tmpA[:, 0:T], in0=tmpA[:, 0:T], scalar1=lb1, scalar2=lb0,
                                                  op0=ALU.mult, op1=ALU.add), [r_tmp[0], r_lb], [r_tmp[0]])
                    act(lambda e: e.activation(out=tmpB[:, 0:T], in_=tmpA[:, 0:T], func=AF.Ln), [r_tmp[0]], [r_tmp[1]])
                    dve(lambda e: e.tensor_tensor_scan(out=tmpC[:, 0:T], data0=rmask[:, 0:T], data1=tmpB[:, 0:T],
                                                       initial=0.0, op0=ALU.mult, op1=ALU.add),
                        [r_tmp[1], rm_res], [r_tmp[2]])
                    segs = []
                    if gi == 0:
                        segs.append((0, 64, 4))
                        segs.append((64, 80, 16))
                        segs.append((80, TG1, 64))
                    else:
                        segs.append((0, T, 64))
                    ebo = 0
                    for (a0, a1, L) in segs:
                        nch = (a1 - a0) // L
                        v = tmpC[:, a0:a1].rearrange("p (a b) -> p a b", b=L)
                        dve(lambda e, v=v, a0=a0, a1=a1, L=L, nch=nch: e.tensor_tensor(
                            out=tmpB[:, a0:a1].rearrange("p (a b) -> p a b", b=L), in0=v,
                            in1=v[:, :, L - 1:L].to_broadcast([128, nch, L]), op=ALU.subtract),
                            [r_tmp[2]], [r_tmp[1]])
                        act(lambda e, v=v, L=L, nch=nch, ebo=ebo, hd=hd: e.activation(
                            out=eb[:, hd, ebo:ebo + nch].unsqueeze(2), in_=v[:, :, L - 1:L], func=AF.Exp),
                            [r_tmp[2]], [r_eb])
                        ebo += nch
                    act(lambda e: e.activation(out=tmpC[:, 0:T], in_=tmpB[:, 0:T], func=AF.Exp, scale=-1.0),
                        [r_tmp[1]], [r_tmp[2]])
                    act(lambda e, hd=hd: e.activation(out=arena2[:, (4 + hd) * TG1:(4 + hd) * TG1 + T],
                                                      in_=tmpB[:, 0:T], func=AF.Exp), [r_tmp[1]], [res("eq%d" % hd)])
                    dve(lambda e, hd=hd: e.scalar_tensor_tensor(out=kT[:, hd, 0:T], in0=tmpD[:, 0:T], scalar=lb1,
                                                                op0=ALU.mult, in1=tmpC[:, 0:T], op1=ALU.mult),
                        [r_tmp[3], r_tmp[2], r_lb], [r_k[hd]])
            if b in (4, 5):
                for bi, (t0, t1) in enumerate(tokblocks):
                    nt = t1 - t0
                    pbi = 4 + (psrot[0] % 2)
                    psrot[0] += 1
                    tis = sorted(set(ti for ti, (a, bb) in enumerate(tiles) if a < t1 and bb > t0))
                    mm_group(PB[pbi][0:nt, 0:256], [(xn[:, k, t0:t1], wv[:, k, :]) for k in range(8)],
                             [r_ringA[slot]] + [r_xn[k][ti] for k in range(8) for ti in tis], [PR[pbi]])
                    cb = (b - 4) * 256
                    if bi % 2 == 0:
                        act(lambda e, pbi=pbi, nt=nt, bi=bi, cb=cb: e.activation(out=Vtok[0:nt, bi, cb:cb + 256],
                                                                                in_=PB[pbi][0:nt, 0:256], func=AF.Copy),
                            [PR[pbi]], [r_Vb[(bi, b - 4)]])
                    else:
                        dve(lambda e, pbi=pbi, nt=nt, bi=bi, cb=cb: e.tensor_copy(out=Vtok[0:nt, bi, cb:cb + 256],
                                                                                 in_=PB[pbi][0:nt, 0:256]),
                            [PR[pbi]], [r_Vb[(bi, b - 4)]])
            WS.done("A", 1)

        if hgrn:
            hgrn_fn(G, l, gi, chunks, tokblocks)
        else:
            for hd in range(4):
                dve(lambda e, hd=hd: e.memset(xn[:, hd, 0:T], 0.0), [], [r_xn[hd][ti] for ti in range(ntile)])
        if s5:
            s5_fn(G, l, gi)
        else:
            for hd in range(4):
                dve(lambda e, hd=hd: e.memset(xn[:, 4 + hd, 0:T], 0.0), [], [r_xn[4 + hd][ti] for ti in range(ntile)])
            if True:
                WS.take("A") if False else None
        kb.alias(r_wdh, mix_res + S5A1)
        WS.release("f2_%d_%d" % (gi, l))
        sA = WS.take("A", 4)
        for ti, (c0, c1) in enumerate(tiles):
            n = c1 - c0
            for ot in range(8):
                pbi = 4 + (psrot[0] % 2)
                psrot[0] += 1
                mm_group(PB[pbi][:, 0:n],
                         [(ringA[:, (sA + k // 2) % NRA, (k % 2) * 1024 + ot * 128:(k % 2) * 1024 + (ot + 1) * 128],
                           xn[:, k, c0:c1]) for k in range(8)],
                         r_ringA + [r_xn[k][ti] for k in range(8)], [PR[pbi]])
                dve(lambda e, pbi=pbi, ot=ot: e.tensor_tensor(out=hT[:, ot, c0:c1], in0=PB[pbi][:, 0:n],
                                                              in1=hT[:, ot, c0:c1], op=ALU.add),
                    [PR[pbi], r_h[ot][ti]], [r_h[ot][ti]])
        WS.done("A", 4)
        kb.alias([x for a in r_hid for x in a], mix_res + S5A1)

    def hg_norm_out(G, l, hd, ti, c0, c1, pbo):
        n = c1 - c0
        s = sqi[0] % 3
        sqi[0] += 1
        act(lambda e: e.activation(out=sq[:, s, 0:n], in_=PB[pbo][:, 0:n], func=AF.Square), [PR[pbo]], [r_sq[s]])
        pe(lambda e: e.matmul(PB[6][:, 0:n], lhsT=onesb[:], rhs=sq[:, s, 0:n], start=True, stop=True),
           [r_sq[s], r_ones], [PR[6]])
        j = sqi[0] % 2
        act(lambda e: e.activation(out=rt[:, j, 0:n], in_=PB[6][:, 0:n], func=AF.Ln, bias=epsT[:, 0:1],
                                        scale=1.0 / 128), [PR[6], r_misc], [r_rt[j]])
        act(lambda e: e.activation(out=rt[:, j, 0:n], in_=rt[:, j, 0:n], func=AF.Exp, scale=-0.5),
            [r_rt[j]], [r_rt[j]])
        dve(lambda e: e.tensor_tensor(out=sgt[:, j, 0:n], in0=PB[pbo][:, 0:n], in1=rt[:, j, 0:n], op=ALU.mult),
            [PR[pbo], r_rt[j]], [r_sg[j]])
        gcol = R_HGN + 4 * l + hd
        dve(lambda e: e.scalar_tensor_tensor(out=xn[:, hd, c0:c1], in0=sgt[:, j, 0:n], scalar=PV[:, gcol:gcol + 1],
                                             op0=ALU.mult, in1=gate[:, hd, c0:c1], op1=ALU.mult),
            [r_sg[j], r_gate[hd], r_PV], [r_xn[hd][ti]])

    def hgrn_fn(G, l, gi, chunks, tokblocks):
        tiles = G["tiles"]
        r_pu = [PR[4 + h % 2] for h in range(4)]
        r_psc = [PR[4 + h % 2] for h in range(4)]
        r_ptr = [PRb for h in range(4)]
        pu = [PB[4 + h % 2][:, 0:128] for h in range(4)]
        psc = [PB[4 + h % 2][:, 0:128] for h in range(4)]
        ptr = [PBb[:, 0:128] for h in range(4)]
        for ti, (c0t, c1t) in enumerate(tiles):
            tch = [c for c in chunks if c[0] >= c0t and c[1] <= c1t]
            tbl = [(bi, tb) for bi, tb in enumerate(tokblocks) if tb[0] >= c0t and tb[1] <= c1t]
            for (bi, (t0, t1)) in tbl:
                nb = t1 - t0
                msk = mask0 if (gi == 0 and bi == 0) else maskP

                def fsc(e, nb=nb, t0=t0, t1=t1):
                    ins = None
                    for hd in range(4):
                        ins = e.matmul(PB[6][0:nb, 128 * hd:128 * hd + nb], lhsT=kT[:, hd, t0:t1], rhs=qT[:, hd, t0:t1],
                                       start=True, stop=True)
                    return ins
                pe(fsc, r_k + r_q, [PR[6]])
                dve(lambda e, msk=msk, nb=nb: e.tensor_tensor(
                    out=Am[0:nb, :, 0:nb], in0=PB[6][0:nb, :].rearrange("p (a t) -> p a t", t=128)[:, :, 0:nb],
                    in1=msk[0:nb, 0:nb].unsqueeze(1).to_broadcast([nb, 4, nb]), op=ALU.mult),
                    [PR[6], r_masks], r_Am)

                def ftr(e, nb=nb, t0=t0, t1=t1):
                    ins = None
                    for hd in range(4):
                        ins = e.transpose(PBb[0:nb, 128 * hd:128 * (hd + 1)], kT[:, hd, t0:t1], identb[:, :])
                    return ins
                pe(ftr, r_k + [r_identb], [PRb])
                act(lambda e, nb=nb: e.activation(out=ktok[0:nb, :, :].rearrange("p a t -> p (a t)"),
                                                  in_=PBb[0:nb, 0:512], func=AF.Copy), [PRb], r_ktok)
                if gi == 0 and bi == 0:
                    for sq_ in range(NSEQ):
                        hss = []
                        for hd in range(4):
                            hs = cnt["hgst"] % 4
                            cnt["hgst"] += 1
                            hss.append(hs)
                            kb.dma("sp", hgst[:, hs, :], st_hg[l, sq_, hd], r_hgst[hs], [], [r_hgst[hs]])
                        dve(lambda e, sq_=sq_: e.tensor_scalar(out=kms[:, :, :], in0=ktok[0:64, :, :],
                                                               scalar1=rowm[:, sq_:sq_ + 1], scalar2=None, op0=ALU.mult),
                            r_ktok + [r_masks], r_kms)
                        for hd in range(4):
                            act(lambda e, hd=hd, hs=hss[hd], sq_=sq_: e.activation(
                                out=Sdb[:, hd, :], in_=hgst[:, hs, :], func=AF.Copy, scale=eb[:, hd, sq_:sq_ + 1]),
                                [r_hgst[hss[hd]], r_eb], [r_Sdb[hd]])
                        for pr in range(2):
                            def fu(e, pr=pr):
                                ins = None
                                for hd in (pr, pr + 2):
                                    ins = e.matmul(PB[4 + pr][:, 128 * (hd // 2):128 * (hd // 2) + 128],
                                                   lhsT=kms[:, hd, :], rhs=Vtok[0:64, 0, hd * 128:(hd + 1) * 128],
                                                   start=True, stop=True)
                                return ins
                            pe(fu, [r_kms[pr], r_kms[pr + 2], r_Vb[(0, 0)], r_Vb[(0, 1)]], [PR[4 + pr]])
                        for hd in range(4):
                            pe(lambda e, hd=hd, sq_=sq_: e.matmul(PB[hd][:, 4 * sq_:4 * sq_ + 4], lhsT=Sdb[:, hd, :],
                                                                  rhs=qT[:, hd, 4 * sq_:4 * sq_ + 4],
                                                                  start=(sq_ == 0), stop=False),
                               [r_Sdb[hd], r_q[hd]], [PR[hd]])
                        for hd in range(4):
                            ho = cnt["hgout"] % 7
                            cnt["hgout"] += 1
                            pr = hd % 2
                            dve(lambda e, ho=ho, hd=hd, pr=pr, hs=hss[hd], sq_=sq_: e.scalar_tensor_tensor(
                                out=hgo[ho], in0=hgst[:, hs, :], scalar=eb[:, hd, sq_:sq_ + 1], op0=ALU.mult,
                                in1=PB[4 + pr][:, 128 * (hd // 2):128 * (hd // 2) + 128], op1=ALU.add),
                                [PR[4 + pr], r_hgst[hss[hd]], r_eb], [r_hgout[ho]])
                            kb.dma("act", o_hgs[l, sq_, hd], hgo[ho], r_hgout[ho], [r_hgout[ho]], [])
                    for hd in range(4):
                        pe(lambda e, hd=hd: e.matmul(PB[hd][:, 0:64], lhsT=Vtok[0:64, 0, hd * 128:(hd + 1) * 128],
                                                     rhs=Am[0:64, hd, 0:64], start=False, stop=True),
                           [r_Vb[(0, hd // 2)], r_Am[hd]], [PR[hd]])
                    blk_chunks = [(64, 80, 16)]
                else:
                    blk_chunks = [(c[0], c[1], (17 if gi == 0 else 0) + chunks.index(c) - (1 if gi == 0 else 0))
                                  for c in tch if c[0] >= t0 and c[1] <= t1]
                for (c0, c1, ecol_i) in blk_chunks:
                    L = c1 - c0
                    pb_ = c0 - t0
                    for pr in range(2):
                        def fu(e, pr=pr, pb_=pb_, L=L, bi=bi):
                            ins = None
                            for hd in (pr, pr + 2):
                                ins = e.matmul(PB[4 + pr][:, 128 * (hd // 2):128 * (hd // 2) + 128],
                                               lhsT=ktok[pb_:pb_ + L, hd, :],
                                               rhs=Vtok[pb_:pb_ + L, bi, hd * 128:(hd + 1) * 128], start=True, stop=True)
                            return ins
                        pe(fu, [r_ktok[pr], r_ktok[pr + 2], r_Vb[(bi, 0)], r_Vb[(bi, 1)]], [PR[4 + pr]])
                    for hd in range(4):
                        act(lambda e, hd=hd, ecol_i=ecol_i: e.activation(out=Sdb[:, hd, :], in_=Sst[:, l, hd, :],
                                                                         func=AF.Copy,
                                                                         scale=eb[:, hd, ecol_i:ecol_i + 1]),
                            [r_S[l][hd], r_eb], [r_Sdb[hd]])
                    for hd in range(4):
                        def fo(e, hd=hd, c0=c0, c1=c1, pb_=pb_, L=L, bi=bi):
                            e.matmul(PB[hd][:, c0 - c0t:c1 - c0t], lhsT=Sdb[:, hd, :], rhs=qT[:, hd, c0:c1],
                                     start=True, stop=False)
                            return e.matmul(PB[hd][:, c0 - c0t:c1 - c0t],
                                            lhsT=Vtok[pb_:pb_ + L, bi, hd * 128:(hd + 1) * 128],
                                            rhs=Am[pb_:pb_ + L, hd, pb_:pb_ + L], start=False, stop=True)
                        pe(fo, [r_Sdb[hd], r_q[hd], r_Vb[(bi, hd // 2)], r_Am[hd]], [PR[hd]])
                    for hd in range(4):
                        pr = hd % 2
                        dve(lambda e, hd=hd, pr=pr, ecol_i=ecol_i: e.scalar_tensor_tensor(
                            out=Sst[:, l, hd, :], in0=Sst[:, l, hd, :], scalar=eb[:, hd, ecol_i:ecol_i + 1], op0=ALU.mult,
                            in1=PB[4 + pr][:, 128 * (hd // 2):128 * (hd // 2) + 128], op1=ALU.add),
                            [PR[4 + pr], r_S[l][hd], r_eb], [r_S[l][hd]])
            for hd in range(4):
                hg_norm_out(G, l, hd, ti, c0t, c1t, hd)
        if gi == 1:
            for hd in range(4):
                ho = cnt["hgout"] % 7
                cnt["hgout"] += 1
                dve(lambda e, ho=ho, hd=hd: e.tensor_copy(out=hgo[ho], in_=Sst[:, l, hd, :]),
                    [r_S[l][hd]], [r_hgout[ho]])
                kb.dma("sp", o_hgp[l, hd], hgo[ho], r_hgout[ho], [r_hgout[ho]], [])

    def s5_tables(l):
        sm = lambda i: s5sm[:, i, :]
        LRE, LIM, LDT = (S5P[:, l * 48 + i * 16:l * 48 + (i + 1) * 16] for i in range(3))
        DT, LR, MAG, ANG, Q, RS, RC, SN, CS, AR, AI, DEN, CR, CI, T1, T2, M1 = (sm(i) for i in range(17))
        r = r_s5sm
        rp = res("S5Pv")
        act(lambda e: e.activation(out=DT, in_=LDT, func=AF.Exp), [r_PV], [r])
        dve(lambda e: e.tensor_scalar(out=LR, in0=LRE, scalar1=-1e-4, scalar2=None, op0=ALU.min), [r_PV], [r])
        dve(lambda e: e.tensor_tensor(out=T1, in0=LR, in1=DT, op=ALU.mult), [r], [r])
        act(lambda e: e.activation(out=MAG, in_=T1, func=AF.Exp), [r], [r])
        dve(lambda e: e.tensor_tensor(out=ANG, in0=LIM, in1=DT, op=ALU.mult), [r, r_PV], [r])

        T2b, Qb, M1b = sm(24), sm(25), sm(26)
        ch = [dict(dst=RS, shift=0.0, T2=T2, Q=Q, M1=M1, I=s5i[:, 0, :], res=res("s5redA")),
              dict(dst=RC, shift=math.pi / 2, T2=T2b, Q=Qb, M1=M1b, I=s5i[:, 1, :], res=res("s5redB"))]
        steps = [
            lambda e, c: e.tensor_scalar(out=c["T2"], in0=ANG, scalar1=c["shift"], scalar2=None, op0=ALU.add),
            lambda e, c: e.tensor_scalar(out=c["Q"], in0=c["T2"], scalar1=1.0 / TWO_PI, scalar2=None, op0=ALU.mult),
            lambda e, c: e.tensor_copy(out=c["I"], in_=c["Q"]),
            lambda e, c: e.tensor_copy(out=c["Q"], in_=c["I"]),
            lambda e, c: e.scalar_tensor_tensor(out=c["dst"], in0=c["Q"], scalar=-TWO_PI, op0=ALU.mult, in1=c["T2"],
                                                op1=ALU.add),
            lambda e, c: e.tensor_scalar(out=c["M1"], in0=c["dst"], scalar1=math.pi, scalar2=None, op0=ALU.is_gt),
            lambda e, c: e.scalar_tensor_tensor(out=c["dst"], in0=c["M1"], scalar=-TWO_PI, op0=ALU.mult, in1=c["dst"],
                                                op1=ALU.add),
            lambda e, c: e.tensor_scalar(out=c["M1"], in0=c["dst"], scalar1=-math.pi, scalar2=None, op0=ALU.is_lt),
            lambda e, c: e.scalar_tensor_tensor(out=c["dst"], in0=c["M1"], scalar=TWO_PI, op0=ALU.mult, in1=c["dst"],
                                                op1=ALU.add),
        ]
        for si, stp in enumerate(steps):
            for c in ch:
                rd = [r] if si == 0 else [c["res"]]
                dve(lambda e, stp=stp, c=c: stp(e, c), rd, [c["res"]])
        r_redA, r_redB = ch[0]["res"], ch[1]["res"]
        act(lambda e: e.activation(out=SN, in_=RS, func=AF.Sin), [r, r_redA], [r])
        act(lambda e: e.activation(out=CS, in_=RC, func=AF.Sin), [r, r_redB], [r])
        dve(lambda e: e.tensor_tensor(out=AR, in0=MAG, in1=CS, op=ALU.mult), [r], [r])
        dve(lambda e: e.tensor_tensor(out=AI, in0=MAG, in1=SN, op=ALU.mult), [r], [r])
        dve(lambda e: e.tensor_tensor(out=DEN, in0=LR, in1=LR, op=ALU.mult), [r], [r])
        dve(lambda e: e.tensor_tensor(out=T1, in0=LIM, in1=LIM, op=ALU.mult), [r, r_PV], [r])
        dve(lambda e: e.tensor_tensor(out=DEN, in0=DEN, in1=T1, op=ALU.add), [r], [r])
        dve(lambda e: e.reciprocal(out=DEN, in_=DEN), [r], [r])
        dve(lambda e: e.tensor_scalar(out=T1, in0=AR, scalar1=-1.0, scalar2=None, op0=ALU.add), [r], [r])
        dve(lambda e: e.tensor_tensor(out=CR, in0=T1, in1=LR, op=ALU.mult), [r], [r])
        dve(lambda e: e.tensor_tensor(out=T2, in0=AI, in1=LIM, op=ALU.mult), [r, r_PV], [r])
        dve(lambda e: e.tensor_tensor(out=CR, in0=CR, in1=T2, op=ALU.add), [r], [r])
        dve(lambda e: e.tensor_tensor(out=CR, in0=CR, in1=DEN, op=ALU.mult), [r], [r])
        dve(lambda e: e.tensor_tensor(out=CI, in0=AI, in1=LR, op=ALU.mult), [r], [r])
        dve(lambda e: e.tensor_tensor(out=T2, in0=T1, in1=LIM, op=ALU.mult), [r, r_PV], [r])
        dve(lambda e: e.tensor_tensor(out=CI, in0=CI, in1=T2, op=ALU.subtract), [r], [r])
        dve(lambda e: e.tensor_tensor(out=CI, in0=CI, in1=DEN, op=ALU.mult), [r], [r])
        return dict(MAG=MAG, SN=SN, CS=CS, CR=CR, CI=CI)

    r_Braw = dres("Braw")
    r_Craw = dres("Craw")
    r_BB = res("BB")
    r_srcB = res("srcB")
    r_srcBL = [[res("srcB%d_%d" % (a_, b_)) for b_ in range(2)] for a_ in range(2)]
    r_srcB_all = [x for a_ in r_srcBL for x in a_]
    r_BBL = [res("BB0"), res("BB1")]
    r_srcCL = [[res("srcC%d_%d" % (a_, b_)) for b_ in range(2)] for a_ in range(2)]
    r_srcC_all = [x for a_ in r_srcCL for x in a_]
    r_srcC = res("srcC")
    r_BtL = {(a_, ri_, eo_): res("Bt%d_%d_%d" % (a_, ri_, eo_)) for a_ in range(4) for ri_ in range(2) for eo_ in range(2)}
    r_CtL = {(st_, ri_): res("Ct%d_%d" % (st_, ri_)) for st_ in range(16) for ri_ in range(2)}
    r_Bt_all = list(r_BtL.values())
    r_Ct_all = list(r_CtL.values())
    r_X0 = res("X0")
    r_tq = [res("tq%d" % i) for i in range(4)]
    r_rin = [res("rin0"), res("rin1")]
    r_rr = [res("rr0"), res("rr1")]
    r_xTb = res("xTb")
    r_yfp = res("yfp")
    r_ygb = res("ygb")
    r_coef0 = res("coef0")
    r_scr = [res("scr%d" % i) for i in range(DEPTH)]
    r_scrS, r_scrL = dres("scrS"), dres("scrL")
    r_pq = [res("pq0"), res("pq1")]
    r_rr2 = [r_rr, [res("rrB0"), res("rrB1")]]
    r_rrs = [[[res("rrs%d_%d_%d" % (b_, ri_, s4_)) for s4_ in range(4)] for ri_ in range(2)] for b_ in range(2)]
    r_carq2 = [[res("carq%d_%d" % (i, ri_)) for ri_ in range(2)] for i in range(4)]
    r_xTb2 = [r_xTb, res("xTbB")]
    r_w1, r_w2 = res("w1f"), res("w2f")
    r_yfp2 = [r_yfp, res("yfpB")]

    S5A2.extend(r_tq + r_rin + r_rr + r_pq + [x for a_ in r_rrs[0] for x in a_] + [r_s5t, r_coef0, r_yfp, r_srcB, r_srcC, r_BB, r_Braw, r_Craw] + r_srcB_all + r_BBL + r_srcC_all)
    S5A1.extend([r_xTb, r_ygb] + r_Ct_all + [r_xTb2[1], r_w1, r_w2] + r_rr2[1] + [x for a_ in r_rrs[1] for x in a_])

    def s5_fn(G, l, gi):
        T = G["T"]
        tiles = G["tiles"]
        kb.alias(S5A2, r_tmp + [res("eq%d" % h) for h in range(4)] + [res("rmask")])
        kb.alias(S5A1, r_q + r_k + r_gate + r_Vall)
        kb.alias([r_yfp2[1]], r_sg)
        cached = (gi == 1)
        if cached:
            kb.defer = []
        dve(lambda e: e.memset(Ct[:], 0.0), [], r_Ct_all)
        dve(lambda e: e.memset(srcB[:], 0.0), [], r_srcB_all)
        kb.dma("sp", Braw[:, 0, :, :], b_re[l].rearrange("(a q) h -> q a h", q=128), r_Braw, [], [r_Braw])
        kb.dma("sp", Braw[:, 1, :, :], b_im[l].rearrange("(a q) h -> q a h", q=128), r_Braw, [], [r_Braw])
        kb.dma("sp", Craw[:, 0, :, :], c_re[l].rearrange("(a q) p -> q a p", q=128), r_Craw, [], [r_Craw])
        kb.dma("sp", Craw[:, 1, :, :], c_im[l].rearrange("(a q) p -> q a p", q=128), r_Craw, [], [r_Craw])
        tb = s5_tables(l)
        MAG, SN, CS, CR, CI = tb["MAG"], tb["SN"], tb["CS"], tb["CR"], tb["CI"]
        r = r_s5sm
        bc = lambda v: v.unsqueeze(2).to_broadcast([128, 16, 16])
        tmpX = srcC[:, :, :].rearrange("p a c -> p (a c)").rearrange("p (a c) -> p a c", c=16)
        rX, rY = res("bbX"), res("bbY")
        dve(lambda e: e.tensor_tensor(out=BB[:, 0], in0=Braw[:, 0], in1=bc(CR), op=ALU.mult), [r_Braw, r], [r_BBL[0]])
        dve(lambda e: e.tensor_tensor(out=BB[:, 1], in0=Braw[:, 1], in1=bc(CR), op=ALU.mult), [r_Braw, r], [r_BBL[1]])
        dve(lambda e: e.tensor_tensor(out=tmpX, in0=Braw[:, 1], in1=bc(CI), op=ALU.mult), [r_Braw, r] + r_srcC_all, [rX])
        dve(lambda e: e.tensor_tensor(out=Braw[:, 0], in0=Braw[:, 0], in1=bc(CI), op=ALU.mult), [r_Braw, r, r_BBL[0]],
            [rY])
        dve(lambda e: e.tensor_tensor(out=BB[:, 0], in0=BB[:, 0], in1=tmpX, op=ALU.subtract), [r_BBL[0], rX], [r_BBL[0]])
        dve(lambda e: e.tensor_tensor(out=BB[:, 1], in0=BB[:, 1], in1=Braw[:, 0], op=ALU.add), [r_BBL[1], rY], [r_BBL[1]])
        for h in range(2):
            for ri in range(2):
                for eo in range(2):
                    v_src = BB[64 * h:64 * h + 64, ri, :, :].rearrange("p (a q) c -> p a q c", q=4)[:, :, eo::2, :]
                    v_dst = srcB[64 * h:64 * h + 64, ri, eo, :, 16 * h:16 * h + 16].rearrange(
                        "p (a q) c -> p a q c", q=4)[:, :, eo::2, :]
                    dve(lambda e, v_dst=v_dst, v_src=v_src: e.tensor_copy(out=v_dst, in_=v_src),
                        [r_BBL[ri]], [r_srcBL[ri][eo]])
        for a in range(4):
            for ri in range(2):
                for eo, Bt in ((0, BtE), (1, BtO)):
                    pbt = 5 + (a * 4 + ri * 2 + eo) % 2
                    pe(lambda e, a=a, ri=ri, eo=eo, pbt=pbt: e.transpose(
                        PB[pbt][:, 0:128], srcB[:, ri, eo, 4 * a:4 * a + 4, :].rearrange("p a c -> p (a c)"), identf[:, :]),
                        [r_srcBL[ri][eo], r_identf], [PR[pbt]])
                    act(lambda e, a=a, ri=ri, Bt=Bt, pbt=pbt: e.activation(out=Bt[:, a, ri, :], in_=PB[pbt][:, 0:128],
                                                                        func=AF.Copy),
                        [PR[pbt]], [r_BtL[(a, ri, eo)]])
        for ot in range(4):
            for ri in range(2):
                m0 = evod[:, 0:1] if ri == 0 else evod[:, 2:3]
                m1 = evod[:, 1:2] if ri == 0 else evod[:, 3:4]
                dve(lambda e, ot=ot, ri=ri, m0=m0: e.tensor_scalar(out=srcC[:, ri, 0:64], in0=Craw[:, ri, ot, :],
                                                                   scalar1=m0, scalar2=None, op0=ALU.mult),
                    [r_Craw, r_misc, rX], [r_srcCL[ri][0]])
                dve(lambda e, ot=ot, ri=ri, m1=m1: e.tensor_scalar(out=srcC[:, ri, 64:128], in0=Craw[:, ri, ot, :],
                                                                   scalar1=m1, scalar2=None, op0=ALU.mult),
                    [r_Craw, r_misc, rX], [r_srcCL[ri][1]])
                pbt = 5 + (ot * 2 + ri) % 2
                pe(lambda e, ri=ri, pbt=pbt: e.transpose(PB[pbt][:, 0:128], srcC[:, ri, :], identf[:, :]),
                   r_srcCL[ri] + [r_identf], [PR[pbt]])
                for j in range(4):
                    act(lambda e, ot=ot, ri=ri, j=j, pbt=pbt: e.activation(out=Ct[:, 4 * ot + j, ri, 32 * j:32 * j + 32],
                                                                           in_=PB[pbt][:, 32 * j:32 * j + 32],
                                                                           func=AF.Copy),
                        [PR[pbt]], [r_CtL[(4 * ot + j, ri)]])
        rt_ = r_s5t
        r_cosT, r_sinT = res("cosTw"), res("sinTw")
        kb.alias([r_cosT, r_sinT], [rt_])
        dve(lambda e: e.tensor_copy(out=cosT[:, :, 0:1], in_=CS.unsqueeze(2)), [r, rt_], [r_cosT])
        dve(lambda e: e.tensor_copy(out=sinT[:, :, 0:1], in_=SN.unsqueeze(2)), [r, rt_], [r_sinT])
        m = 1
        while m < 128:
            parts = [(0, 16)] if 16 * m <= 512 else [(i * (512 // m), (i + 1) * (512 // m)) for i in range(16 * m // 512)]
            for (s0, s1) in parts:
                ns = s1 - s0
                c_lo, s_lo = cosT[:, s0:s1, 0:m], sinT[:, s0:s1, 0:m]
                cmb = cosT[:, s0:s1, m - 1:m].to_broadcast([128, ns, m])
                smb = sinT[:, s0:s1, m - 1:m].to_broadcast([128, ns, m])
                f1, f2, f3, f4 = (tq[i][:, :, :].rearrange("p a t -> p (a t)")[:, 0:ns * m].rearrange(
                    "p (a t) -> p a t", t=m) for i in range(4))
                dve(lambda e, c_lo=c_lo, cmb=cmb, f1=f1: e.tensor_tensor(out=f1, in0=c_lo, in1=cmb, op=ALU.mult),
                    [r_cosT], [r_tq[0]])
                dve(lambda e, s_lo=s_lo, smb=smb, f2=f2: e.tensor_tensor(out=f2, in0=s_lo, in1=smb, op=ALU.mult),
                    [r_sinT], [r_tq[1]])
                dve(lambda e, s_lo=s_lo, cmb=cmb, f3=f3: e.tensor_tensor(out=f3, in0=s_lo, in1=cmb, op=ALU.mult),
                    [r_sinT, r_cosT], [r_tq[2]])
                dve(lambda e, c_lo=c_lo, smb=smb, f4=f4: e.tensor_tensor(out=f4, in0=c_lo, in1=smb, op=ALU.mult),
                    [r_cosT, r_sinT], [r_tq[3]])
                dve(lambda e, s0=s0, s1=s1, f1=f1, f2=f2, m=m: e.tensor_tensor(out=cosT[:, s0:s1, m:2 * m], in0=f1,
                                                                              in1=f2, op=ALU.subtract),
                    [r_tq[0], r_tq[1]], [r_cosT])
                dve(lambda e, s0=s0, s1=s1, f3=f3, f4=f4, m=m: e.tensor_tensor(out=sinT[:, s0:s1, m:2 * m], in0=f3,
                                                                              in1=f4, op=ALU.add),
                    [r_tq[2], r_tq[3]], [r_sinT])
            m *= 2
        kb.alias([rt_], [r_cosT, r_sinT])
        smflat = s5sm[:, 0:17, :].rearrange("p a c -> p (a c)")
        if cached:
            kb.defer = None
            kb.dma("sp", smflat, scr_sm[l], r_scrL, [r_scr[l]], [r_s5sm])
            kb.dma("sp", BtE[:, :, :, :].rearrange("p a r c -> p (a r c)"), scr_bt[l, 0], r_scrL, [r_scr[l]],
                   [r_BtL[k_] for k_ in r_BtL if k_[2] == 0])
            kb.dma("sp", BtO[:, :, :, :].rearrange("p a r c -> p (a r c)"), scr_bt[l, 1], r_scrL, [r_scr[l]],
                   [r_BtL[k_] for k_ in r_BtL if k_[2] == 1])
            kb.dma("sp", arena2[:, 0:4096], scr_cs[l], r_scrL, [r_scr[l]], [rt_])
            kb.dma("sp", arena1[:, 4 * TG1:4 * TG1 + 4096], scr_ct[l], r_scrL, [r_scr[l]], r_Ct_all)
        if gi == 0:
            for ri, srcst in ((0, st_re), (1, st_im)):
                kb.dma("sp", s5stg[:, :], srcst[l], r_stg[0], [], r_stg)

                def ft(e):
                    ins = None
                    for st in range(16):
                        ins = e.transpose(PB[5][:, st * 16:(st + 1) * 16], s5stg[:, st * 128:(st + 1) * 128],
                                          identf[0:16, 0:16])
                    return ins
                pe(ft, r_stg + [r_identf], [PR[5]])
                dve(lambda e, ri=ri: e.tensor_copy(out=X0[:, ri, :, :].rearrange("p a b -> p (a b)"), in_=PB[5][:, 0:256]),
                    [PR[5]], [r_X0])
                dve(lambda e, ri=ri: e.tensor_tensor(out=inj[:, ri], in0=X0[:, ri], in1=bc(MAG), op=ALU.mult),
                    [r_X0, r], [r_X0])
            kb.alias([r_coef0, r_yfp], [r_Braw, r_Craw] + r_BBL + r_srcC_all)
            dve(lambda e: e.tensor_copy(out=coef0[:, :, :], in_=MAG.unsqueeze(2).to_broadcast([128, 16, 80])),
                [r], [r_coef0])
            dve(lambda e: e.memset(coef0[:, :, 0:64].rearrange("p a (s j) -> p a s j", j=4)[:, :, :, 0:1], 0.0),
                [r_coef0], [r_coef0])
            dve(lambda e: e.memset(coef0[:, :, 64:65], 0.0), [r_coef0], [r_coef0])

        if gi == 1:
            kb.alias([r_coef0, r_yfp], [r_Braw, r_Craw] + r_BBL + r_srcC_all)
        else:
            kb.dma("sp", scr_sm[l], smflat, r_scrS, [r_s5sm], [r_scr[l]])
            kb.dma("sp", scr_bt[l, 0], BtE[:, :, :, :].rearrange("p a r c -> p (a r c)"), r_scrS, r_Bt_all, [r_scr[l]])
            kb.dma("sp", scr_bt[l, 1], BtO[:, :, :, :].rearrange("p a r c -> p (a r c)"), r_scrS, r_Bt_all, [r_scr[l]])
            kb.dma("sp", scr_cs[l], arena2[:, 0:4096], r_scrS, [rt_], [r_scr[l]])
            kb.dma("sp", scr_ct[l], arena1[:, 4 * TG1:4 * TG1 + 4096], r_scrS, r_Ct_all, [r_scr[l]])
        kb.alias(r_rin + r_rr + [x for a_ in r_rrs[0] for x in a_], r_srcB_all)
        if gi == 0:
            sblocks = [(0, 80)] + [(80 + 256 * i, 336 + 256 * i) for i in range(4)]
        else:
            sblocks = [(256 * i, 256 * i + 256) for i in range(4)]
        GLs = WS.take("A")
        wgl = ringA[:, GLs, 0:2048].rearrange("p (k c) -> p k c", c=512)
        fcount = [0]
        pend_post = []
        for sbi, (c0, c1) in enumerate(sblocks):
            ncol = c1 - c0
            ybuf = yfp2[sbi % 2]
            r_ybuf = r_yfp2[sbi % 2]
            ti = [i for i, t in enumerate(tiles) if t[0] <= c0 and c1 <= t[1]][0]
            block0 = (gi == 0 and c0 == 0)
            frames = [(0, 80)] if block0 else [(f, f + 128) for f in range(c0, c1, 128)]
            nfr = len(frames)

            def emit_B(qd, fi):
                f0, f1 = frames[fi]
                L = f1 - f0
                fb_ = (fbase + qd * nfr + fi) % 2
                psF = PBall[:, fb_ * 1024:(fb_ + 1) * 1024].rearrange("p (s r c) -> p s r c", s=4, r=2)

                def fn(e):
                    ins = None
                    for s4 in range(4):
                        pb_ = 64 * (s4 // 2)
                        Bt = BtE if s4 % 2 == 0 else BtO
                        for ri in range(2):
                            ins = e.matmul(psF[:, s4, ri, 0:L], lhsT=Bt[pb_:pb_ + 64, qd, ri, :],
                                           rhs=uT[pb_:pb_ + 64, qd, f0:f1], start=True, stop=True)
                    return ins
                pe(fn, r_Bt_all + [r_u[qd]], [PR[2 * fb_], PR[2 * fb_ + 1]])
            pendC = []
            pend_carry = []

            def emit_C(qd):
                xq = xTb2[qd % 2]
                r_xq = r_xTb2[qd % 2]
                pby = 4
                mm_group(PB[pby][:, 0:ncol], [(Ct[:, 4 * qd + s4, ri, :], xq[:, s4, ri, 0:ncol])
                                              for s4 in range(4) for ri in range(2)], r_Ct_all + [r_xq], [PR[pby]])
                dcol = R_S5D + 4 * l + qd
                dve(lambda e, qd=qd, dcol=dcol: e.scalar_tensor_tensor(out=ybuf[:, qd, 0:ncol], in0=uT[:, qd, c0:c1],
                                                                       scalar=PV[:, dcol:dcol + 1], op0=ALU.mult,
                                                                       in1=PB[pby][:, 0:ncol], op1=ALU.add),
                    [r_u[qd], r_PV, PR[pby]], [r_ybuf])
            fbase = fcount[0]
            for fi in range(nfr):
                emit_B(0, fi)
            for qd in range(4):
                xq = xTb2[qd % 2]
                r_xq = r_xTb2[qd % 2]
                for fi, (f0, f1) in enumerate(frames):
                    L = f1 - f0
                    o0 = f0 - c0
                    stq = slice(4 * qd, 4 * qd + 4)
                    fb_ = fcount[0] % 2
                    fcount[0] += 1
                    psF = PBall[:, fb_ * 1024:(fb_ + 1) * 1024].rearrange("p (s r c) -> p s r c", s=4, r=2)
                    PRF = [PR[2 * fb_], PR[2 * fb_ + 1]]
                    rrb = rr2[fb_]
                    r_rrb = [r_rrs[fb_][0], r_rrs[fb_][1]]
                    if block0:
                        segs = [(0, 64, 0), (64, 80, 0)]
                    else:
                        segs = [(0, L, 0)]
                    for (a0, a1, _) in segs:
                        if block0 and a0 == 0:
                            tcv = cosT[:, stq, 0:4].unsqueeze(2).to_broadcast([128, 4, 16, 4])
                            tsv = sinT[:, stq, 0:4].unsqueeze(2).to_broadcast([128, 4, 16, 4])
                            shp = lambda v: v.rearrange("p a (s j) -> p a s j", j=4)
                        else:
                            tcv = cosT[:, stq, 0:a1 - a0]
                            tsv = sinT[:, stq, 0:a1 - a0]
                            shp = lambda v: v
                        bur = shp(psF[:, :, 0, a0:a1])
                        bui = shp(psF[:, :, 1, a0:a1])
                        t1, t2, t3, t4 = (shp(tq[i][:, :, a0:a1]) for i in range(4))
                        dve(lambda e, t1=t1, bur=bur, tcv=tcv: e.tensor_tensor(out=t1, in0=bur, in1=tcv, op=ALU.mult),
                            PRF + [rt_], [r_tq[0]])
                        dve(lambda e, t2=t2, bui=bui, tsv=tsv: e.tensor_tensor(out=t2, in0=bui, in1=tsv, op=ALU.mult),
                            PRF + [rt_], [r_tq[1]])
                        dve(lambda e, t3=t3, bui=bui, tcv=tcv: e.tensor_tensor(out=t3, in0=bui, in1=tcv, op=ALU.mult),
                            PRF + [rt_], [r_tq[2]])
                        dve(lambda e, t4=t4, bur=bur, tsv=tsv: e.tensor_tensor(out=t4, in0=bur, in1=tsv, op=ALU.mult),
                            PRF + [rt_], [r_tq[3]])
                    for cf in pend_carry:
                        cf(0)
                    if qd < 3:
                        emit_B(qd + 1, fi)
                    if fi == nfr - 1 and pendC:
                        emit_C(pendC.pop(0))
                    dve(lambda e: e.tensor_tensor(out=rin[0][:, :, 0:L], in0=tq[0][:, :, 0:L], in1=tq[1][:, :, 0:L],
                                                  op=ALU.add), [r_tq[0], r_tq[1]], [r_rin[0]])
                    dve(lambda e: e.tensor_tensor(out=rin[1][:, :, 0:L], in0=tq[2][:, :, 0:L], in1=tq[3][:, :, 0:L],
                                                  op=ALU.subtract), [r_tq[2], r_tq[3]], [r_rin[1]])
                    if block0:
                        for ri in range(2):
                            v = rin[ri][:, :, 0:64].rearrange("p a (s j) -> p a s j", j=4)[:, :, :, 0:1]
                            dve(lambda e, v=v, ri=ri: e.tensor_tensor(out=v, in0=v, in1=inj[:, ri, stq, :].unsqueeze(3),
                                                                      op=ALU.add), [r_rin[ri], r_X0], [r_rin[ri]])
                    while pend_carry:
                        pend_carry.pop(0)(1)
                    for s4 in range(4):
                        st = 4 * qd + s4
                        for ri in range(2):
                            if block0:
                                d0 = coef0[:, st, 0:L]
                                init = 0.0
                                rds = [r_rin[ri], r_coef0]
                            else:
                                d0 = MAG[:, st:st + 1].to_broadcast([128, L])
                                init = car[:, l, ri, st:st + 1]
                                rds = [r_rin[ri], r, r_carq2[qd][ri]]
                            dve(lambda e, s4=s4, ri=ri, d0=d0, init=init, rrb=rrb: e.tensor_tensor_scan(
                                out=rrb[ri][:, s4, 0:L], data0=d0, data1=rin[ri][:, s4, 0:L], initial=init,
                                op0=ALU.mult, op1=ALU.add), rds, [r_rrb[ri][s4]])
                    for _ in range(min(POST_DRAIN, len(pend_post))):
                        pend_post.pop(0)()
                    def carry_fn(phase, l=l, stq=stq, L=L, rrp=rrP2[fb_], r_rrb=r_rrb, block0=block0, qd=qd,
                                 csb=(fcount[0] % 2) * 2):
                        cC = s5sm[:, 17 + csb, 0:8].rearrange("p (r a) -> p r a", r=2).unsqueeze(3)
                        cS = s5sm[:, 18 + csb, 0:8].rearrange("p (r a) -> p r a", r=2).unsqueeze(3)
                        jl = (15 if block0 else L - 1)
                        tcl = cosT[:, stq, jl:jl + 1].unsqueeze(1).to_broadcast([128, 2, 4, 1])
                        tsl = sinT[:, stq, jl:jl + 1].unsqueeze(1).to_broadcast([128, 2, 4, 1])
                        rl = rrp[:, :, :, L - 1:L]
                        r_cC, r_cS = res("s5cC%d" % csb), res("s5cS%d" % csb)
                        if phase == 1:
                            dve(lambda e: e.tensor_tensor(out=car[:, l, 0, stq].unsqueeze(2), in0=cC[:, 0], in1=cS[:, 1],
                                                          op=ALU.subtract), [r_cC, r_cS], [r_carq2[qd][0]])
                            dve(lambda e: e.tensor_tensor(out=car[:, l, 1, stq].unsqueeze(2), in0=cC[:, 1], in1=cS[:, 0],
                                                          op=ALU.add), [r_cC, r_cS], [r_carq2[qd][1]])
                            return
                        dve(lambda e: e.tensor_tensor(out=cC, in0=rl, in1=tcl, op=ALU.mult),
                            r_rrb[0] + r_rrb[1] + [rt_], [r_cC])
                        dve(lambda e: e.tensor_tensor(out=cS, in0=rl, in1=tsl, op=ALU.mult),
                            r_rrb[0] + r_rrb[1] + [rt_], [r_cS])

                    pend_carry.append(carry_fn)
                    for (a0, a1, _) in segs:
                        if block0 and a0 == 0:
                            tcv = cosT[:, stq, 0:4].unsqueeze(2).to_broadcast([128, 4, 16, 4])
                            tsv = sinT[:, stq, 0:4].unsqueeze(2).to_broadcast([128, 4, 16, 4])
                            shp = lambda v: v.rearrange("p a (s j) -> p a s j", j=4)
                        else:
                            tcv = cosT[:, stq, 0:a1 - a0]
                            tsv = sinT[:, stq, 0:a1 - a0]
                            shp = lambda v: v
                        rrv, riv = shp(rrb[0][:, :, a0:a1]), shp(rrb[1][:, :, a0:a1])
                        p0, p1 = shp(pq[0][:, :, a0:a1]), shp(pq[1][:, :, a0:a1])
                        xr = shp(xq[:, :, 0, o0 + a0:o0 + a1])
                        xi = shp(xq[:, :, 1, o0 + a0:o0 + a1])
                        pool(lambda e, p0=p0, rrv=rrv, tcv=tcv: e.tensor_tensor(out=p0, in0=rrv, in1=tcv, op=ALU.mult),
                             r_rrb[0] + [rt_], [r_pq[0]])
                        pool(lambda e, p1=p1, riv=riv, tsv=tsv: e.tensor_tensor(out=p1, in0=riv, in1=tsv, op=ALU.mult),
                             r_rrb[1] + [rt_], [r_pq[1]])
                        pool(lambda e, p0=p0, p1=p1, xr=xr: e.tensor_tensor(out=xr, in0=p0, in1=p1, op=ALU.subtract),
                             [r_pq[0], r_pq[1]], [r_xq])
                        if block0 and a0 == 0:
                            pool(lambda e, p0=p0, p1=p1: e.tensor_tensor(out=X1[:, 0, stq, :].unsqueeze(3),
                                                                         in0=p0[:, :, :, 3:4], in1=p1[:, :, :, 3:4],
                                                                         op=ALU.subtract), [r_pq[0], r_pq[1]], [r_X0])
                        pool(lambda e, p0=p0, riv=riv, tcv=tcv: e.tensor_tensor(out=p0, in0=riv, in1=tcv, op=ALU.mult),
                             r_rrb[1] + [rt_], [r_pq[0]])
                        pool(lambda e, p1=p1, rrv=rrv, tsv=tsv: e.tensor_tensor(out=p1, in0=rrv, in1=tsv, op=ALU.mult),
                             r_rrb[0] + [rt_], [r_pq[1]])
                        pool(lambda e, p0=p0, p1=p1, xi=xi: e.tensor_tensor(out=xi, in0=p0, in1=p1, op=ALU.add),
                             [r_pq[0], r_pq[1]], [r_xq])
                        if block0 and a0 == 0:
                            pool(lambda e, p0=p0, p1=p1: e.tensor_tensor(out=X1[:, 1, stq, :].unsqueeze(3),
                                                                         in0=p0[:, :, :, 3:4], in1=p1[:, :, :, 3:4],
                                                                         op=ALU.add), [r_pq[0], r_pq[1]], [r_X0])
                pendC.append(qd)
            while pend_carry:
                cf = pend_carry.pop(0)
                cf(0)
                cf(1)
            while pendC:
                emit_C(pendC.pop(0))
            def emit_post(c0=c0, c1=c1, ncol=ncol, ti=ti, ybuf=ybuf, r_ybuf=r_ybuf):
                w1 = w1f[:, 0:4 * ncol].rearrange("p (a t) -> p a t", t=ncol)
                w2 = w2f[:, 0:4 * ncol].rearrange("p (a t) -> p a t", t=ncol)
                yv = ybuf[:, :, 0:ncol]
                pool(lambda e: e.tensor_tensor(out=w1, in0=yv, in1=yv, op=ALU.mult), [r_ybuf], [r_w1])
                pool(lambda e: e.tensor_scalar(out=w1, in0=w1, scalar1=0.044715, scalar2=1.0, op0=ALU.mult, op1=ALU.add),
                     [r_w1], [r_w1])
                pool(lambda e: e.tensor_tensor(out=w1, in0=w1, in1=yv, op=ALU.mult), [r_w1, r_ybuf],
                     [r_w1])
                act(lambda e: e.activation(out=w2, in_=w1, func=AF.Sigmoid, scale=2.0 * math.sqrt(2.0 / math.pi)),
                    [r_w1], [r_w2])
                dve(lambda e: e.tensor_tensor(out=yv, in0=yv, in1=w2, op=ALU.mult), [r_ybuf, r_w2], [r_ybuf])
                act(lambda e: e.activation(out=ygb[:, :, 0:ncol], in_=yv, func=AF.Copy), [r_ybuf], [r_ygb])
                for ot in range(4):
                    pbg = 5
                    mm_group(PB[pbg][:, 0:ncol], [(wgl[:, k, ot * 128:(ot + 1) * 128], ygb[:, k, 0:ncol]) for k in range(4)],
                             [r_ringA[GLs], r_ygb], [PR[pbg]])
                    bcol = R_BGLU + 4 * l + ot
                    act(lambda e, ot=ot, bcol=bcol: e.activation(out=w2[:, ot, :], in_=PB[pbg][:, 0:ncol], func=AF.Sigmoid,
                                                                 bias=PV[:, bcol:bcol + 1]), [PR[pbg], r_PV],
                        [r_w2])
                dve(lambda e: e.tensor_tensor(out=yv, in0=yv, in1=w2, op=ALU.mult), [r_ybuf, r_w2], [r_ybuf])
                for ot in range(4):
                    s = sqi[0] % 3
                    sqi[0] += 1
                    act(lambda e, ot=ot, s=s: e.activation(out=sq[:, s, 0:ncol], in_=ybuf[:, ot, 0:ncol], func=AF.Square),
                        [r_ybuf], [r_sq[s]])
                    pe(lambda e, ot=ot, s=s: e.matmul(PB[6][:, 0:ncol], lhsT=onesb[:], rhs=sq[:, s, 0:ncol],
                                                      start=(ot == 0), stop=(ot == 3)), [r_sq[s], r_ones], [PR[6]])
                j = sqi[0] % 2
                act(lambda e: e.activation(out=rt[:, j, 0:ncol], in_=PB[6][:, 0:ncol], func=AF.Ln, bias=epsT[:, 0:1],
                                           scale=1.0 / 512), [PR[6], r_misc], [r_rt[j]])
                act(lambda e: e.activation(out=rt[:, j, 0:ncol], in_=rt[:, j, 0:ncol], func=AF.Exp, scale=-0.5),
                    [r_rt[j]], [r_rt[j]])
                for ot in range(4):
                    gcol = R_S5N + 4 * l + ot
                    dve(lambda e, ot=ot, gcol=gcol: e.scalar_tensor_tensor(out=xn[:, 4 + ot, c0:c1], in0=ybuf[:, ot, 0:ncol],
                                                                           scalar=PV[:, gcol:gcol + 1], op0=ALU.mult,
                                                                           in1=rt[:, j, 0:ncol], op1=ALU.mult),
                        [r_ybuf, r_rt[j], r_PV], [r_xn[4 + ot][ti]])

            while pend_post:
                pend_post.pop(0)()
            kb.defer = []
            emit_post()
            pend_post.extend(kb.defer)
            kb.defer = None
        while pend_post:
            pend_post.pop(0)()
        kb.alias(r_sg, [r_yfp2[1]])
        WS.done("A", 1)
        if gi == 0:
            for ri, dst in ((0, o_s5s_re), (1, o_s5s_im)):
                for q4 in range(4):
                    def ft(e, ri=ri, q4=q4):
                        ins = None
                        for s4 in range(4):
                            st = 4 * q4 + s4
                            ins = e.transpose(PB[5][0:16, s4 * 128:(s4 + 1) * 128], X1[:, ri, st, :], identf[:, :])
                        return ins
                    pe(ft, [r_X0, r_identf], [PR[5]])
                    dve(lambda e, q4=q4: e.tensor_copy(out=s5stg[:, q4 * 512:(q4 + 1) * 512],
                                                       in_=PB[5][0:16, 0:512]), [PR[5]], r_stg)
                kb.dma("sp", dst[l], s5stg[:, :], r_stg[0], r_stg, [])
        else:
            for ri, dst in ((0, o_s5p_re), (1, o_s5p_im)):
                pe(lambda e, ri=ri: e.transpose(PB[5][0:16, 0:128], car[:, l, ri, :], identf[:, :]),
                   [x for a_ in r_carq2 for x in a_] + [r_identf], [PR[5]])
                dve(lambda e: e.tensor_copy(out=s5stg[:, 0:128], in_=PB[5][0:16, 0:128]), [PR[5]], r_stg)
                kb.dma("sp", dst[l], s5stg[:, 0:128], r_stg[0], r_stg, [])

    for gi, G in enumerate(groups):
        load_x(G)
        for l in range(depth):
            rmsnorm_to_xn(G, R_NF1 + 8 * l)
            ffn(G, l)
            if mixer:
                mixer_fn(G, l, gi)
            rmsnorm_to_xn(G, R_NF2 + 8 * l)
            ffn(G, l)
        store_y(G)

    for name, E in kb.eng.items():
        if E["obj"] is None and E["count"] > 0:
            nc.sync.wait_ge(E["sem"], E["count"])
    for name in ("pe", "act", "dve", "pool"):
        E = kb.eng[name]
        if E["count"] > 0:
            nc.sync.wait_ge(E["sem"], E["count"])
    es.close()
    return nc


def _pack_params(inp):
    rows = []
    rows.append(inp["norm_ffn1"].reshape(32, 128))
    rows.append(inp["norm_mix"].reshape(32, 128))
    rows.append(inp["norm_ffn2"].reshape(32, 128))
    rows.append(inp["norm_final"].reshape(8, 128))
    rows.append(inp["hgrn_norm"].reshape(16, 128))
    rows.append(inp["s5_norm"].reshape(16, 128))
    rows.append(inp["s5_b_glu"].reshape(16, 128))
    rows.append(inp["lb_param"].reshape(16, 128))
    rows.append(inp["s5_d"].reshape(16, 128))
    pv = np.ascontiguousarray(np.concatenate(rows, axis=0).astype(np.float32))
    assert pv.shape == (NPROW, 128)
    s5 = []
    for l in range(DEPTH):
        s5.append(inp["s5_lambda_re"][l].reshape(16, 128))
        s5.append(inp["s5_lambda_im"][l].reshape(16, 128))
        s5.append(np.repeat(inp["s5_log_dt"][l], 64).reshape(16, 128))
    s5 = np.ascontiguousarray(np.concatenate(s5, axis=0).astype(np.float32))
    return pv, s5


_NC_CACHE = {}


def make_in_maps(inp, cores):
    inp = {k: np.asarray(v) for k, v in inp.items()}
    pv, s5 = _pack_params(inp)
    shared = dict(
        pvec=pv, s5pv=s5,
        ffn1_w_gate=inp["ffn1_w_gate"], ffn1_w_up=inp["ffn1_w_up"], ffn1_w_down=inp["ffn1_w_down"],
        ffn2_w_gate=inp["ffn2_w_gate"], ffn2_w_up=inp["ffn2_w_up"], ffn2_w_down=inp["ffn2_w_down"],
        w_in=inp["w_in"], w_out=inp["w_out"], s5_w_glu=inp["s5_w_glu"],
        s5_b_re=inp["s5_b_re"].reshape(DEPTH, 2048, 16), s5_b_im=inp["s5_b_im"].reshape(DEPTH, 2048, 16),
        s5_c_re=inp["s5_c_re"].reshape(DEPTH, 512, 64), s5_c_im=inp["s5_c_im"].reshape(DEPTH, 512, 64),
    )
    maps = []
    for c in cores:
        xs = inp["x_sample"][NSEQ * c:NSEQ * (c + 1)].reshape(64, D)
        xp = inp["x_prompt"][c]
        x = np.ascontiguousarray(np.concatenate([xs, inp["meta_tokens"], xp], axis=0).astype(np.float32))
        m = dict(shared)
        m["xin"] = x
        m["st_hg"] = np.ascontiguousarray(inp["state_hgrn"][:, NSEQ * c:NSEQ * (c + 1)])
        m["st_re"] = np.ascontiguousarray(inp["state_s5_re"][:, NSEQ * c:NSEQ * (c + 1)].reshape(DEPTH, NSEQ, 2048))
        m["st_im"] = np.ascontiguousarray(inp["state_s5_im"][:, NSEQ * c:NSEQ * (c + 1)].reshape(DEPTH, NSEQ, 2048))
        maps.append(m)
    return maps


def gather(results, ncores):
    B = ncores
    y_prompt = np.zeros((B, 2048, D), np.float32)
    y_sample = np.zeros((NSEQ * B, 4, D), np.float32)
    hgp = np.zeros((DEPTH, B, 4, 128, 128), np.float32)
    s5pr = np.zeros((DEPTH, B, 32, 64), np.float32)
    s5pi = np.zeros((DEPTH, B, 32, 64), np.float32)
    hgs = np.zeros((DEPTH, NSEQ * B, 4, 128, 128), np.float32)
    s5sr = np.zeros((DEPTH, NSEQ * B, 32, 64), np.float32)
    s5si = np.zeros((DEPTH, NSEQ * B, 32, 64), np.float32)
    for c, r in enumerate(results):
        y = np.asarray(r["yout"])
        y_sample[NSEQ * c:NSEQ * (c + 1)] = y[0:64].reshape(NSEQ, 4, D)
        y_prompt[c] = y[80:]
        hgp[:, c] = np.asarray(r["o_hgp"])
        s5pr[:, c] = np.asarray(r["o_s5p_re"]).reshape(DEPTH, 32, 64)
        s5pi[:, c] = np.asarray(r["o_s5p_im"]).reshape(DEPTH, 32, 64)
        hgs[:, NSEQ * c:NSEQ * (c + 1)] = np.asarray(r["o_hgs"])
        s5sr[:, NSEQ * c:NSEQ * (c + 1)] = np.asarray(r["o_s5s_re"]).reshape(DEPTH, NSEQ, 32, 64)
        s5si[:, NSEQ * c:NSEQ * (c + 1)] = np.asarray(r["o_s5s_im"]).reshape(DEPTH, NSEQ, 32, 64)
    return (y_prompt, y_sample, hgp, s5pr, s5pi, hgs, s5sr, s5si)


def kernel(**inputs):
    if "nc" not in _NC_CACHE:
        _NC_CACHE["nc"] = build_nc()
    nc = _NC_CACHE["nc"]
    maps = make_in_maps(inputs, list(range(NCORE)))
    res = run_bass_kernel_spmd(nc, maps, core_ids=list(range(NCORE)))
    return gather(res.results, NCORE)
```
